# Optimizing a Trainium2 kernel written in Bass

```python
import math, functools
import jax, jax.numpy as jnp
from jax import lax
import numpy as np

D_MODEL = 1024
BATCH = 32
SEQ = 256
DEPTH = 2
DEC_BATCH = 4
DEC_SEQ = 2048
PAST_LEN = 256

GRID_W = 64
POOL_WIDTH = D_MODEL // 4
POOL_GROUPS = 4
POOL_GROUP_DIM = POOL_WIDTH // POOL_GROUPS
POOL_WINDOWS = (2, 4, 8, 16)
HY_WIDTH = D_MODEL // 4
HY_ORDER = 2
HY_DIRS = 2
HY_BANDS = 16
HY_EMB = 1 + 2 * HY_BANDS
HY_FFN = 64
HY_SHORT = 3
HY_TARGET = 1e-2
HY_FAST = 0.3
HY_SLOW = 1.5
NA_HEAD_DIM = 64
NA_WIDTH = D_MODEL // 2
NA_HEADS = NA_WIDTH // NA_HEAD_DIM
NA_MAX_KH = 8
NA_KW = 16
MIX_WIDTH = POOL_WIDTH + HY_WIDTH + NA_WIDTH
IN_WIDTH = POOL_WIDTH + (HY_ORDER + 1) * HY_WIDTH + 3 * NA_WIDTH
D_FF = 4 * D_MODEL
N_MOD = 6
NORM_EPS = 1e-6
Q_BLOCK = 128

kernel_name = "hybrid_pool_hyena_natten_diffusion_step"


def rms_norm(x, g):
    x32 = x.astype(jnp.float32)
    y = x32 * lax.rsqrt(jnp.mean(x32 * x32, axis=-1, keepdims=True) + NORM_EPS)
    return (y * g.astype(jnp.float32)).astype(x.dtype)


def adaln(cond, w_mod, b_mod):
    m = jax.nn.silu(cond) @ w_mod + b_mod
    return jnp.split(m, N_MOD, axis=-1)


def pool_mixer(u, pool_w, pool_scale):
    B, L, _ = u.shape
    u32 = u.astype(jnp.float32).reshape(B, L, POOL_GROUPS, POOL_GROUP_DIM)
    cs = jnp.concatenate([jnp.zeros_like(u32[:, :1]), jnp.cumsum(u32, axis=1)], axis=1)
    t = jnp.arange(L)
    outs = []
    for g, w in enumerate(POOL_WINDOWS):
        lo = jnp.clip(t - w // 2, 0, L - 1)
        hi = jnp.clip(t + (w - 1 - w // 2), 0, L - 1)
        cnt = (hi - lo + 1).astype(jnp.float32)[None, :, None]
        cs_g = cs[:, :, g]
        mean = (cs_g[:, hi + 1] - cs_g[:, lo]) / cnt
        outs.append(mean - u32[:, :, g])
    pooled = jnp.stack(outs, axis=2).astype(u.dtype)
    y = jnp.einsum('blgc,gcd->blgd', pooled, pool_w)
    return y.reshape(B, L, POOL_WIDTH) * pool_scale


def hyena_position_features(L):
    t = jnp.linspace(0.0, 1.0, L, dtype=jnp.float32)[:, None]
    w = 2.0 * math.pi * jnp.arange(L, dtype=jnp.float32)[:, None] / L
    f = jnp.linspace(1e-4, HY_BANDS - 1, HY_BANDS, dtype=jnp.float32)[None, :]
    z = jnp.concatenate([t, jnp.cos(f * w), -jnp.sin(f * w)], axis=-1)
    return z, t


def hyena_filters(L, f1_w, f1_b, f1_freq, f2_w, f2_b, f2_freq, f3_w):
    f32 = jnp.float32
    z, t = hyena_position_features(L)
    h = jnp.sin(f1_freq.astype(f32) * (z @ f1_w.astype(f32) + f1_b.astype(f32)))
    h = jnp.sin(f2_freq.astype(f32) * (h @ f2_w.astype(f32) + f2_b.astype(f32)))
    k = (h @ f3_w.astype(f32)).reshape(L, HY_DIRS, HY_ORDER, HY_WIDTH)
    deltas = jnp.abs(jnp.linspace(math.log(HY_TARGET) / HY_SLOW, math.log(HY_TARGET) / HY_FAST,
                                  HY_WIDTH, dtype=f32))
    decay = jnp.exp(-t * deltas[None, :])
    k = k * decay[:, None, None, :]
    return k / (jnp.sum(jnp.abs(k), axis=0, keepdims=True) + 1e-6)


def two_sided_fft_conv(z, kf, kb):
    L = z.shape[1]
    filt = jnp.concatenate([kf, jnp.zeros_like(kf[:1]), kb[:0:-1]], axis=0)
    zf = jnp.fft.rfft(z, n=2 * L, axis=1)
    ff = jnp.fft.rfft(filt, axis=0)
    return jnp.fft.irfft(zf * ff[None], n=2 * L, axis=1)[:, :L]


def short_conv(u, w, b):
    L = u.shape[1]
    pad = HY_SHORT // 2
    up = jnp.pad(u, ((0, 0), (pad, HY_SHORT - 1 - pad), (0, 0)))
    y = b
    for j in range(HY_SHORT):
        y = y + up[:, j:j + L] * w[j]
    return y


def hyena_mixer(u, conv_w, conv_b, f1_w, f1_b, f1_freq, f2_w, f2_b, f2_freq, f3_w, hy_bias):
    L = u.shape[1]
    u = short_conv(u, conv_w, conv_b)
    x1, x2, v = jnp.split(u, 3, axis=-1)
    k = hyena_filters(L, f1_w, f1_b, f1_freq, f2_w, f2_b, f2_freq, f3_w)
    z = v.astype(jnp.float32)
    for o, gate in enumerate((x1, x2)):
        conv = two_sided_fft_conv(z, k[:, 0, o], k[:, 1, o])
        z = gate.astype(jnp.float32) * (conv + hy_bias[o].astype(jnp.float32) * z)
    return z.astype(u.dtype)


def context_attention(q, k, v):
    B, L, H, Dh = q.shape
    nb = L // Q_BLOCK
    qb = q.reshape(B, nb, Q_BLOCK, H, Dh).transpose(1, 0, 2, 3, 4)
    scale = Dh ** -0.5

    def block(qi):
        s = jnp.einsum('bqhd,bkhd->bhqk', qi, k).astype(jnp.float32) * scale
        p = jax.nn.softmax(s, axis=-1).astype(v.dtype)
        return jnp.einsum('bhqk,bkhd->bqhd', p, v)

    o = lax.map(block, qb)
    return o.transpose(1, 0, 2, 3, 4).reshape(B, L, H * Dh)


def neighbourhood_attention(q, k, v, k_ctx, v_ctx, rel_bias):
    B, L, H, Dh = q.shape
    rows = L // GRID_W
    kh = min(NA_MAX_KH, rows)
    r = jnp.arange(rows)
    c = jnp.arange(GRID_W)
    row_start = jnp.clip(r - kh // 2, 0, rows - kh)
    row_idx = row_start[:, None] + jnp.arange(kh)[None, :]
    col_start = jnp.clip(c - NA_KW // 2, 0, GRID_W - NA_KW)
    col_mask = (c[None, :] >= col_start[:, None]) & (c[None, :] < col_start[:, None] + NA_KW)
    dr = row_idx - r[:, None] + NA_MAX_KH - 1
    dc = jnp.clip(c[None, :] - c[:, None], -(NA_KW - 1), NA_KW - 1) + NA_KW - 1
    bias = rel_bias[:, dr][:, :, :, dc]
    bias = bias.transpose(0, 1, 3, 2, 4).astype(jnp.float32)

    scale = Dh ** -0.5
    qg = q.reshape(B, rows, GRID_W, H, Dh)
    kg = k.reshape(B, rows, GRID_W, H, Dh)[:, row_idx]
    vg = v.reshape(B, rows, GRID_W, H, Dh)[:, row_idx]
    s_nb = jnp.einsum('brchd,brikhd->bhrcik', qg, kg).astype(jnp.float32) * scale + bias[None]
    s_nb = jnp.where(col_mask[:, None, :], s_nb, -jnp.inf)
    s_nb = s_nb.reshape(B, H, rows, GRID_W, kh * GRID_W)
    s_ctx = jnp.einsum('brchd,bkhd->bhrck', qg, k_ctx).astype(jnp.float32) * scale
    p = jax.nn.softmax(jnp.concatenate([s_nb, s_ctx], axis=-1), axis=-1).astype(v.dtype)
    p_nb = p[..., :kh * GRID_W].reshape(B, H, rows, GRID_W, kh, GRID_W)
    p_ctx = p[..., kh * GRID_W:]
    o = (jnp.einsum('bhrcik,brikhd->brchd', p_nb, vg)
         + jnp.einsum('bhrck,bkhd->brchd', p_ctx, v_ctx))
    return o.reshape(B, L, H * Dh)


def trunk_layer(x, cond, attend, norm1_g, norm2_g, w_mod, b_mod, w_in, pool_w, pool_scale,
                hy_conv_w, hy_conv_b, hy_f1_w, hy_f1_b, hy_f1_freq, hy_f2_w, hy_f2_b, hy_f2_freq,
                hy_f3_w, hy_bias, q_norm_g, k_norm_g, w_out, w_up, w_down):
    B, L, _ = x.shape
    shift1, scale1, gate1, shift2, scale2, gate2 = adaln(cond, w_mod, b_mod)
    h = rms_norm(x, norm1_g) * (1.0 + scale1) + shift1
    u = h @ w_in
    s1 = POOL_WIDTH
    s2 = s1 + (HY_ORDER + 1) * HY_WIDTH
    s3 = s2 + NA_WIDTH
    s4 = s3 + NA_WIDTH
    u_pool, u_hy, u_q, u_k, u_v = jnp.split(u, [s1, s2, s3, s4], axis=-1)
    y_pool = pool_mixer(u_pool, pool_w, pool_scale)
    y_hy = hyena_mixer(u_hy, hy_conv_w, hy_conv_b, hy_f1_w, hy_f1_b, hy_f1_freq,
                       hy_f2_w, hy_f2_b, hy_f2_freq, hy_f3_w, hy_bias)
    q = rms_norm(u_q.reshape(B, L, NA_HEADS, NA_HEAD_DIM), q_norm_g)
    k = rms_norm(u_k.reshape(B, L, NA_HEADS, NA_HEAD_DIM), k_norm_g)
    v = u_v.reshape(B, L, NA_HEADS, NA_HEAD_DIM)
    y_na = attend(q, k, v)
    y = jnp.concatenate([y_pool, y_hy, y_na], axis=-1) @ w_out
    x = x + gate1 * y
    h = rms_norm(x, norm2_g) * (1.0 + scale2) + shift2
    x = x + gate2 * (jnp.square(jax.nn.relu(h @ w_up)) @ w_down)
    return x, k, v


def setup_inputs(seed: int = 0) -> dict:
    key = jax.random.key(seed)
    ks = jax.random.split(key, 32)
    f32 = jnp.float32
    D = D_MODEL

    def nrm(k, shape, scale):
        return scale * jax.random.normal(k, shape, f32)

    return {
        "x_prompt": nrm(ks[0], (BATCH, SEQ, D), 1.0),
        "x_sample": nrm(ks[1], (DEC_BATCH, DEC_SEQ, D), 1.0),
        "cache_k": nrm(ks[2], (DEC_BATCH, DEPTH, PAST_LEN, NA_HEADS, NA_HEAD_DIM), 1.0),
        "cache_v": nrm(ks[3], (DEC_BATCH, DEPTH, PAST_LEN, NA_HEADS, NA_HEAD_DIM), 1.0),
        "c": nrm(ks[4], (DEC_BATCH, D), 1.0),
        "c_ctx": nrm(ks[5], (D,), 1.0),
        "norm1_g": 1.0 + nrm(ks[6], (DEPTH, D), 0.02),
        "norm2_g": 1.0 + nrm(ks[7], (DEPTH, D), 0.02),
        "w_mod": nrm(ks[8], (DEPTH, D, N_MOD * D), 0.5 * D ** -0.5),
        "b_mod": nrm(ks[9], (DEPTH, N_MOD * D), 0.01),
        "w_in": nrm(ks[10], (DEPTH, D, IN_WIDTH), D ** -0.5),
        "pool_w": nrm(ks[11], (DEPTH, POOL_GROUPS, POOL_GROUP_DIM, POOL_GROUP_DIM), POOL_GROUP_DIM ** -0.5),
        "pool_scale": 1.0 + nrm(ks[12], (DEPTH, POOL_WIDTH), 0.1),
        "hy_conv_w": nrm(ks[13], (DEPTH, HY_SHORT, (HY_ORDER + 1) * HY_WIDTH), HY_SHORT ** -0.5),
        "hy_conv_b": nrm(ks[14], (DEPTH, (HY_ORDER + 1) * HY_WIDTH), 0.01),
        "hy_f1_w": nrm(ks[15], (DEPTH, HY_EMB, HY_FFN), 2.0 * HY_EMB ** -0.5),
        "hy_f1_b": nrm(ks[16], (DEPTH, HY_FFN), 0.1),
        "hy_f1_freq": 1.0 + nrm(ks[17], (DEPTH, HY_FFN), 0.1),
        "hy_f2_w": nrm(ks[18], (DEPTH, HY_FFN, HY_FFN), 2.0 * HY_FFN ** -0.5),
        "hy_f2_b": nrm(ks[19], (DEPTH, HY_FFN), 0.1),
        "hy_f2_freq": 1.0 + nrm(ks[20], (DEPTH, HY_FFN), 0.1),
        "hy_f3_w": nrm(ks[21], (DEPTH, HY_FFN, HY_DIRS * HY_ORDER * HY_WIDTH), HY_FFN ** -0.5),
        "hy_bias": nrm(ks[22], (DEPTH, HY_ORDER, HY_WIDTH), 0.1),
        "q_norm_g": 1.0 + nrm(ks[23], (DEPTH, NA_HEAD_DIM), 0.02),
        "k_norm_g": 1.0 + nrm(ks[24], (DEPTH, NA_HEAD_DIM), 0.02),
        "rel_bias": nrm(ks[25], (DEPTH, NA_HEADS, 2 * NA_MAX_KH - 1, 2 * NA_KW - 1), 0.1),
        "w_out": nrm(ks[26], (DEPTH, MIX_WIDTH, D), MIX_WIDTH ** -0.5),
        "w_up": nrm(ks[27], (DEPTH, D, D_FF), D ** -0.5),
        "w_down": nrm(ks[28], (DEPTH, D_FF, D), D_FF ** -0.5),
    }


def reference(x_prompt, x_sample, cache_k, cache_v, c, c_ctx, norm1_g, norm2_g, w_mod, b_mod,
              w_in, pool_w, pool_scale, hy_conv_w, hy_conv_b, hy_f1_w, hy_f1_b, hy_f1_freq,
              hy_f2_w, hy_f2_b, hy_f2_freq, hy_f3_w, hy_bias, q_norm_g, k_norm_g, rel_bias,
              w_out, w_up, w_down):
    def layer_params(l):
        return (norm1_g[l], norm2_g[l], w_mod[l], b_mod[l], w_in[l], pool_w[l], pool_scale[l],
                hy_conv_w[l], hy_conv_b[l], hy_f1_w[l], hy_f1_b[l], hy_f1_freq[l],
                hy_f2_w[l], hy_f2_b[l], hy_f2_freq[l], hy_f3_w[l], hy_bias[l],
                q_norm_g[l], k_norm_g[l], w_out[l], w_up[l], w_down[l])

    cond_ctx = c_ctx[None, None, :]
    xp = x_prompt
    ks, vs = [], []
    for l in range(DEPTH):
        xp, k_l, v_l = trunk_layer(xp, cond_ctx, context_attention, *layer_params(l))
        ks.append(k_l)
        vs.append(v_l)
    new_k = jnp.stack(ks, axis=1)
    new_v = jnp.stack(vs, axis=1)

    cond_lat = c[:, None, :]
    xs = x_sample
    for l in range(DEPTH):
        attend = functools.partial(neighbourhood_attention, k_ctx=cache_k[:, l], v_ctx=cache_v[:, l],
                                   rel_bias=rel_bias[l])
        xs, _, _ = trunk_layer(xs, cond_lat, attend, *layer_params(l))

    return (xp, xs, new_k, new_v)
```

```python
import numpy as np
import ml_dtypes
import concourse.bass as bass
import concourse.mybir as mybir
from concourse.bass_utils import run_bass_kernel_spmd

F32 = mybir.dt.float32
BF16 = mybir.dt.bfloat16
ALU = mybir.AluOpType
AF = mybir.ActivationFunctionType
AX = mybir.AxisListType
NPBF = ml_dtypes.bfloat16

D = 1024
NT = 2048
NTILE = 16
NCH = 8
L_DEPTH = 2
IN_W = 2560
DFF = 4096
NEG = -30000.0
PI_C = 3.14159
MAGIC = 12582912.0
TWO_PI = 6.283185307179586

C_N1G, C_N2G, C_BMOD, C_PSC, C_CW, C_CB, C_HB = 0, 8, 16, 64, 66, 84, 90
PL = 94
C_COND = 2 * PL
C_FLAG = C_COND + 8
C_EPS = C_FLAG + 1
C_TAU0 = C_EPS + 1
C_SGN = C_TAU0 + 16
NV = C_SGN + 1
NV64 = 8


class Sched:
    NDMA_SEM = 8

    def __init__(self, nc):
        self.nc = nc
        self.eng = {"pe": nc.tensor, "act": nc.scalar, "dve": nc.vector, "pool": nc.gpsimd,
                    "sp": nc.sync}
        self.sem, self.cnt = {}, {}
        for e in ("pe", "act", "dve", "pool"):
            self.sem[e] = nc.alloc_semaphore(name=f"s_{e}")
            self.cnt[e] = 0
        self.dsem, self.dcnt = {}, {}
        for q in ("sp", "pool"):
            self.dsem[q] = [nc.alloc_semaphore(name=f"d_{q}{i}") for i in range(self.NDMA_SEM)]
            self.dcnt[q] = 0
        self.known = {e: {} for e in ("pe", "act", "dve", "pool", "sp")}
        self.last_w, self.readers = {}, {}

    def _tok_wait(self, tok):
        if tok[0] == "c":
            return ("c", tok[1]), self.sem[tok[1]], tok[2]
        q, m = tok[1], tok[2]
        r, j = m % self.NDMA_SEM, m // self.NDMA_SEM
        return ("d", q, r), self.dsem[q][r], 16 * (j + 1)

    @staticmethod
    def _excl(reads, writes):
        return list(writes) + [r for r in reads if r.startswith("ps")]

    def _deps(self, reads, writes):
        writes = self._excl(reads, writes)
        deps = []
        for r in reads:
            t = self.last_w.get(r)
            if t is not None:
                deps.append(t)
        for r in writes:
            t = self.last_w.get(r)
            if t is not None:
                deps.append(t)
            deps.extend(self.readers.get(r, ()))
        return deps

    def _emit_waits(self, waiter, deps, self_n=None):
        h = self.eng[waiter]
        best = {}
        for tok in deps:
            if tok[0] == "c" and tok[1] == waiter:
                if waiter == "pe":
                    continue
                if self_n is not None and tok[2] < self_n - 2:
                    continue
            key, sem, val = self._tok_wait(tok)
            if self.known[waiter].get(key, 0) >= val:
                continue
            if key not in best or best[key][1] < val:
                best[key] = (sem, val)
        for key, (sem, val) in best.items():
            h.wait_ge(sem, val)
            self.known[waiter][key] = val

    def _record(self, tok, reads, writes):
        writes = self._excl(reads, writes)
        for r in reads:
            self.readers.setdefault(r, []).append(tok)
        for r in writes:
            self.last_w[r] = tok
            self.readers[r] = []

    def op(self, eng, fn, reads=(), writes=()):
        reads, writes = list(reads), list(writes)
        deps = self._deps(reads, writes)
        n = self.cnt[eng] + 1
        self._emit_waits(eng, deps, self_n=n)
        ins = fn(self.eng[eng])
        ins.then_inc(self.sem[eng], 1)
        self.cnt[eng] = n
        self._record(("c", eng, n), reads, writes)

    def dma(self, q, out, in_, reads=(), writes=()):
        reads, writes = list(reads), list(writes)
        deps = self._deps(reads, writes)
        m = self.dcnt[q]
        if m >= self.NDMA_SEM:
            deps.append(("d", q, m - self.NDMA_SEM))
        self._emit_waits(q, deps)
        r = m % self.NDMA_SEM
        self.eng[q].dma_start(out=out, in_=in_).then_inc(self.dsem[q][r], 16)
        self.dcnt[q] = m + 1
        self._record(("d", q, m), reads, writes)

    def barrier(self):
        toks = []
        for e in ("pe", "act", "dve", "pool"):
            if self.cnt[e] > 0:
                toks.append(("c", e, self.cnt[e]))
        for q in ("sp", "pool"):
            for m in range(max(0, self.dcnt[q] - self.NDMA_SEM), self.dcnt[q]):
                toks.append(("d", q, m))
        for w in ("pe", "act", "dve", "pool", "sp"):
            h = self.eng[w]
            for tok in toks:
                if tok[0] == "c" and tok[1] == w:
                    continue
                key, sem, val = self._tok_wait(tok)
                if self.known[w].get(key, 0) >= val:
                    continue
                h.wait_ge(sem, val)
                self.known[w][key] = val
        self.last_w.clear()
        self.readers.clear()


class Arena:
    BASE, TOP = 16512, 229344

    def __init__(self, nc):
        self.nc = nc
        self.persist = self.BASE
        self.cur = self.BASE
        self.n = 0

    def _alloc(self, name, shape, dt, off):
        self.n += 1
        return self.nc.alloc_sbuf_tensor_at(f"{name}_{self.n}", list(shape), dt, offset=off)

    @staticmethod
    def _bytes(shape, dt):
        n = 1
        for s in shape[1:]:
            n *= s
        return (n * (2 if dt == BF16 else 4) + 31) // 32 * 32

    def p(self, name, shape, dt):
        assert self.cur == self.persist, "persistent alloc after scratch"
        t = self._alloc(name, shape, dt, self.persist)
        self.persist += self._bytes(shape, dt)
        self.cur = self.persist
        assert self.persist <= self.TOP
        return t

    def s(self, name, shape, dt):
        t = self._alloc(name, shape, dt, self.cur)
        self.cur += self._bytes(shape, dt)
        assert self.cur <= self.TOP, f"SBUF overflow at {name}: {self.cur}"
        return t

    def reset(self, to=None):
        self.cur = self.persist if to is None else to

    def at(self, name, shape, dt, off):
        return self._alloc(name, shape, dt, off)

    def mark(self):
        return self.cur


def pid_attn(i):
    return {0: 0, 1: 1, 14: 4, 15: 5}.get(i, 2 if i % 2 == 0 else 3)


def pid_pool(j):
    return 0 if j == 0 else (3 if j == 15 else (1 if j % 2 == 0 else 2))


def start_attn(i):
    return min(max(i - 2, 0), 11)


def build_nc(stop=None, debug=False):
    nc = bass.Bass("TRN2", target_bir_lowering=False)

    def din(name, shape, dt=F32):
        return nc.dram_tensor(name, list(shape), dt, kind="ExternalInput").ap()

    x_in = din("x_in", [NT, D])
    w_mod = din("w_mod", [2, D, 6 * D])
    w_in = din("w_in", [2, D, IN_W])
    w_out = din("w_out", [2, D, D])
    w_up = din("w_up", [2, D, DFF])
    w_down = din("w_down", [2, DFF, D])
    pool_w = din("pool_w", [2, 4, 64, 64])
    f1_w = din("f1_w", [2, 33, 64])
    f2_w = din("f2_w", [2, 64, 64])
    f3_w = din("f3_w", [2, 64, 1024])
    small = din("small", [128, NV])
    small64 = din("small64", [64, NV64])
    gqk = din("gqk", [2, 128, 1024])
    gcol = din("gcol", [2, 128, 2])
    cachek = din("cachek", [2, 256, 512])
    cachev = din("cachev", [2, 256, 8 * 65])
    zT_d = din("zT", [33, NT])
    decay_d = din("decay", [128, 16, 256])
    wn_d = din("wn", [128, 16, 128])
    apool_d = din("apool", [128, 4 * 3 * 4 * 128], BF16)
    bias_d = din("biasbank", [2, 8, 128, 30 * 128], BF16)
    fwd_d = din("fwd", [16, 128, 16 * 128], BF16)
    inv_d = din("inv", [2, 16, 128, 1024], BF16)
    identb_d = din("identb", [128, 128], BF16)
    identf_d = din("identf", [128, 128])
    onesm_d = din("onesm", [128, 128], BF16)

    y_out = nc.dram_tensor("y", [NT, D], F32, kind="ExternalOutput").ap()
    k_out = nc.dram_tensor("kout", [2, NT, 512], F32, kind="ExternalOutput").ap()
    v_out = nc.dram_tensor("vout", [2, NT, 512], F32, kind="ExternalOutput").ap()
    khat = nc.dram_tensor("khat", [2, 32, 128, 512], F32, kind="Internal").ap()

    S = Sched(nc)
    A = Arena(nc)

    psS0 = nc.alloc_psum_tensor("psS0", [128, 1024], F32)
    psS1 = nc.alloc_psum_tensor("psS1", [128, 1024], F32)
    psO = nc.alloc_psum_tensor("psO", [128, 512], F32)
    psG0 = nc.alloc_psum_tensor("psG0", [128, 512], F32)
    psG1 = nc.alloc_psum_tensor("psG1", [128, 512], F32)
    psT = nc.alloc_psum_tensor("psT", [128, 1024], BF16)
    psT_f32 = psS1
    BANKS = [(psG0[:, :], "psG0"), (psG1[:, :], "psG1"), (psO[:, :], "psO"),
             (psS0[:, 0:512], "psS0.0"), (psS0[:, 512:1024], "psS0.1"),
             (psS1[:, 0:512], "psS1.0"), (psS1[:, 512:1024], "psS1.1")]

    smallt = A.p("small", [128, NV], F32)
    small64t = A.p("small64", [64, NV64], F32)
    identb = A.p("identb", [128, 128], BF16)
    identf = A.p("identf", [128, 128], F32)
    onesm = A.p("onesm", [128, 128], BF16)
    modv = A.p("modv", [128, 2, 48], F32)
    modA = A.p("modA", [128, 2, 16], F32)
    nw = A.p("nw", [128, 2, 12], F32)
    PRE_X = A.mark()
    xT = A.p("xT", [128, NCH, NT], F32)
    HT_OFF = A.mark()
    hT = A.p("hT", [128, NCH, NT], BF16)
    WN = A.p("wnext", [128, 8, 768], BF16)

    def prefetch(kind, l, hh=0):
        if kind == "pool":
            load_w(WN[:, :, 0:256], w_in[l], (0, D), (0, 256), "wnext")
        elif kind == "attn":
            for part in range(3):
                c0 = 1024 + part * 512 + hh * 256
                load_w(WN[:, :, part * 256:(part + 1) * 256], w_in[l], (0, D), (c0, c0 + 256), "wnext")
        elif kind == "hyena":
            load_w(WN[:, :, :], w_in[l], (0, D), (256, 1024), "wnext")
        elif kind == "mlp":
            load_w(WN[:, :, 0:512], w_up[l], (0, D), (0, 512), "wnext")

    def sm(col, n=1):
        return smallt[:, col:col + n]

    S.dma("sp", smallt[:, :], small, writes=["small"])
    S.dma("sp", small64t[:, :], small64, writes=["small64"])
    S.dma("sp", identb[:, :], identb_d, writes=["identb"])
    S.dma("sp", identf[:, :], identf_d, writes=["identf"])
    S.dma("sp", onesm[:, :], onesm_d, writes=["onesm"])
    CONSTS = ["small", "small64", "identb", "identf", "onesm"]
    eps_ap = sm(C_EPS)

    def copy_op(eng, out, in_, reads, writes):
        if eng == "act":
            S.op("act", lambda e: e.copy(out=out, in_=in_), reads, writes)
        elif eng == "dve":
            S.op("dve", lambda e: e.tensor_copy(out=out, in_=in_), reads, writes)
        else:
            S.op("pool", lambda e: e.tensor_copy(out=out, in_=in_), reads, writes)

    def load_w(dst, src_ap, rows, cols, wname):
        r0, r1 = rows
        c0, c1 = cols
        src = src_ap[r0:r1, c0:c1].rearrange("(k p) c -> p k c", p=128)
        S.dma("pool", dst, src, writes=[wname])

    def prologue(l):
        A.reset(PRE_X)
        lo = l * PL
        silu_c = A.s("silu_c", [128, 8], BF16)
        sig = A.s("sig", [128, 8], F32)
        wslabs = [A.s(f"wm{i}", [128, 8, 512], BF16) for i in range(3)]

        def adaln_steps():
            S.op("act", lambda e: e.activation(out=sig[:, :], in_=sm(C_COND, 8), func=AF.Sigmoid),
                 reads=["small"], writes=["sig"])
            S.op("dve", lambda e: e.tensor_tensor(out=silu_c[:, :], in0=sig[:, :], in1=sm(C_COND, 8),
                                                  op=ALU.mult), reads=["sig", "small"],
                 writes=["silu_c"])
            pm = psT_f32
            for sl in range(3):
                load_w(wslabs[sl][:, :, :], w_mod[l], (0, D), (sl * 512, (sl + 1) * 512), f"wm{sl}")
            yield
            for sl in range(12):
                wt = wslabs[sl % 3]
                wn_ = f"wm{sl % 3}"

                def mm(e, sl=sl, wt=wt):
                    ins = None
                    for jc in range(4):
                        j = sl * 4 + jc
                        for k in range(8):
                            ins = e.matmul(pm[:, j:j + 1], lhsT=wt[:, k, jc * 128:(jc + 1) * 128],
                                           rhs=silu_c[:, k:k + 1], start=(k == 0), stop=(k == 7))
                    return ins
                S.op("pe", mm, reads=[wn_, "silu_c"], writes=["psS1.0"])
                if sl + 3 < 12:
                    load_w(wt[:, :, :], w_mod[l], (0, D), ((sl + 3) * 512, (sl + 4) * 512), wn_)
                yield
            S.op("dve", lambda e: e.tensor_tensor(out=modv[:, l, :], in0=pm[:, 0:48],
                                                  in1=sm(lo + C_BMOD, 48), op=ALU.add),
                 reads=["psS1.0", "small"], writes=["modv"])
            for which, (cg, cs) in enumerate(((C_N1G, 8), (C_N2G, 32))):
                S.op("dve", lambda e, which=which, cg=cg, cs=cs: e.scalar_tensor_tensor(
                    out=modA[:, l, which * 8:(which + 1) * 8], in0=modv[:, l, cs:cs + 8], scalar=1.0,
                    in1=sm(lo + cg, 8), op0=ALU.add, op1=ALU.mult),
                    reads=["modv", "small"], writes=["modA"])
            for which, tap in enumerate((0, 2)):
                S.op("dve", lambda e, which=which, tap=tap: e.tensor_scalar(
                    out=nw[:, l, which * 6:(which + 1) * 6], in0=sm(lo + C_CW + tap * 6, 6),
                    scalar1=sm(C_FLAG), scalar2=-1.0, op0=ALU.mult, op1=ALU.mult),
                    reads=["small"], writes=["nw"])
            yield

        ada = adaln_steps()

        def ada_step():
            try:
                next(ada)
            except StopIteration:
                pass

        ada_step()
        h2 = A.s("h2", [64, NT], BF16)
        w1 = A.s("w1", [33, 64], F32)
        w2 = A.s("w2", [64, 64], F32)
        w3 = A.s("w3", [64, 1024], BF16)
        decay = A.s("decay", [128, 16, 256], F32)
        wn = A.s("wn", [128, 16, 128], BF16)
        fb = A.s("fb", [64, 2], F32)
        ovl = A.mark()
        zT = A.s("zT", [33, NT], F32)
        pre = A.s("pre", [64, NT], F32)
        tmp = A.s("tmp", [64, NT], F32)
        h1 = A.s("h1", [64, NT], F32)
        S.dma("sp", zT[:, :], zT_d, writes=["zT"])
        S.dma("sp", w1[:, :], f1_w[l], writes=["w1"])
        S.dma("sp", w2[:, :], f2_w[l], writes=["w2"])
        S.dma("pool", w3[:, :], f3_w[l], writes=["w3"])
        S.dma("sp", decay[:, :, :], decay_d, writes=["decay"])
        S.dma("pool", wn[:, :, :], wn_d, writes=["wn"])
        for li in range(2):
            S.op("dve", lambda e, li=li: e.tensor_tensor(
                out=fb[:, li:li + 1], in0=small64t[:, l * 4 + 2 * li:l * 4 + 2 * li + 1],
                in1=small64t[:, l * 4 + 2 * li + 1:l * 4 + 2 * li + 2], op=ALU.mult),
                reads=["small64"], writes=["fb"])

        def sine_layer(li, wmat, kdim, src, dst, srcname, dstname):
            for b in range(4):
                bank, bname = BANKS[b % 2]
                S.op("pe", lambda e, b=b, bank=bank: e.matmul(
                    bank[0:64, :], lhsT=wmat[0:kdim, :], rhs=src[0:kdim, b * 512:(b + 1) * 512],
                    start=True, stop=True), reads=[srcname, f"w{li + 1}"], writes=[bname])
                S.op("dve", lambda e, b=b, bank=bank: e.tensor_scalar(
                    out=pre[:, b * 512:(b + 1) * 512], in0=bank[0:64, :],
                    scalar1=small64t[:, l * 4 + 2 * li + 1:l * 4 + 2 * li + 2],
                    scalar2=fb[:, li:li + 1], op0=ALU.mult, op1=ALU.add),
                    reads=[bname, "small64", "fb"], writes=["pre"])
            S.op("dve", lambda e: e.tensor_scalar(out=tmp[:, :], in0=pre[:, :], scalar1=1.0 / TWO_PI,
                                                  scalar2=MAGIC, op0=ALU.mult, op1=ALU.add),
                 reads=["pre"], writes=["tmp"])
            S.op("dve", lambda e: e.tensor_scalar(out=tmp[:, :], in0=tmp[:, :], scalar1=MAGIC,
                                                  scalar2=-TWO_PI, op0=ALU.subtract, op1=ALU.mult),
                 reads=["tmp"], writes=["tmp"])
            S.op("dve", lambda e: e.tensor_tensor(out=tmp[:, :], in0=tmp[:, :], in1=pre[:, :],
                                                  op=ALU.add), reads=["tmp", "pre"], writes=["tmp"])
            S.op("dve", lambda e: e.tensor_scalar(out=tmp[:, :], in0=tmp[:, :], scalar1=PI_C,
                                                  scalar2=-PI_C, op0=ALU.min, op1=ALU.max),
                 reads=["tmp"], writes=["tmp"])
            S.op("act", lambda e: e.activation(out=dst[:, :], in_=tmp[:, :], func=AF.Sin),
                 reads=["tmp"], writes=[dstname])

        sine_layer(0, w1, 33, zT, h1, "zT", "h1")
        ada_step()
        sine_layer(1, w2, 64, h1, h2, "h1", "h2")
        ada_step()
        ada_step()

        A.reset(ovl)
        a_t = A.s("a_t", [128, 16, 512], BF16)
        d_t = A.s("d_t", [128, 16, 512], BF16)
        KD_OFF = A.mark()
        kd = A.s("kd", [128, 16, 1024], BF16)
        absk = [A.s(f"absk{i}", [128, 1024], BF16) for i in range(2)]
        def kd_mm(t):
            for cb in range(2):
                bank, bname = BANKS[cb] if t % 2 == 0 else BANKS[4 + 2 * cb]
                S.op("pe", lambda e, cb=cb, bank=bank: e.matmul(
                    bank, lhsT=h2[:, t * 128:(t + 1) * 128], rhs=w3[:, cb * 512:(cb + 1) * 512],
                    start=True, stop=True), reads=["h2", "w3"], writes=[bname])

        def kd_ew(t):
            for cb in range(2):
                bank, bname = BANKS[cb] if t % 2 == 0 else BANKS[4 + 2 * cb]
                dcy = decay[:, t, :]
                dcy_b = bass.AP(tensor=dcy.tensor, offset=dcy.offset,
                                ap=[[dcy.ap[0][0], 128], [0, 2], [1, 256]])
                S.op("dve", lambda e, cb=cb, bank=bank, dcy_b=dcy_b: e.tensor_tensor(
                    out=kd[:, t, cb * 512:(cb + 1) * 512].rearrange("p (h c) -> p h c", h=2),
                    in0=bank.rearrange("p (h c) -> p h c", h=2), in1=dcy_b, op=ALU.mult),
                    reads=[bname, "decay"], writes=[f"kd{t}"])
            ak = absk[t % 2]
            S.op("act", lambda e, ak=ak: e.activation(out=ak[:, :], in_=kd[:, t, :], func=AF.Abs),
                 reads=[f"kd{t}"], writes=[f"absk{t % 2}"])

        def kd_norm(t):
            ak = absk[t % 2]
            for cb in range(2):
                bank, bname = BANKS[2 + cb]
                S.op("pe", lambda e, cb=cb, bank=bank, ak=ak: e.matmul(
                    bank, lhsT=wn[:, t, :], rhs=ak[:, cb * 512:(cb + 1) * 512],
                    start=(t == 0), stop=(t == 15)), reads=[f"absk{t % 2}", "wn"], writes=[bname])

        kd_mm(0)
        for t in range(16):
            if t + 1 < 16:
                kd_mm(t + 1)
            kd_ew(t)
            if t >= 1:
                kd_norm(t - 1)
            if t % 4 == 3:
                ada_step()
        kd_norm(15)
        rnf = A.s("rnf", [128, 1024], F32)
        rn = A.s("rn", [128, 1024], BF16)
        for cb in range(2):
            bank, bname = BANKS[2 + cb]
            S.op("dve", lambda e, cb=cb, bank=bank: e.tensor_scalar(
                out=rnf[:, cb * 512:(cb + 1) * 512], in0=bank, scalar1=1e-6, scalar2=None,
                op0=ALU.add), reads=[bname], writes=["rnf"])
        S.op("dve", lambda e: e.reciprocal(out=rnf[:, :], in_=rnf[:, :]), reads=["rnf"], writes=["rnf"])
        copy_op("act", rn[:, :], rnf[:, :], ["rnf"], ["rn"])
        t1 = [A.s(f"t1{i}", [128, 512], BF16) for i in range(2)]
        t2 = [A.s(f"t2{i}", [128, 512], BF16) for i in range(2)]
        for t in range(16):
            u1, u2 = t1[t % 2], t2[t % 2]
            n1, n2 = f"t1{t % 2}", f"t2{t % 2}"
            S.op("dve", lambda e, t=t, u1=u1: e.tensor_tensor(out=u1[:, :], in0=kd[:, t, 0:512],
                                                             in1=rn[:, 0:512], op=ALU.mult),
                 reads=[f"kd{t}", "rn"], writes=[n1])
            S.op("dve", lambda e, t=t, u2=u2: e.scalar_tensor_tensor(
                out=u2[:, :], in0=kd[:, t, 512:1024], scalar=sm(C_TAU0 + t), in1=rn[:, 512:1024],
                op0=ALU.mult, op1=ALU.mult), reads=[f"kd{t}", "rn", "small"], writes=[n2])
            S.op("dve", lambda e, t=t, u1=u1, u2=u2: e.tensor_tensor(
                out=a_t[:, t, :], in0=u1[:, :], in1=u2[:, :], op=ALU.add), reads=[n1, n2],
                writes=["a_t"])
            S.op("dve", lambda e, t=t, u1=u1, u2=u2: e.tensor_tensor(
                out=d_t[:, t, :], in0=u1[:, :], in1=u2[:, :], op=ALU.subtract), reads=[n1, n2],
                writes=["d_t"])
            if t % 4 == 3:
                ada_step()
        NR = 6
        ring = [A.s(f"fw{i}", [128, 16, 128], BF16) for i in range(NR)]
        kst = [A.s(f"kst{i}", [128, 512], F32) for i in range(4)]
        est = [A.s(f"est{i}", [128, 512], F32) for i in range(2)]

        def load_fw(jt):
            S.dma("sp", ring[jt % NR][:, :, :], fwd_d[jt].rearrange("p (s f) -> p s f", s=16),
                  writes=[f"fw{jt % NR}"])
        for jt in range(NR):
            load_fw(jt)
        for jt in range(16):
            ri, j = jt // 8, jt % 8
            fw = ring[jt % NR]
            fn_ = f"fw{jt % NR}"
            (bE, nE), (bO, nO) = (BANKS[0], BANKS[1]) if jt % 2 == 0 else (BANKS[3], BANKS[4])
            src, sname = (a_t, "a_t") if ri == 0 else (d_t, "d_t")
            for half, (bank, bname) in enumerate(((bE, nE), (bO, nO))):
                def mm(e, fw=fw, bank=bank, src=src, half=half):
                    ins = None
                    for s_ in range(8):
                        ins = e.matmul(bank, lhsT=fw[:, half * 8 + s_, :], rhs=src[:, half * 8 + s_, :],
                                       start=(s_ == 0), stop=(s_ == 7))
                    return ins
                S.op("pe", mm, reads=[fn_, sname], writes=[bname])
            if jt + NR < 16:
                load_fw(jt + NR)
            es, esn = est[jt % 2], f"est{jt % 2}"
            copy_op("act", es[:, :], bE, [nE], [esn])
            kb_, kbn_ = kst[(jt % 2) * 2], f"kst{(jt % 2) * 2}"
            km_, kmn_ = kst[(jt % 2) * 2 + 1], f"kst{(jt % 2) * 2 + 1}"
            S.op("dve", lambda e, bO=bO, es=es, kb_=kb_: e.tensor_tensor(
                out=kb_[:, :], in0=bO, in1=es[:, :], op=ALU.add), reads=[nO, esn], writes=[kbn_])
            S.op("dve", lambda e, bO=bO, es=es, km_=km_: e.scalar_tensor_tensor(
                out=km_[:, :], in0=bO, scalar=-1.0, in1=es[:, :], op0=ALU.mult, op1=ALU.add),
                reads=[nO, esn], writes=[kmn_])
            S.dma("sp", khat[l, ri * 16 + j], kb_[:, :], reads=[kbn_], writes=["khat"])
            S.dma("sp", khat[l, ri * 16 + 8 + j], km_[:, :], reads=[kmn_], writes=["khat"])
            ada_step()
        for _ in range(20):
            ada_step()
        S.barrier()

    for l in range(2):
        prologue(l)
    A.reset()

    prefetch("pool", 0)
    xst = [A.s(f"xst{i}", [128, D], F32) for i in range(2)]
    for i in range(NTILE):
        st = xst[i % 2]
        sn = f"xst{i % 2}"
        S.dma("sp", st[:, :], x_in[i * 128:(i + 1) * 128, :], writes=[sn])
        for hb in range(2):
            bank, bname = BANKS[hb]

            def tr(e, hb=hb, bank=bank, st=st):
                ins = None
                for kk in range(4):
                    k = hb * 4 + kk
                    ins = e.transpose(out=bank[:, kk * 128:(kk + 1) * 128],
                                      in_=st[:, k * 128:(k + 1) * 128], identity=identf[:, :])
                return ins
            S.op("pe", tr, reads=[sn, "identf"], writes=[bname])
            copy_op("act" if hb == 0 else "dve",
                    xT[:, hb * 4:(hb + 1) * 4, i * 128:(i + 1) * 128],
                    bank.rearrange("p (k t) -> p k t", k=4), [bname], [f"xT{i // 4}"])
    S.barrier()

    def HTN(b):
        return [f"hT{b}.{k}" for k in range(8)]

    def rmsnorm(l, which):
        A.reset()
        rstd_l = [A.s(f"rstd{i}", [128, 512], F32) for i in range(2)]
        lnv_l = [A.s(f"lnv{i}", [128, 512], F32) for i in range(2)]
        tmpn = [A.s(f"tmpn{i}", [128, 512], F32) for i in range(4)]
        cB = 0 if which == 0 else 24
        nt_ = [0]

        def stage1(b):
            bs = slice(b * 512, (b + 1) * 512)
            for k in range(8):
                if k % 2 == 0:
                    S.op("act", lambda e, k=k: e.activation(out=hT[:, k, bs], in_=xT[:, k, bs],
                                                            func=AF.Square),
                         reads=[f"xT{b}"], writes=[f"hT{b}.{k}"])
                else:
                    S.op("dve", lambda e, k=k: e.tensor_tensor(out=hT[:, k, bs], in0=xT[:, k, bs],
                                                               in1=xT[:, k, bs], op=ALU.mult),
                         reads=[f"xT{b}"], writes=[f"hT{b}.{k}"])
            bank, bname = BANKS[b % 2]

            def mm(e):
                ins = None
                for k in range(8):
                    ins = e.matmul(bank, lhsT=onesm[:, :], rhs=hT[:, k, bs], start=(k == 0),
                                   stop=(k == 7))
                return ins
            S.op("pe", mm, reads=HTN(b) + ["onesm"], writes=[bname])

        def stage2(b):
            bs = slice(b * 512, (b + 1) * 512)
            rstd, lnv = rstd_l[b % 2], lnv_l[b % 2]
            Nr, Nl = f"rstd{b % 2}", f"lnv{b % 2}"
            bank, bname = BANKS[b % 2]
            S.op("act", lambda e: e.activation(out=lnv[:, :], in_=bank, func=AF.Ln, bias=eps_ap),
                 reads=[bname, "small"], writes=[Nl])
            S.op("act", lambda e: e.activation(out=rstd[:, :], in_=lnv[:, :], func=AF.Exp, scale=-0.5),
                 reads=[Nl], writes=[Nr])
            for k in range(8):
                tn = tmpn[nt_[0] % 4]
                tnn = f"tmpn{nt_[0] % 4}"
                nt_[0] += 1
                S.op("dve", lambda e, k=k, tn=tn: e.scalar_tensor_tensor(
                    out=tn[:, :], in0=xT[:, k, bs], scalar=modA[:, l, which * 8 + k:which * 8 + k + 1],
                    in1=rstd[:, :], op0=ALU.mult, op1=ALU.mult),
                    reads=[f"xT{b}", "modA", Nr], writes=[tnn])
                S.op("act", lambda e, k=k, tn=tn: e.activation(
                    out=hT[:, k, bs], in_=tn[:, :], func=AF.Identity,
                    bias=modv[:, l, cB + k:cB + k + 1]), reads=[tnn, "modv"],
                    writes=[f"hT{b}.{k}"])

        stage1(0)
        for b in range(4):
            if b + 1 < 4:
                stage1(b + 1)
            stage2(b)
        S.barrier()

    def wout_load(l, row0, nk):
        wo = A.s("wo", [128, nk, D], BF16)
        load_w(wo[:, :, :], w_out[l], (row0, row0 + nk * 128), (0, D), "wo")
        return wo

    def wout_partial(l, yT, yname, row0, nk, gate_col, wo=None, rhs_fn=None, ynames=None):
        if wo is None:
            wo = wout_load(l, row0, nk)
        if rhs_fn is None:
            rhs_fn = lambda kc, b: yT[:, kc, b * 512:(b + 1) * 512]
        n = 0
        for dc in range(8):
            for b in range(4):
                bs = slice(b * 512, (b + 1) * 512)
                bank, bname = BANKS[n % 4]
                n += 1

                def mm(e, dc=dc, b=b, bank=bank):
                    ins = None
                    for kc in range(nk):
                        ins = e.matmul(bank, lhsT=wo[:, kc, dc * 128:(dc + 1) * 128], rhs=rhs_fn(kc, b),
                                       start=(kc == 0), stop=(kc == nk - 1))
                    return ins
                S.op("pe", mm, reads=["wo"] + (ynames(b) if ynames else [yname]), writes=[bname])
                S.op("dve", lambda e, dc=dc, bs=bs, bank=bank: e.scalar_tensor_tensor(
                    out=xT[:, dc, bs], in0=bank, scalar=modv[:, l, gate_col + dc:gate_col + dc + 1],
                    in1=xT[:, dc, bs], op0=ALU.mult, op1=ALU.add),
                    reads=[bname, "modv", f"xT{b}"], writes=[f"xT{b}"])

    Y_DONE = [False]

    def store_y_tile(i, st, sn, banks2):
        for hb in range(2):
            bank, bname = banks2[hb]

            def tr(e, hb=hb, bank=bank):
                ins = None
                for kk in range(4):
                    k = hb * 4 + kk
                    ins = e.transpose(out=bank[:, kk * 128:(kk + 1) * 128],
                                      in_=xT[:, k, i * 128:(i + 1) * 128], identity=identf[:, :])
                return ins
            S.op("pe", tr, reads=[f"xT{i // 4}", "identf"], writes=[bname])
            copy_op("act", st[:, hb * 512:(hb + 1) * 512], bank, [bname], [sn])
        S.dma("sp", y_out[i * 128:(i + 1) * 128, :], st[:, :], reads=[sn])

    def pool_phase(l):
        A.reset()
        lo = l * PL
        wp = WN
        wo_pre = wout_load(l, 0, 2)
        apool = A.s("apool", [128, 4, 3, 4, 128], BF16)
        S.dma("sp", apool[:, :, :, :, :],
              apool_d.rearrange("p (a r g t) -> p a r g t", a=4, r=3, g=4), writes=["apool"])
        wblk = A.s("wblk", [128, 2, 128], BF16)
        S.op("dve", lambda e: e.memset(wblk[:, :, :], 0.0), writes=["wblk"])
        for g in range(4):
            gp, gl = g // 2, g % 2
            S.dma("pool", wblk[gl * 64:(gl + 1) * 64, gp, gl * 64:(gl + 1) * 64], pool_w[l, g],
                  writes=["wblk"])
        upool = A.s("upool", [128, 16, 256], BF16)
        for i in range(16):
            bank, bname = BANKS[i % 2]

            def mm(e, i=i, bank=bank):
                ins = None
                for k in range(8):
                    ins = e.matmul(bank[:, 0:256], lhsT=hT[:, k, i * 128:(i + 1) * 128],
                                   rhs=wp[:, k, 0:256], start=(k == 0), stop=(k == 7))
                return ins
            S.op("pe", mm, reads=HTN(i // 4) + ["wnext"], writes=[bname])
            copy_op("act" if i % 2 == 0 else "dve", upool[:, i, :], bank[:, 0:256], [bname],
                    [f"upool{i}"])
        prefetch("attn", l, 0)
        pooledT = A.s("pooledT", [128, 2, NT], BF16)
        n = 0
        for b in range(4):
            for gp in range(2):
                for gl in range(2):
                    g = gp * 2 + gl
                    bank, bname = BANKS[n % 4]
                    n += 1

                    def mm(e, b=b, gp=gp, g=g, bank=bank):
                        ins = None
                        for jl in range(4):
                            j = b * 4 + jl
                            rs = [r for r in (-1, 0, 1) if 0 <= j + r < 16]
                            for r in rs:
                                ins = e.matmul(bank[:, jl * 128:(jl + 1) * 128],
                                               lhsT=upool[:, j + r, gp * 128:(gp + 1) * 128],
                                               rhs=apool[:, pid_pool(j), r + 1, g, :],
                                               start=(r == rs[0]), stop=(r == rs[-1]))
                        return ins
                    rd = [f"upool{j}" for j in range(max(0, b * 4 - 1), min(16, b * 4 + 5))]
                    S.op("pe", mm, reads=rd + ["apool"], writes=[bname])
                    copy_op("act" if gl == 0 else "dve",
                            pooledT[gl * 64:(gl + 1) * 64, gp, b * 512:(b + 1) * 512],
                            bank[gl * 64:(gl + 1) * 64, :], [bname], [f"pooledT{b}.{gp}.{gl}"])
        ypT = A.s("ypT", [128, 2, NT], BF16)
        for b in range(4):
            for gp in range(2):
                bank, bname = BANKS[(b * 2 + gp) % 4]
                S.op("pe", lambda e, b=b, gp=gp, bank=bank: e.matmul(
                    bank, lhsT=wblk[:, gp, :], rhs=pooledT[:, gp, b * 512:(b + 1) * 512],
                    start=True, stop=True),
                    reads=["wblk", f"pooledT{b}.{gp}.0", f"pooledT{b}.{gp}.1"], writes=[bname])
                S.op("act", lambda e, b=b, gp=gp, bank=bank: e.activation(
                    out=ypT[:, gp, b * 512:(b + 1) * 512], in_=bank, func=AF.Identity,
                    scale=sm(lo + C_PSC + gp)), reads=[bname, "small"], writes=["ypT"])
        wout_partial(l, ypT, "ypT", 0, 2, 16, wo=wo_pre)
        S.barrier()

    def attn_phase(l, hh):
        A.reset()
        QT = A.s("QT", [128, 2, NT], BF16)
        KT = A.s("KT", [128, 2, NT], BF16)
        Vaug = A.s("Vaug", [128, 16, 4, 65], BF16)
        kc_tok = A.s("kc_tok", [128, 2, 256], BF16)
        KcT = A.s("KcT", [128, 2, 256], BF16)
        Vc = A.s("Vc", [128, 2, 4, 65], BF16)
        ytok = A.s("ytok", [128, 16, 256], BF16)
        gprod = A.s("gprod", [128, 1], F32)
        gkrow = A.s("gkrow", [128, 256], F32)
        wo_pre = wout_load(l, 512 + hh * 256, 2)
        ebt = [A.s(f"ebt{i}", [128, 30, 128], BF16) for i in range(2)]
        S.dma("sp", ebt[0][:, :, :], bias_d[l, hh * 4].rearrange("p (a k) -> p a k", a=30),
              writes=["ebt0"])
        markB = A.mark()
        wqkv = WN
        S.dma("sp", gkrow[:, :], gqk[l, :, 512:768], writes=["gkrow"])
        gtmp = A.s("gtmp", [128, 2], F32)
        S.dma("sp", gtmp[:, :], gcol[l], writes=["gtmp"])
        S.op("dve", lambda e: e.tensor_tensor(out=gprod[:, :], in0=gtmp[:, 0:1], in1=gtmp[:, 1:2],
                                              op=ALU.mult), reads=["gtmp"], writes=["gprod"])
        S.op("dve", lambda e: e.memset(Vaug[:, :, :, 64:65], 1.0), writes=["Vones"])
        S.dma("pool", kc_tok[:, :, :],
              cachek[l].rearrange("(t p) c -> p t c", p=128)[:, :, hh * 256:(hh + 1) * 256],
              writes=["kc_tok"])
        S.dma("pool", Vc[:, :, :, :],
              cachev[l].rearrange("(t p) (h c) -> p t h c", p=128, c=65)[:, :, hh * 4:(hh + 1) * 4, :],
              writes=["Vc"])

        qkf_l = [A.s(f"qkf{i}", [128, 512], F32) for i in range(2)]
        sq_l = [A.s(f"sq{i}", [128, 512], F32) for i in range(2)]
        ss_l = [A.s(f"ss{i}", [128, 8], F32) for i in range(2)]
        rs_l = [A.s(f"rs{i}", [128, 8], F32) for i in range(2)]
        qn = [A.s(f"qn{i}", [128, 512], F32) for i in range(2)]
        qkb_l = [A.s(f"qkb{i}", [128, 512], BF16) for i in range(3)]
        vf = [A.s(f"vf{i}", [128, 256], F32) for i in range(2)]

        def bc64(t):
            a_ = t[:, :]
            return bass.AP(tensor=a_.tensor, offset=a_.offset, ap=[[8, 128], [1, 8], [0, 64]])

        def stage_mm(i):
            ts_ = slice(i * 128, (i + 1) * 128)
            bqk, nqk = BANKS[0] if i % 2 == 0 else BANKS[3]
            bv, nv = BANKS[1] if i % 2 == 0 else BANKS[4]

            def mmqk(e):
                ins = None
                for k in range(8):
                    ins = e.matmul(bqk, lhsT=hT[:, k, ts_], rhs=wqkv[:, k, 0:512], start=(k == 0),
                                   stop=(k == 7))
                return ins
            S.op("pe", mmqk, reads=HTN(i // 4) + ["wnext"], writes=[nqk])

            def mmv(e):
                ins = None
                for k in range(8):
                    ins = e.matmul(bv[:, 0:256], lhsT=hT[:, k, ts_], rhs=wqkv[:, k, 512:768],
                                   start=(k == 0), stop=(k == 7))
                return ins
            S.op("pe", mmv, reads=HTN(i // 4) + ["wnext"], writes=[nv])

        def stage_ew1(i):
            ts_ = slice(i * 128, (i + 1) * 128)
            bqk, nqk = BANKS[0] if i % 2 == 0 else BANKS[3]
            bv, nv = BANKS[1] if i % 2 == 0 else BANKS[4]
            p2 = i % 2
            qkf, sq, ss = qkf_l[p2], sq_l[p2], ss_l[p2]
            Nq, Ns, Nss = f"qkf{p2}", f"sq{p2}", f"ss{p2}"
            vfi = vf[p2]
            copy_op("act", qkf[:, :], bqk, [nqk], [Nq])
            copy_op("act", vfi[:, :], bv[:, 0:256], [nv], [f"vf{p2}"])
            copy_op("dve", Vaug[:, i, :, 0:64], bv[:, 0:256].rearrange("p (h c) -> p h c", h=4),
                    [nv], [f"Vaug{i}"])
            S.dma("sp", v_out[l, ts_, hh * 256:(hh + 1) * 256], vfi[:, :], reads=[f"vf{p2}"])
            S.op("act", lambda e: e.activation(out=sq[:, :], in_=bqk, func=AF.Square),
                 reads=[nqk], writes=[Ns])
            S.op("dve", lambda e: e.tensor_reduce(
                out=ss[:, :], in_=sq[:, :].rearrange("p (h c) -> p h c", h=8), axis=AX.X, op=ALU.add),
                reads=[Ns], writes=[Nss])

        def stage_ew2(i):
            ts_ = slice(i * 128, (i + 1) * 128)
            p2 = i % 2
            qkf, sq, ss, rs_ = qkf_l[p2], sq_l[p2], ss_l[p2], rs_l[p2]
            qkb = qkb_l[i % 3]
            Nq, Ns, Nss, Nrs, Nqb = f"qkf{p2}", f"sq{p2}", f"ss{p2}", f"rs{p2}", f"qkb{i % 3}"
            S.op("act", lambda e: e.activation(out=rs_[:, :], in_=ss[:, :], func=AF.Ln,
                                               scale=1.0 / 64.0, bias=eps_ap),
                 reads=[Nss, "small"], writes=[Nrs])
            S.op("act", lambda e: e.activation(out=rs_[:, :], in_=rs_[:, :], func=AF.Exp, scale=-0.5),
                 reads=[Nrs], writes=[Nrs])
            S.op("dve", lambda e: e.tensor_tensor(
                out=qkb[:, :].rearrange("p (h c) -> p h c", h=8),
                in0=qkf[:, :].rearrange("p (h c) -> p h c", h=8), in1=bc64(rs_), op=ALU.mult),
                reads=[Nq, Nrs], writes=[Nqb])
            qni = qn[p2]
            S.op("pool", lambda e: e.tensor_tensor(
                out=sq[:, 0:256].rearrange("p (h c) -> p h c", h=4),
                in0=qkf[:, 256:512].rearrange("p (h c) -> p h c", h=4),
                in1=bass.AP(tensor=rs_[:, :].tensor, offset=rs_[:, :].offset + 4,
                            ap=[[8, 128], [1, 4], [0, 64]]), op=ALU.mult),
                reads=[Nq, Nrs, Ns], writes=[Ns])
            S.op("pool", lambda e: e.tensor_tensor(out=qni[:, 0:256], in0=sq[:, 0:256], in1=gkrow[:, :],
                                                   op=ALU.mult), reads=[Ns, "gkrow"],
                 writes=[f"qn{p2}"])
            S.dma("sp", k_out[l, ts_, hh * 256:(hh + 1) * 256], qni[:, 0:256], reads=[f"qn{p2}"])

        def stage_tr(i):
            ts_ = slice(i * 128, (i + 1) * 128)
            qkb = qkb_l[i % 3]
            Nqb = f"qkb{i % 3}"

            def trq(e):
                ins = None
                for c4 in range(4):
                    ins = e.transpose(out=psT[:, c4 * 128:(c4 + 1) * 128],
                                      in_=qkb[:, c4 * 128:(c4 + 1) * 128], identity=identb[:, :])
                return ins
            S.op("pe", trq, reads=[Nqb, "identb"], writes=["psT"])
            copy_op("dve", QT[:, :, ts_], psT[:, 0:256].rearrange("p (c t) -> p c t", c=2), ["psT"],
                    [f"QT{i}"])
            S.op("act", lambda e: e.activation(
                out=KT[:, :, ts_], in_=psT[:, 256:512].rearrange("p (c t) -> p c t", c=2),
                func=AF.Identity, scale=gprod[:, 0:1]), reads=["psT", "gprod"], writes=[f"KT{i}"])

        stage_mm(0)
        stage_mm(1)
        def trc(e):
            ins = None
            for t in range(2):
                for c in range(2):
                    ins = e.transpose(out=psT[:, (t * 2 + c) * 128:(t * 2 + c + 1) * 128],
                                      in_=kc_tok[:, t, c * 128:(c + 1) * 128], identity=identb[:, :])
            return ins
        S.op("pe", trc, reads=["kc_tok", "identb"], writes=["psT"])
        for t in range(2):
            S.op("act", lambda e, t=t: e.activation(
                out=KcT[:, :, t * 128:(t + 1) * 128],
                in_=psT[:, t * 256:(t + 1) * 256].rearrange("p (c k) -> p c k", c=2),
                func=AF.Identity, scale=gtmp[:, 0:1]), reads=["psT", "gtmp"], writes=["KcT"])

        stage_ew1(0)
        for i in range(16):
            if i + 2 < 16:
                stage_mm(i + 2)
            if i + 1 < 16:
                stage_ew1(i + 1)
            stage_ew2(i)
            if i >= 1:
                stage_tr(i - 1)
        stage_tr(15)
        S.op("act", lambda e: e.activation(out=ebt[0][:, :, :], in_=ebt[0][:, :, :], func=AF.Exp),
             reads=["ebt0"], writes=["ebt0"])
        S.barrier()

        A.reset(markB)
        if hh == 0:
            prefetch("attn", l, 1)
        else:
            prefetch("hyena", l)
        PT = A.s("PT", [128, 16, 5, 128], BF16)
        PTc = [A.s(f"PTc{i}", [128, 2, NT], BF16) for i in range(2)]
        rcp = [A.s(f"rcp{i}", [128, 1], F32) for i in range(2)]
        psS = [(psS0, ["psS0.0", "psS0.1"]), (psS1, ["psS1.0", "psS1.1"])]
        psOs = [(psO, "psO"), (psG1, "psG1")]
        users = [[i for i in range(16) if start_attn(i) <= j <= start_attn(i) + 4] for j in range(16)]
        done_at = [[i for i in range(16) if start_attn(i) + 4 == j] for j in range(16)]
        PT_ap = PT[:, :, :, :]

        def pt_out(i0, n, j):
            o0 = i0 * 640 + (j - start_attn(i0)) * 128
            if n == 1:
                stride = 640
            else:
                o1 = (i0 + 1) * 640 + (j - start_attn(i0 + 1)) * 128
                stride = o1 - o0
            return bass.AP(tensor=PT_ap.tensor, offset=PT_ap.offset + o0,
                           ap=[[PT_ap.ap[0][0], 128], [stride, n], [1, 128]])

        def runs(us, j):
            out, cur = [], [us[0]]
            for i in us[1:]:
                d_new = (i * 640 + (j - start_attn(i)) * 128) - (cur[-1] * 640 + (j - start_attn(cur[-1])) * 128)
                if len(cur) >= 2:
                    d_old = (cur[1] * 640 + (j - start_attn(cur[1])) * 128) - (cur[0] * 640 + (j - start_attn(cur[0])) * 128)
                    if d_new != d_old:
                        out.append(cur)
                        cur = [i]
                        continue
                cur.append(i)
            out.append(cur)
            return out

        pvn = [0]

        def head_ctx_steps(hl, two_banks=False):
            h = hh * 4 + hl
            c, hp = hl // 2, hl % 2
            pr = slice(hp * 64, (hp + 1) * 64)
            eb, ebn = ebt[hl % 2], f"ebt{hl % 2}"
            ptc, ptcn = PTc[hl % 2], f"PTc{hl % 2}"
            if hl > 0:
                S.dma("sp", eb[:, :, :], bias_d[l, h].rearrange("p (a k) -> p a k", a=30), writes=[ebn])
                yield
                for pc in range(6):
                    S.op("act", lambda e, pc=pc: e.activation(
                        out=eb[:, pc * 5:pc * 5 + 5, :], in_=eb[:, pc * 5:pc * 5 + 5, :], func=AF.Exp),
                        reads=[ebn], writes=[ebn])
                    yield
            n_ = 0
            for t in range(2):
                for b_ in range(4):
                    cb, cbn = (psG1, "psG1") if (two_banks and n_ % 2 == 1) else (psG0, "psG0")
                    n_ += 1
                    S.op("pe", lambda e, cb=cb: e.matmul(
                        cb[:, :], lhsT=KcT[pr, c, t * 128:(t + 1) * 128],
                        rhs=QT[pr, c, b_ * 512:(b_ + 1) * 512], start=True, stop=True),
                        reads=["KcT"] + [f"QT{i}" for i in range(b_ * 4, b_ * 4 + 4)], writes=[cbn])
                    S.op("act", lambda e, cb=cb: e.activation(
                        out=ptc[:, t, b_ * 512:(b_ + 1) * 512], in_=cb[:, :], func=AF.Exp, scale=0.125),
                        reads=[cbn], writes=[f"{ptcn}.{b_}"])
                    yield

        g0 = head_ctx_steps(0, two_banks=True)
        for _ in g0:
            pass
        for hl in range(4):
            h = hh * 4 + hl
            c, hp = hl // 2, hl % 2
            pr = slice(hp * 64, (hp + 1) * 64)
            eb, ebn = ebt[hl % 2], f"ebt{hl % 2}"
            ptc, ptcn = PTc[hl % 2], f"PTc{hl % 2}"
            gnext = head_ctx_steps(hl + 1) if hl + 1 < 4 else iter(())

            def mm_s(j):
                us = users[j]
                i0, n = us[0], len(us)
                pS, pSn = psS[j % 2]

                def f(e):
                    ins = None
                    for c0 in range(0, n * 128, 512):
                        c1 = min(n * 128, c0 + 512)
                        ins = e.matmul(pS[:, c0:c1], lhsT=KT[pr, c, j * 128:(j + 1) * 128],
                                       rhs=QT[pr, c, i0 * 128 + c0:i0 * 128 + c1], start=True, stop=True)
                    return ins
                S.op("pe", f, reads=[f"KT{j}"] + [f"QT{i}" for i in us], writes=pSn)

            def exp_s(j):
                us = users[j]
                i0 = us[0]
                pS, pSn = psS[j % 2]
                for run in runs(us, j):
                    r0, n = run[0], len(run)
                    S.op("act", lambda e, r0=r0, n=n: e.activation(
                        out=pt_out(r0, n, j),
                        in_=pS[:, (r0 - i0) * 128:(r0 - i0 + n) * 128].rearrange("p (i q) -> p i q", i=n),
                        func=AF.Exp, scale=0.125), reads=pSn, writes=[f"PT{i}" for i in run])

            def bias_mul(i):
                S.op("dve", lambda e: e.tensor_tensor(out=PT[:, i, :, :], in0=PT[:, i, :, :],
                                                      in1=eb[:, pid_attn(i) * 5:pid_attn(i) * 5 + 5, :],
                                                      op=ALU.mult), reads=[f"PT{i}", ebn],
                     writes=[f"PT{i}"])

            def pv(i):
                st = start_attn(i)
                pO, pOn = psOs[pvn[0] % 2]
                rc, rcn = rcp[pvn[0] % 2], f"rcp{pvn[0] % 2}"
                pvn[0] += 1

                def f(e):
                    ins = None
                    for s_ in range(7):
                        if s_ < 5:
                            lhsT, rhs = PT[:, i, s_, :], Vaug[:, st + s_, hl, :]
                        else:
                            lhsT, rhs = ptc[:, s_ - 5, i * 128:(i + 1) * 128], Vc[:, s_ - 5, hl, :]
                        ins = e.matmul(pO[:, 0:65], lhsT=lhsT, rhs=rhs, start=(s_ == 0), stop=(s_ == 6))
                    return ins
                S.op("pe", f, reads=[f"PT{i}", f"{ptcn}.{i // 4}", "Vc", "Vones"] +
                     [f"Vaug{j}" for j in range(st, st + 5)], writes=[pOn])
                S.op("dve", lambda e: e.reciprocal(out=rc[:, :], in_=pO[:, 64:65]), reads=[pOn],
                     writes=[rcn])
                S.op("dve", lambda e: e.tensor_scalar(
                    out=ytok[:, i, hl * 64:(hl + 1) * 64], in0=pO[:, 0:64], scalar1=rc[:, 0:1],
                    scalar2=None, op0=ALU.mult), reads=[pOn, rcn], writes=[f"ytok{i}"])

            def tr_y(i):
                def f(e):
                    ins = None
                    for c_ in range(2):
                        ins = e.transpose(out=psT[:, c_ * 128:(c_ + 1) * 128],
                                          in_=ytok[:, i, c_ * 128:(c_ + 1) * 128], identity=identb[:, :])
                    return ins
                S.op("pe", f, reads=[f"ytok{i}", "identb"], writes=["psT"])
                copy_op("dve", PT[:, i, 0:2, :], psT[:, 0:256].rearrange("p (c t) -> p c t", c=2),
                        ["psT"], [f"PT{i}"])

            pend, ydone = [], []
            mm_s(0)
            for j in range(16):
                if j + 1 < 16:
                    mm_s(j + 1)
                exp_s(j)
                if hl == 3:
                    for i in ydone:
                        tr_y(i)
                    ydone = []
                for i in pend:
                    pv(i)
                    ydone.append(i)
                pend = done_at[j]
                for i in pend:
                    bias_mul(i)
                if 1 <= j:
                    next(gnext, None)
            for i in pend:
                pv(i)
                ydone.append(i)
            for _ in gnext:
                pass
            if hl == 3:
                for i in ydone:
                    tr_y(i)
        def yna_rhs(kc, b):
            a_ = PT[:, 4 * b, kc, :]
            return bass.AP(tensor=a_.tensor, offset=a_.offset, ap=[[a_.ap[0][0], 128], [640, 4], [1, 128]])
        wout_partial(l, None, None, 512 + hh * 256, 2, 16, wo=wo_pre, rhs_fn=yna_rhs,
                     ynames=lambda b: [f"PT{i}" for i in range(4 * b, 4 * b + 4)])
        S.barrier()

    def hyena_phase(l):
        A.reset()
        lo = l * PL
        x1T = A.s("x1T", [128, 2, NT], BF16)
        x2T = A.s("x2T", [128, 2, NT], BF16)
        vT = A.s("vT", [128, 2, NT], BF16)
        wo_pre = wout_load(l, 256, 2)
        mark = A.mark()
        whyp = WN
        ust_l = [A.s(f"ust{i}", [128, NT], F32) for i in range(2)]
        acc_l = [A.s(f"acc{i}", [128, NT], F32) for i in range(2)]
        dsts = [x1T, x1T, x2T, x2T, vT, vT]
        HYW = [["hy0"], ["hy1"], ["hy2e", "hy2o"]]
        for fc in range(6):
            ust, acc = ust_l[fc % 2], acc_l[fc % 2]
            Nu, Na = f"ust{fc % 2}", f"acc{fc % 2}"
            for b in range(4):
                bank, bname = BANKS[b % 2]

                def mm(e, fc=fc, b=b, bank=bank):
                    ins = None
                    for k in range(8):
                        ins = e.matmul(bank, lhsT=whyp[:, k, fc * 128:(fc + 1) * 128],
                                       rhs=hT[:, k, b * 512:(b + 1) * 512], start=(k == 0), stop=(k == 7))
                    return ins
                S.op("pe", mm, reads=["wnext"] + HTN(b), writes=[bname])
                copy_op("act", ust[:, b * 512:(b + 1) * 512], bank, [bname], [Nu])
                S.op("act", lambda e, fc=fc, b=b, bank=bank, acc=acc: e.activation(
                    out=acc[:, b * 512:(b + 1) * 512], in_=bank, func=AF.Identity,
                    scale=sm(lo + C_CW + 6 + fc), bias=sm(lo + C_CB + fc)),
                    reads=[bname, "small"], writes=[Na])
            cw = lambda tap, fc=fc: sm(lo + C_CW + tap * 6 + fc)
            dst = dsts[fc]
            S.op("dve", lambda e, fc=fc, ust=ust, acc=acc: e.scalar_tensor_tensor(
                out=acc[:, 1:NT], in0=ust[:, 0:NT - 1], scalar=cw(0), in1=acc[:, 1:NT],
                op0=ALU.mult, op1=ALU.add), reads=[Nu, "small", Na], writes=[Na])
            a_hi = acc[:, 256:NT].rearrange("p (s t) -> p s t", t=256)[:, :, 0:1]
            u_lo = ust[:, 0:NT - 256].rearrange("p (s t) -> p s t", t=256)[:, :, 255:256]
            S.op("dve", lambda e, fc=fc, ust=ust, acc=acc: e.scalar_tensor_tensor(
                out=a_hi, in0=u_lo, scalar=nw[:, l, fc:fc + 1], in1=a_hi, op0=ALU.mult, op1=ALU.add),
                reads=[Nu, "nw", Na], writes=[Na])
            a_lo = acc[:, 0:NT - 256].rearrange("p (s t) -> p s t", t=256)[:, :, 255:256]
            u_hi = ust[:, 256:NT].rearrange("p (s t) -> p s t", t=256)[:, :, 0:1]
            S.op("dve", lambda e, fc=fc, ust=ust, acc=acc: e.scalar_tensor_tensor(
                out=a_lo, in0=u_hi, scalar=nw[:, l, 6 + fc:7 + fc], in1=a_lo, op0=ALU.mult,
                op1=ALU.add), reads=[Nu, "nw", Na], writes=[Na])
            S.op("dve", lambda e, fc=fc, ust=ust, acc=acc, dst=dst: e.scalar_tensor_tensor(
                out=dst[:, fc % 2, 0:NT - 1], in0=ust[:, 1:NT], scalar=cw(2), in1=acc[:, 0:NT - 1],
                op0=ALU.mult, op1=ALU.add), reads=[Nu, "small", Na], writes=HYW[fc // 2])
            S.op("dve", lambda e, fc=fc, acc=acc, dst=dst: e.tensor_copy(
                out=dst[:, fc % 2, NT - 1:NT], in_=acc[:, NT - 1:NT]), reads=[Na],
                writes=HYW[fc // 2])
        S.barrier()
        A.reset(mark)
        prefetch("mlp", l)
        zt = A.s("zt", [128, 16, 256], BF16)
        Ypm = A.s("Ypm", [128, 2, 16, 256], BF16)
        UV = [A.s(f"uv{i}", [128, 256], F32) for i in range(4)]
        NF, NKB, NI, PF, PI = 6, 4, 8, 2, 7
        fring = [A.at(f"fwr{i}", [128, 16, 128], BF16, HT_OFF + i * 4096) for i in range(NF)]
        ztp = A.at("ztp", [128, 8, 256], BF16, HT_OFF + NF * 4096)
        iring = [A.s(f"ivr{i}", [128, 1024], BF16) for i in range(NI - 2)] + \
                [A.at(f"ivr{NI - 2 + i}", [128, 1024], BF16, HT_OFF + NF * 4096 + 4096 + i * 2048)
                 for i in range(2)]
        kb = [A.s(f"kb{i}", [128, 2, 256], F32) for i in range(NKB)]
        tt = [A.s(f"tt{i}", [128, 256], F32) for i in range(8)]
        gst = [A.s(f"gst{i}", [128, 512], F32) for i in range(2)]
        yhT = A.s("yhT", [128, 2, NT], BF16)
        ninv = [0]

        def load_f(j, o):
            for ri in range(2):
                q_ = 2 * j + ri
                S.dma("sp", fring[q_ % NF][:, :, :],
                      fwd_d[j + 8 * ri].rearrange("p (s f) -> p s f", s=16), writes=[f"fwr{q_ % NF}"])

        def load_k(q, o):
            j, m = q // 2, q % 2
            for ri in range(2):
                S.dma("sp", kb[q % NKB][:, ri, :],
                      khat[l, ri * 16 + j + 8 * m, :, o * 256:(o + 1) * 256], writes=[f"kb{q % NKB}"])

        def load_i(half, jj):
            n_ = ninv[0]
            ninv[0] += 1
            S.dma("sp", iring[n_ % NI][:, :], inv_d[half, jj], writes=[f"ivr{n_ % NI}"])
            return n_ % NI

        for o in range(2):
            src = vT
            gate = x1T if o == 0 else x2T
            gname = "hy0" if o == 0 else "hy1"
            for j in range(PF):
                load_f(j, o)
            for q in range(2):
                load_k(q, o)
            for half in range(2):
                for g4 in range(4):

                    def trz(e, half=half, g4=g4):
                        ins = None
                        for tl in range(2):
                            t = half * 8 + g4 * 2 + tl
                            for c in range(2):
                                a_ = src[:, c, 256 * (t % 8) + t // 8:256 * (t % 8) + t // 8 + 1]
                                sel_ = bass.AP(tensor=a_.tensor, offset=a_.offset,
                                               ap=[[a_.ap[0][0], 128], [2, 128]])
                                ins = e.transpose(
                                    out=psT[:, (tl * 2 + c) * 128:(tl * 2 + c + 1) * 128],
                                    in_=sel_, identity=identb[:, :])
                        return ins
                    S.op("pe", trz, reads=["hy2e" if half == 0 else "hy2o", "identb"], writes=["psT"])
                    t0 = half * 8 + g4 * 2
                    copy_op("dve" if g4 % 2 == 0 else "act", zt[:, t0:t0 + 2, :],
                            psT[:, 0:512].rearrange("p (t c) -> p t c", t=2), ["psT"], ["zt"])
            S.op("dve", lambda e: e.tensor_scalar(out=ztp[:, :, :], in0=zt[:, 8:16, :], scalar1=-1.0,
                                                  scalar2=None, op0=ALU.mult), reads=["zt"],
                 writes=["ztp"])
            for q in range(16):
                j, m = q // 2, q % 2
                zname = "zt" if m == 0 else "ztp"
                kbt, kbn = kb[q % NKB], f"kb{q % NKB}"
                banks = (BANKS[(q % 2) * 2], BANKS[(q % 2) * 2 + 1])
                for ri in range(2):
                    q_ = 2 * j + ri
                    fw = fring[q_ % NF]
                    bank, bname = banks[ri]

                    def mm(e, fw=fw, bank=bank, m=m):
                        ins = None
                        for s_ in range(16):
                            rhs = ztp[:, s_ - 8, :] if (m == 1 and s_ >= 8) else zt[:, s_, :]
                            ins = e.matmul(bank[:, 0:256], lhsT=fw[:, s_, :], rhs=rhs,
                                           start=(s_ == 0), stop=(s_ == 15))
                        return ins
                    S.op("pe", mm, reads=[f"fwr{q_ % NF}", "zt", zname], writes=[bname])
                if m == 1 and j + PF < 8:
                    load_f(j + PF, o)
                if q + 2 < 16:
                    load_k(q + 2, o)
                (bR, nR), (bI, nI) = banks
                sR, sI = j + 8 * m, 16 + j + 8 * m
                tb_ = (q % 2) * 4
                T0, T1, T2, T3 = tt[tb_], tt[tb_ + 1], tt[tb_ + 2], tt[tb_ + 3]
                N0, N1, N2, N3 = f"tt{tb_}", f"tt{tb_ + 1}", f"tt{tb_ + 2}", f"tt{tb_ + 3}"
                S.op("dve", lambda e, bR=bR, kbt=kbt, T0=T0: e.tensor_tensor(
                    out=T0[:, :], in0=bR[:, 0:256], in1=kbt[:, 0, :], op=ALU.mult),
                    reads=[nR, kbn], writes=[N0])
                S.op("dve", lambda e, bR=bR, kbt=kbt, T2=T2: e.tensor_tensor(
                    out=T2[:, :], in0=bR[:, 0:256], in1=kbt[:, 1, :], op=ALU.mult),
                    reads=[nR, kbn], writes=[N2])
                S.op("dve", lambda e, bI=bI, kbt=kbt, T1=T1: e.tensor_tensor(
                    out=T1[:, :], in0=bI[:, 0:256], in1=kbt[:, 1, :], op=ALU.mult),
                    reads=[nI, kbn], writes=[N1])
                S.op("dve", lambda e, bI=bI, kbt=kbt, T3=T3: e.tensor_tensor(
                    out=T3[:, :], in0=bI[:, 0:256], in1=kbt[:, 0, :], op=ALU.mult),
                    reads=[nI, kbn], writes=[N3])
                U0, U1 = (UV[0], UV[1]) if m == 0 else (UV[2], UV[3])
                n0, n1 = ("uv0", "uv1") if m == 0 else ("uv2", "uv3")
                S.op("pool", lambda e, T0=T0, T1=T1, U0=U0: e.tensor_tensor(
                    out=U0[:, :], in0=T0[:, :], in1=T1[:, :], op=ALU.subtract),
                    reads=[N0, N1], writes=[n0])
                S.op("pool", lambda e, T2=T2, T3=T3, U1=U1: e.tensor_tensor(
                    out=U1[:, :], in0=T2[:, :], in1=T3[:, :], op=ALU.add),
                    reads=[N2, N3], writes=[n1])
                if m == 1:
                    for ri in range(2):
                        S.op("dve", lambda e, ri=ri: e.tensor_tensor(
                            out=Ypm[:, 0, ri * 8 + j, :], in0=UV[ri][:, :], in1=UV[2 + ri][:, :],
                            op=ALU.add), reads=[f"uv{ri}", f"uv{2 + ri}"], writes=["Ypm"])
                        S.op("dve", lambda e, ri=ri: e.tensor_tensor(
                            out=Ypm[:, 1, ri * 8 + j, :], in0=UV[ri][:, :], in1=UV[2 + ri][:, :],
                            op=ALU.subtract), reads=[f"uv{ri}", f"uv{2 + ri}"], writes=["Ypm"])
                if q == 8:
                    pre_slots = [load_i(0, jj) for jj in range(PI)]
            slots = list(pre_slots)
            for par in range(2):
                acc_b = [BANKS[3], BANKS[4], BANKS[5], BANKS[6]] if par == 0 else \
                        [BANKS[0], BANKS[1], BANKS[2], BANKS[3]]
                for jj in range(16):
                    sl_ = slots.pop(0)
                    iv = iring[sl_]
                    ivn = f"ivr{sl_}"

                    def mm(e, jj=jj, iv=iv, acc_b=acc_b, par=par):
                        ins = None
                        for cc in range(2):
                            for tb in range(2):
                                ins = e.matmul(acc_b[cc * 2 + tb][0],
                                               lhsT=Ypm[:, par, jj, cc * 128:(cc + 1) * 128],
                                               rhs=iv[:, tb * 512:(tb + 1) * 512], start=(jj == 0),
                                               stop=(jj == 15))
                        return ins
                    S.op("pe", mm, reads=[ivn, "Ypm"], writes=[b_[1] for b_ in acc_b])
                    nxt = jj + PI
                    if nxt < 16:
                        slots.append(load_i(par, nxt))
                    elif par == 0:
                        slots.append(load_i(1, nxt - 16))
                for cc in range(2):
                    for tb in range(2):
                        bank, bname = acc_b[cc * 2 + tb]
                        t0_ = tb * 1024 + par
                        g_ = gst[(cc * 2 + tb) % 2]
                        gn_ = f"gst{(cc * 2 + tb) % 2}"
                        dst = vT if o == 0 else yhT
                        hyp = "hy2e" if par == 0 else "hy2o"
                        dn = hyp if o == 0 else "yhT"

                        def sel(tn, cc=cc, t0_=t0_):
                            a_ = tn[:, cc, t0_:t0_ + 1]
                            return bass.AP(tensor=a_.tensor, offset=a_.offset,
                                           ap=[[a_.ap[0][0], 128], [2, 512]])
                        S.op("dve", lambda e, cc=cc, bank=bank, g_=g_: e.scalar_tensor_tensor(
                            out=g_[:, :], in0=sel(src), scalar=sm(lo + C_HB + o * 2 + cc), in1=bank,
                            op0=ALU.mult, op1=ALU.add), reads=[hyp, "small", bname], writes=[gn_])
                        S.op("pool", lambda e, g_=g_, dst=dst: e.tensor_tensor(
                            out=sel(dst), in0=g_[:, :], in1=sel(gate), op=ALU.mult),
                            reads=[gn_, gname], writes=[dn])
        wout_partial(l, yhT, "yhT", 256, 2, 16, wo=wo_pre)
        S.barrier()

    def mlp_phase(l):
        A.reset()
        act = A.s("act", [128, 8, NT], BF16)
        ring = [A.s(f"wr{i}", [128, 8, 512], BF16) for i in range(4)]
        rl = [A.s(f"rl{i}", [128, 512], BF16) for i in range(2)]
        nld = 0
        for fg in range(4):
            ups = []
            for hf in range(2):
                if fg == 0 and hf == 0:
                    ups.append((WN, "wnext"))
                    continue
                wt, wn_ = ring[nld % 4], f"wr{nld % 4}"
                nld += 1
                c0 = fg * 1024 + hf * 512
                load_w(wt[:, :, :], w_up[l], (0, D), (c0, c0 + 512), wn_)
                ups.append((wt, wn_))
            dns = []
            for hf in range(2):
                wt, wn_ = ring[nld % 4], f"wr{nld % 4}"
                nld += 1
                load_w(wt[:, :, :], w_down[l], (fg * 1024, (fg + 1) * 1024), (hf * 512, (hf + 1) * 512),
                       wn_)
                dns.append((wt, wn_))
            n = 0
            for fc in range(8):
                wt, wn_ = ups[fc // 4]
                for b in range(4):
                    bs = slice(b * 512, (b + 1) * 512)
                    bank, bname = BANKS[n % 4]
                    r_, rn_ = rl[n % 2], f"rl{n % 2}"
                    n += 1

                    def mm(e, fc=fc, bs=bs, bank=bank, wt=wt):
                        ins = None
                        for k in range(8):
                            ins = e.matmul(bank, lhsT=wt[:, k, (fc % 4) * 128:(fc % 4 + 1) * 128],
                                           rhs=hT[:, k, bs], start=(k == 0), stop=(k == 7))
                        return ins
                    S.op("pe", mm, reads=[wn_] + HTN(b), writes=[bname])
                    S.op("act", lambda e, bank=bank, r_=r_: e.activation(out=r_[:, :], in_=bank,
                                                                         func=AF.Relu),
                         reads=[bname], writes=[rn_])
                    S.op("pool", lambda e, fc=fc, bs=bs, r_=r_: e.tensor_tensor(
                        out=act[:, fc, bs], in0=r_[:, :], in1=r_[:, :], op=ALU.mult),
                        reads=[rn_], writes=[f"act{b}"])
            if fg == 0 and l + 1 < 2:
                prefetch("pool", l + 1)
            last = (l == 1 and fg == 3)
            order = [(dc, b) for b in range(4) for dc in range(8)] if last else \
                    [(dc, b) for dc in range(8) for b in range(4)]
            for (dc, b) in order:
                wt, wn_ = dns[dc // 4]
                bs = slice(b * 512, (b + 1) * 512)
                bank, bname = BANKS[n % 4]
                n += 1

                def mm(e, dc=dc, bs=bs, bank=bank, wt=wt):
                    ins = None
                    for fc in range(8):
                        ins = e.matmul(bank, lhsT=wt[:, fc, (dc % 4) * 128:(dc % 4 + 1) * 128],
                                       rhs=act[:, fc, bs], start=(fc == 0), stop=(fc == 7))
                    return ins
                S.op("pe", mm, reads=[wn_, f"act{b}"], writes=[bname])
                S.op("dve", lambda e, dc=dc, bs=bs, bank=bank: e.scalar_tensor_tensor(
                    out=xT[:, dc, bs], in0=bank, scalar=modv[:, l, 40 + dc:41 + dc], in1=xT[:, dc, bs],
                    op0=ALU.mult, op1=ALU.add), reads=[bname, "modv", f"xT{b}"],
                    writes=[f"xT{b}"])
                if last and dc == 7:
                    if b == 0:
                        yst_l = [A.s(f"yst{i}", [128, D], F32) for i in range(2)]
                        Y_DONE[0] = True
                    for i in range(4 * b, 4 * b + 4):
                        store_y_tile(i, yst_l[i % 2], f"yst{i % 2}", (BANKS[5], BANKS[6]))
        S.barrier()

    phases = []
    for l in range(2):
        phases += [("norm1", lambda l=l: rmsnorm(l, 0)), ("pool", lambda l=l: pool_phase(l)),
                   ("attn0", lambda l=l: attn_phase(l, 0)), ("attn1", lambda l=l: attn_phase(l, 1)),
                   ("hyena", lambda l=l: hyena_phase(l)), ("norm2", lambda l=l: rmsnorm(l, 1)),
                   ("mlp", lambda l=l: mlp_phase(l))]
    for idx, (pname, fn) in enumerate(phases):
        if stop is not None and idx >= stop:
            break
        fn()

    if not Y_DONE[0]:
        A.reset()
        yst = [A.s(f"yst{i}", [128, D], F32) for i in range(2)]
        for i in range(NTILE):
            store_y_tile(i, yst[i % 2], f"yst{i % 2}", (BANKS[0], BANKS[1]))
    S.barrier()
    return nc


_PERM = np.concatenate([np.arange(0, NT, 2), np.arange(1, NT, 2)])


def _dft_tables(L, nblk):
    N = 2 * L
    T = L * nblk
    pos = np.arange(L, dtype=np.float64)
    fwd = np.zeros((16, T, 128), np.float32)
    inv = np.zeros((32 * 128, T), np.float32)
    for j in range(8):
        for p in range(128):
            bq, f = (0, j * 128 + p) if nblk == 1 else (j, p)
            sl = slice(bq * L, (bq + 1) * L)
            th = 2.0 * np.pi * (f + 0.5) * pos / N
            thm = 2.0 * np.pi * (L - 1 - f + 0.5) * pos / N
            fwd[j, sl, p] = np.cos(th)
            fwd[8 + j, sl, p] = -np.sin(th)
            inv[j * 128 + p, sl] = (2.0 / N) * np.cos(th)
            inv[(8 + j) * 128 + p, sl] = (2.0 / N) * np.cos(thm)
            inv[(16 + j) * 128 + p, sl] = -(2.0 / N) * np.sin(th)
            inv[(24 + j) * 128 + p, sl] = (2.0 / N) * np.sin(thm)
    fwd = fwd[:, _PERM, :]
    fwd_b = fwd.reshape(16, 16, 128, 128).transpose(0, 2, 1, 3).reshape(16, 128, 16 * 128)
    invp = np.zeros((2, 16 * 128, T // 2), np.float32)
    for j in range(8):
        for p in range(128):
            bq, f = (0, j * 128 + p) if nblk == 1 else (j, p)
            for par in range(2):
                tl = np.arange(par, L, 2, dtype=np.float64)
                cols = ((bq * L + tl - par) // 2).astype(np.int64)
                th = 2.0 * np.pi * (f + 0.5) * tl / N
                invp[par, j * 128 + p, cols] = (2.0 / N) * np.cos(th)
                invp[par, (8 + j) * 128 + p, cols] = -(2.0 / N) * np.sin(th)
    inv_b = invp.reshape(2, 16, 128, 1024)
    return (np.ascontiguousarray(fwd_b).astype(NPBF), np.ascontiguousarray(inv_b).astype(NPBF))


def _hyena_consts(L, nblk):
    t = np.linspace(0.0, 1.0, L, dtype=np.float32)[:, None]
    w = (2.0 * np.pi * np.arange(L, dtype=np.float32)[:, None] / L).astype(np.float32)
    f = np.linspace(1e-4, 15, 16, dtype=np.float32)[None, :]
    z = np.concatenate([t, np.cos(f * w), -np.sin(f * w)], axis=-1).astype(np.float32)
    deltas = np.abs(np.linspace(np.log(1e-2) / 1.5, np.log(1e-2) / 0.3, 256, dtype=np.float32))
    decay = np.exp(-t * deltas[None, :]).astype(np.float32)
    zT = np.ascontiguousarray(np.tile(z, (nblk, 1))[_PERM].T)
    dec = np.tile(decay, (nblk, 1))[_PERM].reshape(16, 128, 256).transpose(1, 0, 2)
    pos = np.arange(NT)[_PERM]
    wn = (pos < L).astype(np.float32).reshape(16, 128).T
    wn = np.repeat(wn[:, :, None], 128, axis=2)
    tau0 = (pos % L != 0).astype(np.float32).reshape(16, 128).T
    return zT, np.ascontiguousarray(dec), np.ascontiguousarray(wn), np.ascontiguousarray(tau0)


def _pool_consts(L, nblk):
    T = L * nblk
    out = np.zeros((128, 4, 3, 4, 128), np.float32)
    tl = np.arange(L)
    for g, wd in enumerate((2, 4, 8, 16)):
        lo = np.clip(tl - wd // 2, 0, L - 1)
        hi = np.clip(tl + (wd - 1 - wd // 2), 0, L - 1)
        M1 = np.zeros((L, L), np.float64)
        for t_ in range(L):
            M1[t_, lo[t_]:hi[t_] + 1] = 1.0 / (hi[t_] - lo[t_] + 1)
        M1 -= np.eye(L)
        M = np.zeros((T, T), np.float64)
        for b in range(nblk):
            M[b * L:(b + 1) * L, b * L:(b + 1) * L] = M1
        for pat, j in enumerate((0, 2, 3, 15)):
            for ri, r in enumerate((-1, 0, 1)):
                if 0 <= j + r < 16:
                    blk = M[j * 128:(j + 1) * 128, (j + r) * 128:(j + r + 1) * 128]
                    out[:, pat, ri, g, :] = blk.T
    return out.reshape(128, -1).astype(NPBF)


def _bias_bank(rel_bias, sample):
    out = np.full((2, 8, 128, 30, 128), NEG, np.float32)
    reps = (0, 1, 2, 3, 14, 15)
    if sample:
        r_all = np.arange(32)
        c_all = np.arange(64)
        row_start = np.clip(r_all - 4, 0, 24)
        col_start = np.clip(c_all - 8, 0, 48)
        for pat, i in enumerate(reps):
            st = start_attn(i)
            q = i * 128 + np.arange(128)
            qr, qc = q // 64, q % 64
            for s_ in range(5):
                k = (st + s_) * 128 + np.arange(128)
                kr, kc = k // 64, k % 64
                vr = (kr[None, :] >= row_start[qr][:, None]) & (kr[None, :] < row_start[qr][:, None] + 8)
                vc = (kc[None, :] >= col_start[qc][:, None]) & (kc[None, :] < col_start[qc][:, None] + 16)
                valid = vr & vc
                dr = np.clip(kr[None, :] - qr[:, None] + 7, 0, 14)
                dc = np.clip(kc[None, :] - qc[:, None], -15, 15) + 15
                vals = rel_bias[:, :, dr, dc]
                out[:, :, :, pat * 5 + s_, :] = np.where(valid[None, None], vals, NEG)
    else:
        for pat, i in enumerate(reps):
            st = start_attn(i)
            for s_ in range(5):
                j = st + s_
                if j // 2 == i // 2:
                    out[:, :, :, pat * 5 + s_, :] = 0.0
    out = np.ascontiguousarray(out.transpose(0, 1, 4, 3, 2))
    return out.reshape(2, 8, 128, 30 * 128).astype(NPBF)


def _colvec(v):
    return np.ascontiguousarray(np.asarray(v, np.float32).reshape(-1, 128).T)


_CACHE = {}


def _consts(sample):
    key = ("c", sample)
    if key not in _CACHE:
        L, nblk = (2048, 1) if sample else (256, 8)
        fwd_b, inv_b = _dft_tables(L, nblk)
        zT, dec, wn, tau0 = _hyena_consts(L, nblk)
        _CACHE[key] = dict(fwd=fwd_b, inv=inv_b, zT=zT, decay=dec, wn=wn, tau0=tau0,
                           apool=_pool_consts(L, nblk))
    return _CACHE[key]


def _get_nc(stop=None):
    key = ("nc", stop)
    if key not in _CACHE:
        _CACHE[key] = build_nc(stop=stop)
    return _CACHE[key]


def make_in_maps(inp):
    g = {k: np.asarray(v) for k, v in inp.items()}
    shared = dict(
        w_mod=g["w_mod"], w_in=g["w_in"], w_out=g["w_out"], w_up=g["w_up"], w_down=g["w_down"],
        pool_w=g["pool_w"], f1_w=g["hy_f1_w"], f2_w=g["hy_f2_w"], f3_w=g["hy_f3_w"],
        identb=np.eye(128, dtype=np.float32).astype(NPBF), identf=np.eye(128, dtype=np.float32),
        onesm=np.full((128, 128), 1.0 / 1024.0, np.float32).astype(NPBF))
    small64 = np.zeros((64, NV64), np.float32)
    for l in range(2):
        small64[:, l * 4 + 0] = g["hy_f1_b"][l]
        small64[:, l * 4 + 1] = g["hy_f1_freq"][l]
        small64[:, l * 4 + 2] = g["hy_f2_b"][l]
        small64[:, l * 4 + 3] = g["hy_f2_freq"][l]
    shared["small64"] = small64
    gqk = np.zeros((2, 128, 1024), np.float32)
    for l in range(2):
        gqk[l, :, 0:512] = np.tile(g["q_norm_g"][l], 8)[None, :]
        gqk[l, :, 512:1024] = np.tile(g["k_norm_g"][l], 8)[None, :]
    shared["gqk"] = gqk
    gcol = np.zeros((2, 128, 2), np.float32)
    for l in range(2):
        gcol[l, :, 0] = np.tile(g["q_norm_g"][l], 2)
        gcol[l, :, 1] = np.tile(g["k_norm_g"][l], 2)
    shared["gcol"] = gcol
    bias_s = _bias_bank(g["rel_bias"], True)
    bias_p = _bias_bank(g["rel_bias"], False)
    maps = []
    for core in range(8):
        sample = core >= 4
        cst = _consts(sample)
        m = dict(shared)
        small = np.zeros((128, NV), np.float32)
        for l in range(2):
            lo = l * PL
            small[:, lo + C_N1G:lo + C_N1G + 8] = _colvec(g["norm1_g"][l])
            small[:, lo + C_N2G:lo + C_N2G + 8] = _colvec(g["norm2_g"][l])
            small[:, lo + C_BMOD:lo + C_BMOD + 48] = _colvec(g["b_mod"][l])
            small[:, lo + C_PSC:lo + C_PSC + 2] = _colvec(g["pool_scale"][l])
            for tap in range(3):
                small[:, lo + C_CW + tap * 6:lo + C_CW + tap * 6 + 6] = _colvec(g["hy_conv_w"][l, tap])
            small[:, lo + C_CB:lo + C_CB + 6] = _colvec(g["hy_conv_b"][l])
            for o in range(2):
                small[:, lo + C_HB + o * 2:lo + C_HB + o * 2 + 2] = _colvec(g["hy_bias"][l, o])
        small[:, C_EPS] = 1e-6
        small[:, C_SGN] = np.where(np.arange(128) % 2 == 0, 1.0, -1.0)
        small[:, C_TAU0:C_TAU0 + 16] = cst["tau0"]
        if sample:
            b = core - 4
            m["x_in"] = np.ascontiguousarray(g["x_sample"][b])
            small[:, C_COND:C_COND + 8] = _colvec(g["c"][b])
            small[:, C_FLAG] = 0.0
            m["cachek"] = np.ascontiguousarray(g["cache_k"][b].reshape(2, 256, 512))
            cv = np.ones((2, 256, 8, 65), np.float32)
            cv[..., :64] = g["cache_v"][b]
            m["cachev"] = cv.reshape(2, 256, 8 * 65)
            m["biasbank"] = bias_s
        else:
            m["x_in"] = np.ascontiguousarray(g["x_prompt"][core * 8:(core + 1) * 8].reshape(NT, D))
            small[:, C_COND:C_COND + 8] = _colvec(g["c_ctx"])
            small[:, C_FLAG] = 1.0
            m["cachek"] = np.zeros((2, 256, 512), np.float32)
            m["cachev"] = np.zeros((2, 256, 8 * 65), np.float32)
            m["biasbank"] = bias_p
        m["small"] = small
        m["zT"], m["decay"], m["wn"], m["apool"] = cst["zT"], cst["decay"], cst["wn"], cst["apool"]
        m["fwd"], m["inv"] = cst["fwd"], cst["inv"]
        maps.append(m)
    return maps


def assemble(results):
    y_prompt = np.stack([results[c]["y"] for c in range(4)]).reshape(32, 256, D)
    y_sample = np.stack([results[c]["y"] for c in range(4, 8)])
    nk = np.zeros((32, 2, 256, 8, 64), np.float32)
    nv = np.zeros((32, 2, 256, 8, 64), np.float32)
    for c in range(4):
        ko = results[c]["kout"].reshape(2, 8, 256, 8, 64)
        vo = results[c]["vout"].reshape(2, 8, 256, 8, 64)
        nk[c * 8:(c + 1) * 8] = ko.transpose(1, 0, 2, 3, 4)
        nv[c * 8:(c + 1) * 8] = vo.transpose(1, 0, 2, 3, 4)
    return (y_prompt.astype(np.float32), y_sample.astype(np.float32), nk, nv)


def kernel(**inputs):
    nc = _get_nc()
    maps = make_in_maps(inputs)
    res = run_bass_kernel_spmd(nc, maps, core_ids=list(range(8)))
    return assemble(res.results)
```

```python
import numpy as np
import ml_dtypes
import concourse.bass as bass
import concourse.mybir as mybir
from concourse.bass_utils import run_bass_kernel_spmd

F32 = mybir.dt.float32
BF16 = mybir.dt.bfloat16
ALU = mybir.AluOpType
AF = mybir.ActivationFunctionType
AX = mybir.AxisListType
NPBF = ml_dtypes.bfloat16

D = 1024
NT = 2048
NTILE = 16
NCH = 8
L_DEPTH = 2
IN_W = 2560
DFF = 4096
NEG = -30000.0
PI_C = 3.14159
MAGIC = 12582912.0
TWO_PI = 6.283185307179586

C_N1G, C_N2G, C_BMOD, C_PSC, C_CW, C_CB, C_HB = 0, 8, 16, 64, 66, 84, 90
PL = 94
C_COND = 2 * PL
C_FLAG = C_COND + 8
C_EPS = C_FLAG + 1
C_TAU0 = C_EPS + 1
C_SGN = C_TAU0 + 16
NV = C_SGN + 1
NV64 = 8


class Sched:
    NDMA_SEM = 8

    def __init__(self, nc):
        self.nc = nc
        self.eng = {"pe": nc.tensor, "act": nc.scalar, "dve": nc.vector, "pool": nc.gpsimd,
                    "sp": nc.sync}
        self.sem, self.cnt = {}, {}
        for e in ("pe", "act", "dve", "pool"):
            self.sem[e] = nc.alloc_semaphore(name=f"s_{e}")
            self.cnt[e] = 0
        self.dsem, self.dcnt = {}, {}
        for q in ("sp", "pool"):
            self.dsem[q] = [nc.alloc_semaphore(name=f"d_{q}{i}") for i in range(self.NDMA_SEM)]
            self.dcnt[q] = 0
        self.known = {e: {} for e in ("pe", "act", "dve", "pool", "sp")}
        self.last_w, self.readers = {}, {}

    def _tok_wait(self, tok):
        if tok[0] == "c":
            return ("c", tok[1]), self.sem[tok[1]], tok[2]
        q, m = tok[1], tok[2]
        r, j = m % self.NDMA_SEM, m // self.NDMA_SEM
        return ("d", q, r), self.dsem[q][r], 16 * (j + 1)

    @staticmethod
    def _excl(reads, writes):
        return list(writes) + [r for r in reads if r.startswith("ps")]

    def _deps(self, reads, writes):
        writes = self._excl(reads, writes)
        deps = []
        for r in reads:
            t = self.last_w.get(r)
            if t is not None:
                deps.append(t)
        for r in writes:
            t = self.last_w.get(r)
            if t is not None:
                deps.append(t)
            deps.extend(self.readers.get(r, ()))
        return deps

    def _emit_waits(self, waiter, deps, self_n=None):
        h = self.eng[waiter]
        best = {}
        for tok in deps:
            if tok[0] == "c" and tok[1] == waiter:
                if waiter == "pe":
                    continue
                if self_n is not None and tok[2] < self_n - 2:
                    continue
            key, sem, val = self._tok_wait(tok)
            if self.known[waiter].get(key, 0) >= val:
                continue
            if key not in best or best[key][1] < val:
                best[key] = (sem, val)
        for key, (sem, val) in best.items():
            h.wait_ge(sem, val)
            self.known[waiter][key] = val

    def _record(self, tok, reads, writes):
        writes = self._excl(reads, writes)
        for r in reads:
            self.readers.setdefault(r, []).append(tok)
        for r in writes:
            self.last_w[r] = tok
            self.readers[r] = []

    def op(self, eng, fn, reads=(), writes=()):
        reads, writes = list(reads), list(writes)
        deps = self._deps(reads, writes)
        n = self.cnt[eng] + 1
        self._emit_waits(eng, deps, self_n=n)
        ins = fn(self.eng[eng])
        ins.then_inc(self.sem[eng], 1)
        self.cnt[eng] = n
        self._record(("c", eng, n), reads, writes)

    def dma(self, q, out, in_, reads=(), writes=()):
        reads, writes = list(reads), list(writes)
        deps = self._deps(reads, writes)
        m = self.dcnt[q]
        if m >= self.NDMA_SEM:
            deps.append(("d", q, m - self.NDMA_SEM))
        self._emit_waits(q, deps)
        r = m % self.NDMA_SEM
        self.eng[q].dma_start(out=out, in_=in_).then_inc(self.dsem[q][r], 16)
        self.dcnt[q] = m + 1
        self._record(("d", q, m), reads, writes)

    def barrier(self):
        toks = []
        for e in ("pe", "act", "dve", "pool"):
            if self.cnt[e] > 0:
                toks.append(("c", e, self.cnt[e]))
        for q in ("sp", "pool"):
            for m in range(max(0, self.dcnt[q] - self.NDMA_SEM), self.dcnt[q]):
                toks.append(("d", q, m))
        for w in ("pe", "act", "dve", "pool", "sp"):
            h = self.eng[w]
            for tok in toks:
                if tok[0] == "c" and tok[1] == w:
                    continue
                key, sem, val = self._tok_wait(tok)
                if self.known[w].get(key, 0) >= val:
                    continue
                h.wait_ge(sem, val)
                self.known[w][key] = val
        self.last_w.clear()
        self.readers.clear()


class Arena:
    BASE, TOP = 16512, 229344

    def __init__(self, nc):
        self.nc = nc
        self.persist = self.BASE
        self.cur = self.BASE
        self.n = 0

    def _alloc(self, name, shape, dt, off):
        self.n += 1
        return self.nc.alloc_sbuf_tensor_at(f"{name}_{self.n}", list(shape), dt, offset=off)

    @staticmethod
    def _bytes(shape, dt):
        n = 1
        for s in shape[1:]:
            n *= s
        return (n * (2 if dt == BF16 else 4) + 31) // 32 * 32

    def p(self, name, shape, dt):
        assert self.cur == self.persist, "persistent alloc after scratch"
        t = self._alloc(name, shape, dt, self.persist)
        self.persist += self._bytes(shape, dt)
        self.cur = self.persist
        assert self.persist <= self.TOP
        return t

    def s(self, name, shape, dt):
        t = self._alloc(name, shape, dt, self.cur)
        self.cur += self._bytes(shape, dt)
        assert self.cur <= self.TOP, f"SBUF overflow at {name}: {self.cur}"
        return t

    def reset(self, to=None):
        self.cur = self.persist if to is None else to

    def at(self, name, shape, dt, off):
        return self._alloc(name, shape, dt, off)

    def mark(self):
        return self.cur


def pid_attn(i):
    return {0: 0, 1: 1, 14: 4, 15: 5}.get(i, 2 if i % 2 == 0 else 3)


def pid_pool(j):
    return 0 if j == 0 else (3 if j == 15 else (1 if j % 2 == 0 else 2))


def start_attn(i):
    return min(max(i - 2, 0), 11)


def build_nc(stop=None, debug=False):
    nc = bass.Bass("TRN2", target_bir_lowering=False)

    def din(name, shape, dt=F32):
        return nc.dram_tensor(name, list(shape), dt, kind="ExternalInput").ap()

    x_in = din("x_in", [NT, D])
    w_mod = din("w_mod", [2, D, 6 * D])
    w_in = din("w_in", [2, D, IN_W])
    w_out = din("w_out", [2, D, D])
    w_up = din("w_up", [2, D, DFF])
    w_down = din("w_down", [2, DFF, D])
    pool_w = din("pool_w", [2, 4, 64, 64])
    f1_w = din("f1_w", [2, 33, 64])
    f2_w = din("f2_w", [2, 64, 64])
    f3_w = din("f3_w", [2, 64, 1024])
    small = din("small", [128, NV])
    small64 = din("small64", [64, NV64])
    gqk = din("gqk", [2, 128, 1024])
    gcol = din("gcol", [2, 128, 2])
    cachek = din("cachek", [2, 256, 512])
    cachev = din("cachev", [2, 256, 8 * 65])
    zT_d = din("zT", [33, NT])
    decay_d = din("decay", [128, 16, 256])
    wn_d = din("wn", [128, 16, 128])
    apool_d = din("apool", [128, 4 * 3 * 4 * 128], BF16)
    bias_d = din("biasbank", [2, 8, 128, 30 * 128], BF16)
    fwd_d = din("fwd", [16, 128, 16 * 128], BF16)
    inv_d = din("inv", [2, 16, 128, 1024], BF16)
    identb_d = din("identb", [128, 128], BF16)
    identf_d = din("identf", [128, 128])
    onesm_d = din("onesm", [128, 128], BF16)

    y_out = nc.dram_tensor("y", [NT, D], F32, kind="ExternalOutput").ap()
    k_out = nc.dram_tensor("kout", [2, NT, 512], F32, kind="ExternalOutput").ap()
    v_out = nc.dram_tensor("vout", [2, NT, 512], F32, kind="ExternalOutput").ap()
    khat = nc.dram_tensor("khat", [2, 32, 128, 512], F32, kind="Internal").ap()

    S = Sched(nc)
    A = Arena(nc)

    psS0 = nc.alloc_psum_tensor("psS0", [128, 1024], F32)
    psS1 = nc.alloc_psum_tensor("psS1", [128, 1024], F32)
    psO = nc.alloc_psum_tensor("psO", [128, 512], F32)
    psG0 = nc.alloc_psum_tensor("psG0", [128, 512], F32)
    psG1 = nc.alloc_psum_tensor("psG1", [128, 512], F32)
    psT = nc.alloc_psum_tensor("psT", [128, 1024], BF16)
    psT_f32 = psS1
    BANKS = [(psG0[:, :], "psG0"), (psG1[:, :], "psG1"), (psO[:, :], "psO"),
             (psS0[:, 0:512], "psS0.0"), (psS0[:, 512:1024], "psS0.1"),
             (psS1[:, 0:512], "psS1.0"), (psS1[:, 512:1024], "psS1.1")]

    smallt = A.p("small", [128, NV], F32)
    small64t = A.p("small64", [64, NV64], F32)
    identb = A.p("identb", [128, 128], BF16)
    identf = A.p("identf", [128, 128], F32)
    onesm = A.p("onesm", [128, 128], BF16)
    modv = A.p("modv", [128, 2, 48], F32)
    modA = A.p("modA", [128, 2, 16], F32)
    nw = A.p("nw", [128, 2, 12], F32)
    PRE_X = A.mark()
    xT = A.p("xT", [128, NCH, NT], F32)
    HT_OFF = A.mark()
    hT = A.p("hT", [128, NCH, NT], BF16)
    WN = A.p("wnext", [128, 8, 768], BF16)

    def prefetch(kind, l, hh=0):
        if kind == "pool":
            load_w(WN[:, :, 0:256], w_in[l], (0, D), (0, 256), "wnext")
        elif kind == "attn":
            for part in range(3):
                c0 = 1024 + part * 512 + hh * 256
                load_w(WN[:, :, part * 256:(part + 1) * 256], w_in[l], (0, D), (c0, c0 + 256), "wnext")
        elif kind == "hyena":
            load_w(WN[:, :, :], w_in[l], (0, D), (256, 1024), "wnext")
        elif kind == "mlp":
            load_w(WN[:, :, 0:512], w_up[l], (0, D), (0, 512), "wnext")

    def sm(col, n=1):
        return smallt[:, col:col + n]

    S.dma("sp", smallt[:, :], small, writes=["small"])
    S.dma("sp", small64t[:, :], small64, writes=["small64"])
    S.dma("sp", identb[:, :], identb_d, writes=["identb"])
    S.dma("sp", identf[:, :], identf_d, writes=["identf"])
    S.dma("sp", onesm[:, :], onesm_d, writes=["onesm"])
    CONSTS = ["small", "small64", "identb", "identf", "onesm"]
    eps_ap = sm(C_EPS)

    def copy_op(eng, out, in_, reads, writes):
        if eng == "act":
            S.op("act", lambda e: e.copy(out=out, in_=in_), reads, writes)
        elif eng == "dve":
            S.op("dve", lambda e: e.tensor_copy(out=out, in_=in_), reads, writes)
        else:
            S.op("pool", lambda e: e.tensor_copy(out=out, in_=in_), reads, writes)

    def load_w(dst, src_ap, rows, cols, wname):
        r0, r1 = rows
        c0, c1 = cols
        src = src_ap[r0:r1, c0:c1].rearrange("(k p) c -> p k c", p=128)
        S.dma("pool", dst, src, writes=[wname])

    def prologue(l):
        A.reset(PRE_X)
        lo = l * PL
        silu_c = A.s("silu_c", [128, 8], BF16)
        sig = A.s("sig", [128, 8], F32)
        wslabs = [A.s(f"wm{i}", [128, 8, 512], BF16) for i in range(3)]

        def adaln_steps():
            S.op("act", lambda e: e.activation(out=sig[:, :], in_=sm(C_COND, 8), func=AF.Sigmoid),
                 reads=["small"], writes=["sig"])
            S.op("dve", lambda e: e.tensor_tensor(out=silu_c[:, :], in0=sig[:, :], in1=sm(C_COND, 8),
                                                  op=ALU.mult), reads=["sig", "small"],
                 writes=["silu_c"])
            pm = psT_f32
            for sl in range(3):
                load_w(wslabs[sl][:, :, :], w_mod[l], (0, D), (sl * 512, (sl + 1) * 512), f"wm{sl}")
            yield
            for sl in range(12):
                wt = wslabs[sl % 3]
                wn_ = f"wm{sl % 3}"

                def mm(e, sl=sl, wt=wt):
                    ins = None
                    for jc in range(4):
                        j = sl * 4 + jc
                        for k in range(8):
                            ins = e.matmul(pm[:, j:j + 1], lhsT=wt[:, k, jc * 128:(jc + 1) * 128],
                                           rhs=silu_c[:, k:k + 1], start=(k == 0), stop=(k == 7))
                    return ins
                S.op("pe", mm, reads=[wn_, "silu_c"], writes=["psS1.0"])
                if sl + 3 < 12:
                    load_w(wt[:, :, :], w_mod[l], (0, D), ((sl + 3) * 512, (sl + 4) * 512), wn_)
                yield
            S.op("dve", lambda e: e.tensor_tensor(out=modv[:, l, :], in0=pm[:, 0:48],
                                                  in1=sm(lo + C_BMOD, 48), op=ALU.add),
                 reads=["psS1.0", "small"], writes=["modv"])
            for which, (cg, cs) in enumerate(((C_N1G, 8), (C_N2G, 32))):
                S.op("dve", lambda e, which=which, cg=cg, cs=cs: e.scalar_tensor_tensor(
                    out=modA[:, l, which * 8:(which + 1) * 8], in0=modv[:, l, cs:cs + 8], scalar=1.0,
                    in1=sm(lo + cg, 8), op0=ALU.add, op1=ALU.mult),
                    reads=["modv", "small"], writes=["modA"])
            for which, tap in enumerate((0, 2)):
                S.op("dve", lambda e, which=which, tap=tap: e.tensor_scalar(
                    out=nw[:, l, which * 6:(which + 1) * 6], in0=sm(lo + C_CW + tap * 6, 6),
                    scalar1=sm(C_FLAG), scalar2=-1.0, op0=ALU.mult, op1=ALU.mult),
                    reads=["small"], writes=["nw"])
            yield

        ada = adaln_steps()

        def ada_step():
            try:
                next(ada)
            except StopIteration:
                pass

        ada_step()
        h2 = A.s("h2", [64, NT], BF16)
        w1 = A.s("w1", [33, 64], F32)
        w2 = A.s("w2", [64, 64], F32)
        w3 = A.s("w3", [64, 1024], BF16)
        decay = A.s("decay", [128, 16, 256], F32)
        wn = A.s("wn", [128, 16, 128], BF16)
        fb = A.s("fb", [64, 2], F32)
        ovl = A.mark()
        zT = A.s("zT", [33, NT], F32)
        pre = A.s("pre", [64, NT], F32)
        tmp = A.s("tmp", [64, NT], F32)
        h1 = A.s("h1", [64, NT], F32)
        S.dma("sp", zT[:, :], zT_d, writes=["zT"])
        S.dma("sp", w1[:, :], f1_w[l], writes=["w1"])
        S.dma("sp", w2[:, :], f2_w[l], writes=["w2"])
        S.dma("pool", w3[:, :], f3_w[l], writes=["w3"])
        S.dma("sp", decay[:, :, :], decay_d, writes=["decay"])
        S.dma("pool", wn[:, :, :], wn_d, writes=["wn"])
        for li in range(2):
            S.op("dve", lambda e, li=li: e.tensor_tensor(
                out=fb[:, li:li + 1], in0=small64t[:, l * 4 + 2 * li:l * 4 + 2 * li + 1],
                in1=small64t[:, l * 4 + 2 * li + 1:l * 4 + 2 * li + 2], op=ALU.mult),
                reads=["small64"], writes=["fb"])

        def sine_layer(li, wmat, kdim, src, dst, srcname, dstname):
            for b in range(4):
                bank, bname = BANKS[b % 2]
                S.op("pe", lambda e, b=b, bank=bank: e.matmul(
                    bank[0:64, :], lhsT=wmat[0:kdim, :], rhs=src[0:kdim, b * 512:(b + 1) * 512],
                    start=True, stop=True), reads=[srcname, f"w{li + 1}"], writes=[bname])
                S.op("dve", lambda e, b=b, bank=bank: e.tensor_scalar(
                    out=pre[:, b * 512:(b + 1) * 512], in0=bank[0:64, :],
                    scalar1=small64t[:, l * 4 + 2 * li + 1:l * 4 + 2 * li + 2],
                    scalar2=fb[:, li:li + 1], op0=ALU.mult, op1=ALU.add),
                    reads=[bname, "small64", "fb"], writes=["pre"])
            S.op("dve", lambda e: e.tensor_scalar(out=tmp[:, :], in0=pre[:, :], scalar1=1.0 / TWO_PI,
                                                  scalar2=MAGIC, op0=ALU.mult, op1=ALU.add),
                 reads=["pre"], writes=["tmp"])
            S.op("dve", lambda e: e.tensor_scalar(out=tmp[:, :], in0=tmp[:, :], scalar1=MAGIC,
                                                  scalar2=-TWO_PI, op0=ALU.subtract, op1=ALU.mult),
                 reads=["tmp"], writes=["tmp"])
            S.op("dve", lambda e: e.tensor_tensor(out=tmp[:, :], in0=tmp[:, :], in1=pre[:, :],
                                                  op=ALU.add), reads=["tmp", "pre"], writes=["tmp"])
            S.op("dve", lambda e: e.tensor_scalar(out=tmp[:, :], in0=tmp[:, :], scalar1=PI_C,
                                                  scalar2=-PI_C, op0=ALU.min, op1=ALU.max),
                 reads=["tmp"], writes=["tmp"])
            S.op("act", lambda e: e.activation(out=dst[:, :], in_=tmp[:, :], func=AF.Sin),
                 reads=["tmp"], writes=[dstname])

        sine_layer(0, w1, 33, zT, h1, "zT", "h1")
        ada_step()
        sine_layer(1, w2, 64, h1, h2, "h1", "h2")
        ada_step()
        ada_step()

        A.reset(ovl)
        a_t = A.s("a_t", [128, 16, 512], BF16)
        d_t = A.s("d_t", [128, 16, 512], BF16)
        KD_OFF = A.mark()
        kd = A.s("kd", [128, 16, 1024], BF16)
        absk = [A.s(f"absk{i}", [128, 1024], BF16) for i in range(2)]
        def kd_mm(t):
            for cb in range(2):
                bank, bname = BANKS[cb] if t % 2 == 0 else BANKS[4 + 2 * cb]
                S.op("pe", lambda e, cb=cb, bank=bank: e.matmul(
                    bank, lhsT=h2[:, t * 128:(t + 1) * 128], rhs=w3[:, cb * 512:(cb + 1) * 512],
                    start=True, stop=True), reads=["h2", "w3"], writes=[bname])

        def kd_ew(t):
            for cb in range(2):
                bank, bname = BANKS[cb] if t % 2 == 0 else BANKS[4 + 2 * cb]
                dcy = decay[:, t, :]
                dcy_b = bass.AP(tensor=dcy.tensor, offset=dcy.offset,
                                ap=[[dcy.ap[0][0], 128], [0, 2], [1, 256]])
                S.op("dve", lambda e, cb=cb, bank=bank, dcy_b=dcy_b: e.tensor_tensor(
                    out=kd[:, t, cb * 512:(cb + 1) * 512].rearrange("p (h c) -> p h c", h=2),
                    in0=bank.rearrange("p (h c) -> p h c", h=2), in1=dcy_b, op=ALU.mult),
                    reads=[bname, "decay"], writes=[f"kd{t}"])
            ak = absk[t % 2]
            S.op("act", lambda e, ak=ak: e.activation(out=ak[:, :], in_=kd[:, t, :], func=AF.Abs),
                 reads=[f"kd{t}"], writes=[f"absk{t % 2}"])

        def kd_norm(t):
            ak = absk[t % 2]
            for cb in range(2):
                bank, bname = BANKS[2 + cb]
                S.op("pe", lambda e, cb=cb, bank=bank, ak=ak: e.matmul(
                    bank, lhsT=wn[:, t, :], rhs=ak[:, cb * 512:(cb + 1) * 512],
                    start=(t == 0), stop=(t == 15)), reads=[f"absk{t % 2}", "wn"], writes=[bname])

        kd_mm(0)
        for t in range(16):
            if t + 1 < 16:
                kd_mm(t + 1)
            kd_ew(t)
            if t >= 1:
                kd_norm(t - 1)
            if t % 4 == 3:
                ada_step()
        kd_norm(15)
        rnf = A.s("rnf", [128, 1024], F32)
        rn = A.s("rn", [128, 1024], BF16)
        for cb in range(2):
            bank, bname = BANKS[2 + cb]
            S.op("dve", lambda e, cb=cb, bank=bank: e.tensor_scalar(
                out=rnf[:, cb * 512:(cb + 1) * 512], in0=bank, scalar1=1e-6, scalar2=None,
                op0=ALU.add), reads=[bname], writes=["rnf"])
        S.op("dve", lambda e: e.reciprocal(out=rnf[:, :], in_=rnf[:, :]), reads=["rnf"], writes=["rnf"])
        copy_op("act", rn[:, :], rnf[:, :], ["rnf"], ["rn"])
        t1 = [A.s(f"t1{i}", [128, 512], BF16) for i in range(2)]
        t2 = [A.s(f"t2{i}", [128, 512], BF16) for i in range(2)]
        for t in range(16):
            u1, u2 = t1[t % 2], t2[t % 2]
            n1, n2 = f"t1{t % 2}", f"t2{t % 2}"
            S.op("dve", lambda e, t=t, u1=u1: e.tensor_tensor(out=u1[:, :], in0=kd[:, t, 0:512],
                                                             in1=rn[:, 0:512], op=ALU.mult),
                 reads=[f"kd{t}", "rn"], writes=[n1])
            S.op("dve", lambda e, t=t, u2=u2: e.scalar_tensor_tensor(
                out=u2[:, :], in0=kd[:, t, 512:1024], scalar=sm(C_TAU0 + t), in1=rn[:, 512:1024],
                op0=ALU.mult, op1=ALU.mult), reads=[f"kd{t}", "rn", "small"], writes=[n2])
            S.op("dve", lambda e, t=t, u1=u1, u2=u2: e.tensor_tensor(
                out=a_t[:, t, :], in0=u1[:, :], in1=u2[:, :], op=ALU.add), reads=[n1, n2],
                writes=["a_t"])
            S.op("dve", lambda e, t=t, u1=u1, u2=u2: e.tensor_tensor(
                out=d_t[:, t, :], in0=u1[:, :], in1=u2[:, :], op=ALU.subtract), reads=[n1, n2],
                writes=["d_t"])
            if t % 4 == 3:
                ada_step()
        NR = 6
        ring = [A.s(f"fw{i}", [128, 16, 128], BF16) for i in range(NR)]
        kst = [A.s(f"kst{i}", [128, 512], F32) for i in range(4)]
        est = [A.s(f"est{i}", [128, 512], F32) for i in range(2)]

        def load_fw(jt):
            S.dma("sp", ring[jt % NR][:, :, :], fwd_d[jt].rearrange("p (s f) -> p s f", s=16),
                  writes=[f"fw{jt % NR}"])
        for jt in range(NR):
            load_fw(jt)
        for jt in range(16):
            ri, j = jt // 8, jt % 8
            fw = ring[jt % NR]
            fn_ = f"fw{jt % NR}"
            (bE, nE), (bO, nO) = (BANKS[0], BANKS[1]) if jt % 2 == 0 else (BANKS[3], BANKS[4])
            src, sname = (a_t, "a_t") if ri == 0 else (d_t, "d_t")
            for half, (bank, bname) in enumerate(((bE, nE), (bO, nO))):
                def mm(e, fw=fw, bank=bank, src=src, half=half):
                    ins = None
                    for s_ in range(8):
                        ins = e.matmul(bank, lhsT=fw[:, half * 8 + s_, :], rhs=src[:, half * 8 + s_, :],
                                       start=(s_ == 0), stop=(s_ == 7))
                    return ins
                S.op("pe", mm, reads=[fn_, sname], writes=[bname])
            if jt + NR < 16:
                load_fw(jt + NR)
            es, esn = est[jt % 2], f"est{jt % 2}"
            copy_op("act", es[:, :], bE, [nE], [esn])
            kb_, kbn_ = kst[(jt % 2) * 2], f"kst{(jt % 2) * 2}"
            km_, kmn_ = kst[(jt % 2) * 2 + 1], f"kst{(jt % 2) * 2 + 1}"
            S.op("dve", lambda e, bO=bO, es=es, kb_=kb_: e.tensor_tensor(
                out=kb_[:, :], in0=bO, in1=es[:, :], op=ALU.add), reads=[nO, esn], writes=[kbn_])
            S.op("dve", lambda e, bO=bO, es=es, km_=km_: e.scalar_tensor_tensor(
                out=km_[:, :], in0=bO, scalar=-1.0, in1=es[:, :], op0=ALU.mult, op1=ALU.add),
                reads=[nO, esn], writes=[kmn_])
            S.dma("sp", khat[l, ri * 16 + j], kb_[:, :], reads=[kbn_], writes=["khat"])
            S.dma("sp", khat[l, ri * 16 + 8 + j], km_[:, :], reads=[kmn_], writes=["khat"])
            ada_step()
        for _ in range(20):
            ada_step()
        S.barrier()

    for l in range(2):
        prologue(l)
    A.reset()

    prefetch("pool", 0)
    xst = [A.s(f"xst{i}", [128, D], F32) for i in range(2)]
    for i in range(NTILE):
        st = xst[i % 2]
        sn = f"xst{i % 2}"
        S.dma("sp", st[:, :], x_in[i * 128:(i + 1) * 128, :], writes=[sn])
        for hb in range(2):
            bank, bname = BANKS[hb]

            def tr(e, hb=hb, bank=bank, st=st):
                ins = None
                for kk in range(4):
                    k = hb * 4 + kk
                    ins = e.transpose(out=bank[:, kk * 128:(kk + 1) * 128],
                                      in_=st[:, k * 128:(k + 1) * 128], identity=identf[:, :])
                return ins
            S.op("pe", tr, reads=[sn, "identf"], writes=[bname])
            copy_op("act" if hb == 0 else "dve",
                    xT[:, hb * 4:(hb + 1) * 4, i * 128:(i + 1) * 128],
                    bank.rearrange("p (k t) -> p k t", k=4), [bname], [f"xT{i // 4}"])
    S.barrier()

    def HTN(b):
        return [f"hT{b}.{k}" for k in range(8)]

    def rmsnorm(l, which):
        A.reset()
        rstd_l = [A.s(f"rstd{i}", [128, 512], F32) for i in range(2)]
        lnv_l = [A.s(f"lnv{i}", [128, 512], F32) for i in range(2)]
        tmpn = [A.s(f"tmpn{i}", [128, 512], F32) for i in range(4)]
        cB = 0 if which == 0 else 24
        nt_ = [0]

        def stage1(b):
            bs = slice(b * 512, (b + 1) * 512)
            for k in range(8):
                if k % 2 == 0:
                    S.op("act", lambda e, k=k: e.activation(out=hT[:, k, bs], in_=xT[:, k, bs],
                                                            func=AF.Square),
                         reads=[f"xT{b}"], writes=[f"hT{b}.{k}"])
                else:
                    S.op("dve", lambda e, k=k: e.tensor_tensor(out=hT[:, k, bs], in0=xT[:, k, bs],
                                                               in1=xT[:, k, bs], op=ALU.mult),
                         reads=[f"xT{b}"], writes=[f"hT{b}.{k}"])
            bank, bname = BANKS[b % 2]

            def mm(e):
                ins = None
                for k in range(8):
                    ins = e.matmul(bank, lhsT=onesm[:, :], rhs=hT[:, k, bs], start=(k == 0),
                                   stop=(k == 7))
                return ins
            S.op("pe", mm, reads=HTN(b) + ["onesm"], writes=[bname])

        def stage2(b):
            bs = slice(b * 512, (b + 1) * 512)
            rstd, lnv = rstd_l[b % 2], lnv_l[b % 2]
            Nr, Nl = f"rstd{b % 2}", f"lnv{b % 2}"
            bank, bname = BANKS[b % 2]
            S.op("act", lambda e: e.activation(out=lnv[:, :], in_=bank, func=AF.Ln, bias=eps_ap),
                 reads=[bname, "small"], writes=[Nl])
            S.op("act", lambda e: e.activation(out=rstd[:, :], in_=lnv[:, :], func=AF.Exp, scale=-0.5),
                 reads=[Nl], writes=[Nr])
            for k in range(8):
                tn = tmpn[nt_[0] % 4]
                tnn = f"tmpn{nt_[0] % 4}"
                nt_[0] += 1
                S.op("dve", lambda e, k=k, tn=tn: e.scalar_tensor_tensor(
                    out=tn[:, :], in0=xT[:, k, bs], scalar=modA[:, l, which * 8 + k:which * 8 + k + 1],
                    in1=rstd[:, :], op0=ALU.mult, op1=ALU.mult),
                    reads=[f"xT{b}", "modA", Nr], writes=[tnn])
                S.op("act", lambda e, k=k, tn=tn: e.activation(
                    out=hT[:, k, bs], in_=tn[:, :], func=AF.Identity,
                    bias=modv[:, l, cB + k:cB + k + 1]), reads=[tnn, "modv"],
                    writes=[f"hT{b}.{k}"])

        stage1(0)
        for b in range(4):
            if b + 1 < 4:
                stage1(b + 1)
            stage2(b)
        S.barrier()

    def wout_load(l, row0, nk):
        wo = A.s("wo", [128, nk, D], BF16)
        load_w(wo[:, :, :], w_out[l], (row0, row0 + nk * 128), (0, D), "wo")
        return wo

    def wout_partial(l, yT, yname, row0, nk, gate_col, wo=None, rhs_fn=None, ynames=None):
        if wo is None:
            wo = wout_load(l, row0, nk)
        if rhs_fn is None:
            rhs_fn = lambda kc, b: yT[:, kc, b * 512:(b + 1) * 512]
        n = 0
        for dc in range(8):
            for b in range(4):
                bs = slice(b * 512, (b + 1) * 512)
                bank, bname = BANKS[n % 4]
                n += 1

                def mm(e, dc=dc, b=b, bank=bank):
                    ins = None
                    for kc in range(nk):
                        ins = e.matmul(bank, lhsT=wo[:, kc, dc * 128:(dc + 1) * 128], rhs=rhs_fn(kc, b),
                                       start=(kc == 0), stop=(kc == nk - 1))
                    return ins
                S.op("pe", mm, reads=["wo"] + (ynames(b) if ynames else [yname]), writes=[bname])
                S.op("dve", lambda e, dc=dc, bs=bs, bank=bank: e.scalar_tensor_tensor(
                    out=xT[:, dc, bs], in0=bank, scalar=modv[:, l, gate_col + dc:gate_col + dc + 1],
                    in1=xT[:, dc, bs], op0=ALU.mult, op1=ALU.add),
                    reads=[bname, "modv", f"xT{b}"], writes=[f"xT{b}"])

    Y_DONE = [False]

    def store_y_tile(i, st, sn, banks2):
        for hb in range(2):
            bank, bname = banks2[hb]

            def tr(e, hb=hb, bank=bank):
                ins = None
                for kk in range(4):
                    k = hb * 4 + kk
                    ins = e.transpose(out=bank[:, kk * 128:(kk + 1) * 128],
                                      in_=xT[:, k, i * 128:(i + 1) * 128], identity=identf[:, :])
                return ins
            S.op("pe", tr, reads=[f"xT{i // 4}", "identf"], writes=[bname])
            copy_op("act", st[:, hb * 512:(hb + 1) * 512], bank, [bname], [sn])
        S.dma("sp", y_out[i * 128:(i + 1) * 128, :], st[:, :], reads=[sn])

    def pool_phase(l):
        A.reset()
        lo = l * PL
        wp = WN
        wo_pre = wout_load(l, 0, 2)
        apool = A.s("apool", [128, 4, 3, 4, 128], BF16)
        S.dma("sp", apool[:, :, :, :, :],
              apool_d.rearrange("p (a r g t) -> p a r g t", a=4, r=3, g=4), writes=["apool"])
        wblk = A.s("wblk", [128, 2, 128], BF16)
        S.op("dve", lambda e: e.memset(wblk[:, :, :], 0.0), writes=["wblk"])
        for g in range(4):
            gp, gl = g // 2, g % 2
            S.dma("pool", wblk[gl * 64:(gl + 1) * 64, gp, gl * 64:(gl + 1) * 64], pool_w[l, g],
                  writes=["wblk"])
        upool = A.s("upool", [128, 16, 256], BF16)
        for i in range(16):
            bank, bname = BANKS[i % 2]

            def mm(e, i=i, bank=bank):
                ins = None
                for k in range(8):
                    ins = e.matmul(bank[:, 0:256], lhsT=hT[:, k, i * 128:(i + 1) * 128],
                                   rhs=wp[:, k, 0:256], start=(k == 0), stop=(k == 7))
                return ins
            S.op("pe", mm, reads=HTN(i // 4) + ["wnext"], writes=[bname])
            copy_op("act" if i % 2 == 0 else "dve", upool[:, i, :], bank[:, 0:256], [bname],
                    [f"upool{i}"])
        prefetch("attn", l, 0)
        pooledT = A.s("pooledT", [128, 2, NT], BF16)
        n = 0
        for b in range(4):
            for gp in range(2):
                for gl in range(2):
                    g = gp * 2 + gl
                    bank, bname = BANKS[n % 4]
                    n += 1

                    def mm(e, b=b, gp=gp, g=g, bank=bank):
                        ins = None
                        for jl in range(4):
                            j = b * 4 + jl
                            rs = [r for r in (-1, 0, 1) if 0 <= j + r < 16]
                            for r in rs:
                                ins = e.matmul(bank[:, jl * 128:(jl + 1) * 128],
                                               lhsT=upool[:, j + r, gp * 128:(gp + 1) * 128],
                                               rhs=apool[:, pid_pool(j), r + 1, g, :],
                                               start=(r == rs[0]), stop=(r == rs[-1]))
                        return ins
                    rd = [f"upool{j}" for j in range(max(0, b * 4 - 1), min(16, b * 4 + 5))]
                    S.op("pe", mm, reads=rd + ["apool"], writes=[bname])
                    copy_op("act" if gl == 0 else "dve",
                            pooledT[gl * 64:(gl + 1) * 64, gp, b * 512:(b + 1) * 512],
                            bank[gl * 64:(gl + 1) * 64, :], [bname], [f"pooledT{b}.{gp}.{gl}"])
        ypT = A.s("ypT", [128, 2, NT], BF16)
        for b in range(4):
            for gp in range(2):
                bank, bname = BANKS[(b * 2 + gp) % 4]
                S.op("pe", lambda e, b=b, gp=gp, bank=bank: e.matmul(
                    bank, lhsT=wblk[:, gp, :], rhs=pooledT[:, gp, b * 512:(b + 1) * 512],
                    start=True, stop=True),
                    reads=["wblk", f"pooledT{b}.{gp}.0", f"pooledT{b}.{gp}.1"], writes=[bname])
                S.op("act", lambda e, b=b, gp=gp, bank=bank: e.activation(
                    out=ypT[:, gp, b * 512:(b + 1) * 512], in_=bank, func=AF.Identity,
                    scale=sm(lo + C_PSC + gp)), reads=[bname, "small"], writes=["ypT"])
        wout_partial(l, ypT, "ypT", 0, 2, 16, wo=wo_pre)
        S.barrier()

    def attn_phase(l, hh):
        A.reset()
        QT = A.s("QT", [128, 2, NT], BF16)
        KT = A.s("KT", [128, 2, NT], BF16)
        Vaug = A.s("Vaug", [128, 16, 4, 65], BF16)
        kc_tok = A.s("kc_tok", [128, 2, 256], BF16)
        KcT = A.s("KcT", [128, 2, 256], BF16)
        Vc = A.s("Vc", [128, 2, 4, 65], BF16)
        ytok = A.s("ytok", [128, 16, 256], BF16)
        gprod = A.s("gprod", [128, 1], F32)
        gkrow = A.s("gkrow", [128, 256], F32)
        wo_pre = wout_load(l, 512 + hh * 256, 2)
        ebt = [A.s(f"ebt{i}", [128, 30, 128], BF16) for i in range(2)]
        S.dma("sp", ebt[0][:, :, :], bias_d[l, hh * 4].rearrange("p (a k) -> p a k", a=30),
              writes=["ebt0"])
        markB = A.mark()
        wqkv = WN
        S.dma("sp", gkrow[:, :], gqk[l, :, 512:768], writes=["gkrow"])
        gtmp = A.s("gtmp", [128, 2], F32)
        S.dma("sp", gtmp[:, :], gcol[l], writes=["gtmp"])
        S.op("dve", lambda e: e.tensor_tensor(out=gprod[:, :], in0=gtmp[:, 0:1], in1=gtmp[:, 1:2],
                                              op=ALU.mult), reads=["gtmp"], writes=["gprod"])
        S.op("dve", lambda e: e.memset(Vaug[:, :, :, 64:65], 1.0), writes=["Vones"])
        S.dma("pool", kc_tok[:, :, :],
              cachek[l].rearrange("(t p) c -> p t c", p=128)[:, :, hh * 256:(hh + 1) * 256],
              writes=["kc_tok"])
        S.dma("pool", Vc[:, :, :, :],
              cachev[l].rearrange("(t p) (h c) -> p t h c", p=128, c=65)[:, :, hh * 4:(hh + 1) * 4, :],
              writes=["Vc"])

        qkf_l = [A.s(f"qkf{i}", [128, 512], F32) for i in range(2)]
        sq_l = [A.s(f"sq{i}", [128, 512], F32) for i in range(2)]
        ss_l = [A.s(f"ss{i}", [128, 8], F32) for i in range(2)]
        rs_l = [A.s(f"rs{i}", [128, 8], F32) for i in range(2)]
        qn = [A.s(f"qn{i}", [128, 512], F32) for i in range(2)]
        qkb_l = [A.s(f"qkb{i}", [128, 512], BF16) for i in range(3)]
        vf = [A.s(f"vf{i}", [128, 256], F32) for i in range(2)]

        def bc64(t):
            a_ = t[:, :]
            return bass.AP(tensor=a_.tensor, offset=a_.offset, ap=[[8, 128], [1, 8], [0, 64]])

        def stage_mm(i):
            ts_ = slice(i * 128, (i + 1) * 128)
            bqk, nqk = BANKS[0] if i % 2 == 0 else BANKS[3]
            bv, nv = BANKS[1] if i % 2 == 0 else BANKS[4]

            def mmqk(e):
                ins = None
                for k in range(8):
                    ins = e.matmul(bqk, lhsT=hT[:, k, ts_], rhs=wqkv[:, k, 0:512], start=(k == 0),
                                   stop=(k == 7))
                return ins
            S.op("pe", mmqk, reads=HTN(i // 4) + ["wnext"], writes=[nqk])

            def mmv(e):
                ins = None
                for k in range(8):
                    ins = e.matmul(bv[:, 0:256], lhsT=hT[:, k, ts_], rhs=wqkv[:, k, 512:768],
                                   start=(k == 0), stop=(k == 7))
                return ins
            S.op("pe", mmv, reads=HTN(i // 4) + ["wnext"], writes=[nv])

        def stage_ew1(i):
            ts_ = slice(i * 128, (i + 1) * 128)
            bqk, nqk = BANKS[0] if i % 2 == 0 else BANKS[3]
            bv, nv = BANKS[1] if i % 2 == 0 else BANKS[4]
            p2 = i % 2
            qkf, sq, ss = qkf_l[p2], sq_l[p2], ss_l[p2]
            Nq, Ns, Nss = f"qkf{p2}", f"sq{p2}", f"ss{p2}"
            vfi = vf[p2]
            copy_op("act", qkf[:, :], bqk, [nqk], [Nq])
            copy_op("act", vfi[:, :], bv[:, 0:256], [nv], [f"vf{p2}"])
            copy_op("dve", Vaug[:, i, :, 0:64], bv[:, 0:256].rearrange("p (h c) -> p h c", h=4),
                    [nv], [f"Vaug{i}"])
            S.dma("sp", v_out[l, ts_, hh * 256:(hh + 1) * 256], vfi[:, :], reads=[f"vf{p2}"])
            S.op("act", lambda e: e.activation(out=sq[:, :], in_=bqk, func=AF.Square),
                 reads=[nqk], writes=[Ns])
            S.op("dve", lambda e: e.tensor_reduce(
                out=ss[:, :], in_=sq[:, :].rearrange("p (h c) -> p h c", h=8), axis=AX.X, op=ALU.add),
                reads=[Ns], writes=[Nss])

        def stage_ew2(i):
            ts_ = slice(i * 128, (i + 1) * 128)
            p2 = i % 2
            qkf, sq, ss, rs_ = qkf_l[p2], sq_l[p2], ss_l[p2], rs_l[p2]
            qkb = qkb_l[i % 3]
            Nq, Ns, Nss, Nrs, Nqb = f"qkf{p2}", f"sq{p2}", f"ss{p2}", f"rs{p2}", f"qkb{i % 3}"
            S.op("act", lambda e: e.activation(out=rs_[:, :], in_=ss[:, :], func=AF.Ln,
                                               scale=1.0 / 64.0, bias=eps_ap),
                 reads=[Nss, "small"], writes=[Nrs])
            S.op("act", lambda e: e.activation(out=rs_[:, :], in_=rs_[:, :], func=AF.Exp, scale=-0.5),
                 reads=[Nrs], writes=[Nrs])
            S.op("dve", lambda e: e.tensor_tensor(
                out=qkb[:, :].rearrange("p (h c) -> p h c", h=8),
                in0=qkf[:, :].rearrange("p (h c) -> p h c", h=8), in1=bc64(rs_), op=ALU.mult),
                reads=[Nq, Nrs], writes=[Nqb])
            qni = qn[p2]
            S.op("pool", lambda e: e.tensor_tensor(
                out=sq[:, 0:256].rearrange("p (h c) -> p h c", h=4),
                in0=qkf[:, 256:512].rearrange("p (h c) -> p h c", h=4),
                in1=bass.AP(tensor=rs_[:, :].tensor, offset=rs_[:, :].offset + 4,
                            ap=[[8, 128], [1, 4], [0, 64]]), op=ALU.mult),
                reads=[Nq, Nrs, Ns], writes=[Ns])
            S.op("pool", lambda e: e.tensor_tensor(out=qni[:, 0:256], in0=sq[:, 0:256], in1=gkrow[:, :],
                                                   op=ALU.mult), reads=[Ns, "gkrow"],
                 writes=[f"qn{p2}"])
            S.dma("sp", k_out[l, ts_, hh * 256:(hh + 1) * 256], qni[:, 0:256], reads=[f"qn{p2}"])

        def stage_tr(i):
            ts_ = slice(i * 128, (i + 1) * 128)
            qkb = qkb_l[i % 3]
            Nqb = f"qkb{i % 3}"

            def trq(e):
                ins = None
                for c4 in range(4):
                    ins = e.transpose(out=psT[:, c4 * 128:(c4 + 1) * 128],
                                      in_=qkb[:, c4 * 128:(c4 + 1) * 128], identity=identb[:, :])
                return ins
            S.op("pe", trq, reads=[Nqb, "identb"], writes=["psT"])
            copy_op("dve", QT[:, :, ts_], psT[:, 0:256].rearrange("p (c t) -> p c t", c=2), ["psT"],
                    [f"QT{i}"])
            S.op("act", lambda e: e.activation(
                out=KT[:, :, ts_], in_=psT[:, 256:512].rearrange("p (c t) -> p c t", c=2),
                func=AF.Identity, scale=gprod[:, 0:1]), reads=["psT", "gprod"], writes=[f"KT{i}"])

        stage_mm(0)
        stage_mm(1)
        stage_ew1(0)
        def trc(e):
            ins = None
            for t in range(2):
                for c in range(2):
                    ins = e.transpose(out=psT[:, (t * 2 + c) * 128:(t * 2 + c + 1) * 128],
                                      in_=kc_tok[:, t, c * 128:(c + 1) * 128], identity=identb[:, :])
            return ins
        S.op("pe", trc, reads=["kc_tok", "identb"], writes=["psT"])
        for t in range(2):
            S.op("act", lambda e, t=t: e.activation(
                out=KcT[:, :, t * 128:(t + 1) * 128],
                in_=psT[:, t * 256:(t + 1) * 256].rearrange("p (c k) -> p c k", c=2),
                func=AF.Identity, scale=gtmp[:, 0:1]), reads=["psT", "gtmp"], writes=["KcT"])

        for i in range(16):
            if i + 2 < 16:
                stage_mm(i + 2)
            if i + 1 < 16:
                stage_ew1(i + 1)
            stage_ew2(i)
            if i >= 1:
                stage_tr(i - 1)
        stage_tr(15)
        S.op("act", lambda e: e.activation(out=ebt[0][:, :, :], in_=ebt[0][:, :, :], func=AF.Exp),
             reads=["ebt0"], writes=["ebt0"])
        S.barrier()

        A.reset(markB)
        if hh == 0:
            prefetch("attn", l, 1)
        else:
            prefetch("hyena", l)
        PT = A.s("PT", [128, 16, 5, 128], BF16)
        PTc = [A.s(f"PTc{i}", [128, 2, NT], BF16) for i in range(2)]
        rcp = [A.s(f"rcp{i}", [128, 1], F32) for i in range(2)]
        psS = [(psS0, ["psS0.0", "psS0.1"]), (psS1, ["psS1.0", "psS1.1"])]
        psOs = [(psO, "psO"), (psG1, "psG1")]
        users = [[i for i in range(16) if start_attn(i) <= j <= start_attn(i) + 4] for j in range(16)]
        done_at = [[i for i in range(16) if start_attn(i) + 4 == j] for j in range(16)]
        PT_ap = PT[:, :, :, :]

        def pt_out(i0, n, j):
            o0 = i0 * 640 + (j - start_attn(i0)) * 128
            if n == 1:
                stride = 640
            else:
                o1 = (i0 + 1) * 640 + (j - start_attn(i0 + 1)) * 128
                stride = o1 - o0
            return bass.AP(tensor=PT_ap.tensor, offset=PT_ap.offset + o0,
                           ap=[[PT_ap.ap[0][0], 128], [stride, n], [1, 128]])

        def runs(us, j):
            out, cur = [], [us[0]]
            for i in us[1:]:
                d_new = (i * 640 + (j - start_attn(i)) * 128) - (cur[-1] * 640 + (j - start_attn(cur[-1])) * 128)
                if len(cur) >= 2:
                    d_old = (cur[1] * 640 + (j - start_attn(cur[1])) * 128) - (cur[0] * 640 + (j - start_attn(cur[0])) * 128)
                    if d_new != d_old:
                        out.append(cur)
                        cur = [i]
                        continue
                cur.append(i)
            out.append(cur)
            return out

        pvn = [0]

        def head_ctx_steps(hl, two_banks=False):
            h = hh * 4 + hl
            c, hp = hl // 2, hl % 2
            pr = slice(hp * 64, (hp + 1) * 64)
            eb, ebn = ebt[hl % 2], f"ebt{hl % 2}"
            ptc, ptcn = PTc[hl % 2], f"PTc{hl % 2}"
            if hl > 0:
                S.dma("sp", eb[:, :, :], bias_d[l, h].rearrange("p (a k) -> p a k", a=30), writes=[ebn])
                yield
                for pc in range(6):
                    S.op("act", lambda e, pc=pc: e.activation(
                        out=eb[:, pc * 5:pc * 5 + 5, :], in_=eb[:, pc * 5:pc * 5 + 5, :], func=AF.Exp),
                        reads=[ebn], writes=[ebn])
                    yield
            n_ = 0
            for t in range(2):
                for b_ in range(4):
                    cb, cbn = (psG1, "psG1") if (two_banks and n_ % 2 == 1) else (psG0, "psG0")
                    n_ += 1
                    S.op("pe", lambda e, cb=cb: e.matmul(
                        cb[:, :], lhsT=KcT[pr, c, t * 128:(t + 1) * 128],
                        rhs=QT[pr, c, b_ * 512:(b_ + 1) * 512], start=True, stop=True),
                        reads=["KcT"] + [f"QT{i}" for i in range(b_ * 4, b_ * 4 + 4)], writes=[cbn])
                    S.op("act", lambda e, cb=cb: e.activation(
                        out=ptc[:, t, b_ * 512:(b_ + 1) * 512], in_=cb[:, :], func=AF.Exp, scale=0.125),
                        reads=[cbn], writes=[f"{ptcn}.{b_}"])
                    yield

        g0 = head_ctx_steps(0, two_banks=True)
        for _ in g0:
            pass
        for hl in range(4):
            h = hh * 4 + hl
            c, hp = hl // 2, hl % 2
            pr = slice(hp * 64, (hp + 1) * 64)
            eb, ebn = ebt[hl % 2], f"ebt{hl % 2}"
            ptc, ptcn = PTc[hl % 2], f"PTc{hl % 2}"
            gnext = head_ctx_steps(hl + 1) if hl + 1 < 4 else iter(())

            def mm_s(j):
                us = users[j]
                i0, n = us[0], len(us)
                pS, pSn = psS[j % 2]

                def f(e):
                    ins = None
                    for c0 in range(0, n * 128, 512):
                        c1 = min(n * 128, c0 + 512)
                        ins = e.matmul(pS[:, c0:c1], lhsT=KT[pr, c, j * 128:(j + 1) * 128],
                                       rhs=QT[pr, c, i0 * 128 + c0:i0 * 128 + c1], start=True, stop=True)
                    return ins
                S.op("pe", f, reads=[f"KT{j}"] + [f"QT{i}" for i in us], writes=pSn)

            def exp_s(j):
                us = users[j]
                i0 = us[0]
                pS, pSn = psS[j % 2]
                for run in runs(us, j):
                    r0, n = run[0], len(run)
                    S.op("act", lambda e, r0=r0, n=n: e.activation(
                        out=pt_out(r0, n, j),
                        in_=pS[:, (r0 - i0) * 128:(r0 - i0 + n) * 128].rearrange("p (i q) -> p i q", i=n),
                        func=AF.Exp, scale=0.125), reads=pSn, writes=[f"PT{i}" for i in run])

            def bias_mul(i):
                S.op("dve", lambda e: e.tensor_tensor(out=PT[:, i, :, :], in0=PT[:, i, :, :],
                                                      in1=eb[:, pid_attn(i) * 5:pid_attn(i) * 5 + 5, :],
                                                      op=ALU.mult), reads=[f"PT{i}", ebn],
                     writes=[f"PT{i}"])

            def pv(i):
                st = start_attn(i)
                pO, pOn = psOs[pvn[0] % 2]
                rc, rcn = rcp[pvn[0] % 2], f"rcp{pvn[0] % 2}"
                pvn[0] += 1

                def f(e):
                    ins = None
                    for s_ in range(7):
                        if s_ < 5:
                            lhsT, rhs = PT[:, i, s_, :], Vaug[:, st + s_, hl, :]
                        else:
                            lhsT, rhs = ptc[:, s_ - 5, i * 128:(i + 1) * 128], Vc[:, s_ - 5, hl, :]
                        ins = e.matmul(pO[:, 0:65], lhsT=lhsT, rhs=rhs, start=(s_ == 0), stop=(s_ == 6))
                    return ins
                S.op("pe", f, reads=[f"PT{i}", f"{ptcn}.{i // 4}", "Vc", "Vones"] +
                     [f"Vaug{j}" for j in range(st, st + 5)], writes=[pOn])
                S.op("dve", lambda e: e.reciprocal(out=rc[:, :], in_=pO[:, 64:65]), reads=[pOn],
                     writes=[rcn])
                S.op("dve", lambda e: e.tensor_scalar(
                    out=ytok[:, i, hl * 64:(hl + 1) * 64], in0=pO[:, 0:64], scalar1=rc[:, 0:1],
                    scalar2=None, op0=ALU.mult), reads=[pOn, rcn], writes=[f"ytok{i}"])

            def tr_y(i):
                def f(e):
                    ins = None
                    for c_ in range(2):
                        ins = e.transpose(out=psT[:, c_ * 128:(c_ + 1) * 128],
                                          in_=ytok[:, i, c_ * 128:(c_ + 1) * 128], identity=identb[:, :])
                    return ins
                S.op("pe", f, reads=[f"ytok{i}", "identb"], writes=["psT"])
                copy_op("dve", PT[:, i, 0:2, :], psT[:, 0:256].rearrange("p (c t) -> p c t", c=2),
                        ["psT"], [f"PT{i}"])

            pend, ydone = [], []
            mm_s(0)
            for j in range(16):
                if j + 1 < 16:
                    mm_s(j + 1)
                exp_s(j)
                if hl == 3:
                    for i in ydone:
                        tr_y(i)
                    ydone = []
                for i in pend:
                    pv(i)
                    ydone.append(i)
                pend = done_at[j]
                for i in pend:
                    bias_mul(i)
                if 1 <= j:
                    next(gnext, None)
            for i in pend:
                pv(i)
                ydone.append(i)
            for _ in gnext:
                pass
            if hl == 3:
                for i in ydone:
                    tr_y(i)
        def yna_rhs(kc, b):
            a_ = PT[:, 4 * b, kc, :]
            return bass.AP(tensor=a_.tensor, offset=a_.offset, ap=[[a_.ap[0][0], 128], [640, 4], [1, 128]])
        wout_partial(l, None, None, 512 + hh * 256, 2, 16, wo=wo_pre, rhs_fn=yna_rhs,
                     ynames=lambda b: [f"PT{i}" for i in range(4 * b, 4 * b + 4)])
        S.barrier()

    def hyena_phase(l):
        A.reset()
        lo = l * PL
        x1T = A.s("x1T", [128, 2, NT], BF16)
        x2T = A.s("x2T", [128, 2, NT], BF16)
        vT = A.s("vT", [128, 2, NT], BF16)
        wo_pre = wout_load(l, 256, 2)
        mark = A.mark()
        whyp = WN
        ust_l = [A.s(f"ust{i}", [128, NT], F32) for i in range(2)]
        acc_l = [A.s(f"acc{i}", [128, NT], F32) for i in range(2)]
        dsts = [x1T, x1T, x2T, x2T, vT, vT]
        HYW = [["hy0"], ["hy1"], ["hy2e", "hy2o"]]
        for fc in range(6):
            ust, acc = ust_l[fc % 2], acc_l[fc % 2]
            Nu, Na = f"ust{fc % 2}", f"acc{fc % 2}"
            for b in range(4):
                bank, bname = BANKS[b % 2]

                def mm(e, fc=fc, b=b, bank=bank):
                    ins = None
                    for k in range(8):
                        ins = e.matmul(bank, lhsT=whyp[:, k, fc * 128:(fc + 1) * 128],
                                       rhs=hT[:, k, b * 512:(b + 1) * 512], start=(k == 0), stop=(k == 7))
                    return ins
                S.op("pe", mm, reads=["wnext"] + HTN(b), writes=[bname])
                copy_op("act", ust[:, b * 512:(b + 1) * 512], bank, [bname], [Nu])
                S.op("act", lambda e, fc=fc, b=b, bank=bank, acc=acc: e.activation(
                    out=acc[:, b * 512:(b + 1) * 512], in_=bank, func=AF.Identity,
                    scale=sm(lo + C_CW + 6 + fc), bias=sm(lo + C_CB + fc)),
                    reads=[bname, "small"], writes=[Na])
            cw = lambda tap, fc=fc: sm(lo + C_CW + tap * 6 + fc)
            dst = dsts[fc]
            S.op("dve", lambda e, fc=fc, ust=ust, acc=acc: e.scalar_tensor_tensor(
                out=acc[:, 1:NT], in0=ust[:, 0:NT - 1], scalar=cw(0), in1=acc[:, 1:NT],
                op0=ALU.mult, op1=ALU.add), reads=[Nu, "small", Na], writes=[Na])
            a_hi = acc[:, 256:NT].rearrange("p (s t) -> p s t", t=256)[:, :, 0:1]
            u_lo = ust[:, 0:NT - 256].rearrange("p (s t) -> p s t", t=256)[:, :, 255:256]
            S.op("dve", lambda e, fc=fc, ust=ust, acc=acc: e.scalar_tensor_tensor(
                out=a_hi, in0=u_lo, scalar=nw[:, l, fc:fc + 1], in1=a_hi, op0=ALU.mult, op1=ALU.add),
                reads=[Nu, "nw", Na], writes=[Na])
            a_lo = acc[:, 0:NT - 256].rearrange("p (s t) -> p s t", t=256)[:, :, 255:256]
            u_hi = ust[:, 256:NT].rearrange("p (s t) -> p s t", t=256)[:, :, 0:1]
            S.op("dve", lambda e, fc=fc, ust=ust, acc=acc: e.scalar_tensor_tensor(
                out=a_lo, in0=u_hi, scalar=nw[:, l, 6 + fc:7 + fc], in1=a_lo, op0=ALU.mult,
                op1=ALU.add), reads=[Nu, "nw", Na], writes=[Na])
            S.op("dve", lambda e, fc=fc, ust=ust, acc=acc, dst=dst: e.scalar_tensor_tensor(
                out=dst[:, fc % 2, 0:NT - 1], in0=ust[:, 1:NT], scalar=cw(2), in1=acc[:, 0:NT - 1],
                op0=ALU.mult, op1=ALU.add), reads=[Nu, "small", Na], writes=HYW[fc // 2])
            S.op("dve", lambda e, fc=fc, acc=acc, dst=dst: e.tensor_copy(
                out=dst[:, fc % 2, NT - 1:NT], in_=acc[:, NT - 1:NT]), reads=[Na],
                writes=HYW[fc // 2])
        S.barrier()
        A.reset(mark)
        prefetch("mlp", l)
        zt = A.s("zt", [128, 16, 256], BF16)
        Ypm = A.s("Ypm", [128, 2, 16, 256], BF16)
        UV = [A.s(f"uv{i}", [128, 256], F32) for i in range(4)]
        NF, NKB, NI, PF, PI = 6, 4, 8, 2, 7
        fring = [A.at(f"fwr{i}", [128, 16, 128], BF16, HT_OFF + i * 4096) for i in range(NF)]
        ztp = A.at("ztp", [128, 8, 256], BF16, HT_OFF + NF * 4096)
        iring = [A.s(f"ivr{i}", [128, 1024], BF16) for i in range(NI - 2)] + \
                [A.at(f"ivr{NI - 2 + i}", [128, 1024], BF16, HT_OFF + NF * 4096 + 4096 + i * 2048)
                 for i in range(2)]
        kb = [A.s(f"kb{i}", [128, 2, 256], F32) for i in range(NKB)]
        tt = [A.s(f"tt{i}", [128, 256], F32) for i in range(8)]
        gst = [A.s(f"gst{i}", [128, 512], F32) for i in range(2)]
        yhT = A.s("yhT", [128, 2, NT], BF16)
        ninv = [0]

        def load_f(j, o):
            for ri in range(2):
                q_ = 2 * j + ri
                S.dma("sp", fring[q_ % NF][:, :, :],
                      fwd_d[j + 8 * ri].rearrange("p (s f) -> p s f", s=16), writes=[f"fwr{q_ % NF}"])

        def load_k(q, o):
            j, m = q // 2, q % 2
            for ri in range(2):
                S.dma("sp", kb[q % NKB][:, ri, :],
                      khat[l, ri * 16 + j + 8 * m, :, o * 256:(o + 1) * 256], writes=[f"kb{q % NKB}"])

        def load_i(half, jj):
            n_ = ninv[0]
            ninv[0] += 1
            S.dma("sp", iring[n_ % NI][:, :], inv_d[half, jj], writes=[f"ivr{n_ % NI}"])
            return n_ % NI

        for o in range(2):
            src = vT
            gate = x1T if o == 0 else x2T
            gname = "hy0" if o == 0 else "hy1"
            for j in range(PF):
                load_f(j, o)
            for q in range(2):
                load_k(q, o)
            for half in range(2):
                for g4 in range(4):

                    def trz(e, half=half, g4=g4):
                        ins = None
                        for tl in range(2):
                            t = half * 8 + g4 * 2 + tl
                            for c in range(2):
                                a_ = src[:, c, 256 * (t % 8) + t // 8:256 * (t % 8) + t // 8 + 1]
                                sel_ = bass.AP(tensor=a_.tensor, offset=a_.offset,
                                               ap=[[a_.ap[0][0], 128], [2, 128]])
                                ins = e.transpose(
                                    out=psT[:, (tl * 2 + c) * 128:(tl * 2 + c + 1) * 128],
                                    in_=sel_, identity=identb[:, :])
                        return ins
                    S.op("pe", trz, reads=["hy2e" if half == 0 else "hy2o", "identb"], writes=["psT"])
                    t0 = half * 8 + g4 * 2
                    copy_op("dve" if g4 % 2 == 0 else "act", zt[:, t0:t0 + 2, :],
                            psT[:, 0:512].rearrange("p (t c) -> p t c", t=2), ["psT"], ["zt"])
            S.op("dve", lambda e: e.tensor_scalar(out=ztp[:, :, :], in0=zt[:, 8:16, :], scalar1=-1.0,
                                                  scalar2=None, op0=ALU.mult), reads=["zt"],
                 writes=["ztp"])
            for q in range(16):
                j, m = q // 2, q % 2
                zname = "zt" if m == 0 else "ztp"
                kbt, kbn = kb[q % NKB], f"kb{q % NKB}"
                banks = (BANKS[(q % 2) * 2], BANKS[(q % 2) * 2 + 1])
                for ri in range(2):
                    q_ = 2 * j + ri
                    fw = fring[q_ % NF]
                    bank, bname = banks[ri]

                    def mm(e, fw=fw, bank=bank, m=m):
                        ins = None
                        for s_ in range(16):
                            rhs = ztp[:, s_ - 8, :] if (m == 1 and s_ >= 8) else zt[:, s_, :]
                            ins = e.matmul(bank[:, 0:256], lhsT=fw[:, s_, :], rhs=rhs,
                                           start=(s_ == 0), stop=(s_ == 15))
                        return ins
                    S.op("pe", mm, reads=[f"fwr{q_ % NF}", "zt", zname], writes=[bname])
                if m == 1 and j + PF < 8:
                    load_f(j + PF, o)
                if q + 2 < 16:
                    load_k(q + 2, o)
                (bR, nR), (bI, nI) = banks
                sR, sI = j + 8 * m, 16 + j + 8 * m
                tb_ = (q % 2) * 4
                T0, T1, T2, T3 = tt[tb_], tt[tb_ + 1], tt[tb_ + 2], tt[tb_ + 3]
                N0, N1, N2, N3 = f"tt{tb_}", f"tt{tb_ + 1}", f"tt{tb_ + 2}", f"tt{tb_ + 3}"
                S.op("dve", lambda e, bR=bR, kbt=kbt, T0=T0: e.tensor_tensor(
                    out=T0[:, :], in0=bR[:, 0:256], in1=kbt[:, 0, :], op=ALU.mult),
                    reads=[nR, kbn], writes=[N0])
                S.op("dve", lambda e, bR=bR, kbt=kbt, T2=T2: e.tensor_tensor(
                    out=T2[:, :], in0=bR[:, 0:256], in1=kbt[:, 1, :], op=ALU.mult),
                    reads=[nR, kbn], writes=[N2])
                S.op("dve", lambda e, bI=bI, kbt=kbt, T1=T1: e.tensor_tensor(
                    out=T1[:, :], in0=bI[:, 0:256], in1=kbt[:, 1, :], op=ALU.mult),
                    reads=[nI, kbn], writes=[N1])
                S.op("dve", lambda e, bI=bI, kbt=kbt, T3=T3: e.tensor_tensor(
                    out=T3[:, :], in0=bI[:, 0:256], in1=kbt[:, 0, :], op=ALU.mult),
                    reads=[nI, kbn], writes=[N3])
                U0, U1 = (UV[0], UV[1]) if m == 0 else (UV[2], UV[3])
                n0, n1 = ("uv0", "uv1") if m == 0 else ("uv2", "uv3")
                S.op("pool", lambda e, T0=T0, T1=T1, U0=U0: e.tensor_tensor(
                    out=U0[:, :], in0=T0[:, :], in1=T1[:, :], op=ALU.subtract),
                    reads=[N0, N1], writes=[n0])
                S.op("pool", lambda e, T2=T2, T3=T3, U1=U1: e.tensor_tensor(
                    out=U1[:, :], in0=T2[:, :], in1=T3[:, :], op=ALU.add),
                    reads=[N2, N3], writes=[n1])
                if m == 1:
                    for ri in range(2):
                        S.op("dve", lambda e, ri=ri: e.tensor_tensor(
                            out=Ypm[:, 0, ri * 8 + j, :], in0=UV[ri][:, :], in1=UV[2 + ri][:, :],
                            op=ALU.add), reads=[f"uv{ri}", f"uv{2 + ri}"], writes=["Ypm"])
                        S.op("dve", lambda e, ri=ri: e.tensor_tensor(
                            out=Ypm[:, 1, ri * 8 + j, :], in0=UV[ri][:, :], in1=UV[2 + ri][:, :],
                            op=ALU.subtract), reads=[f"uv{ri}", f"uv{2 + ri}"], writes=["Ypm"])
                if q == 8:
                    pre_slots = [load_i(0, jj) for jj in range(PI)]
            slots = list(pre_slots)
            for par in range(2):
                acc_b = [BANKS[3], BANKS[4], BANKS[5], BANKS[6]] if par == 0 else \
                        [BANKS[0], BANKS[1], BANKS[2], BANKS[3]]
                for jj in range(16):
                    sl_ = slots.pop(0)
                    iv = iring[sl_]
                    ivn = f"ivr{sl_}"

                    def mm(e, jj=jj, iv=iv, acc_b=acc_b, par=par):
                        ins = None
                        for cc in range(2):
                            for tb in range(2):
                                ins = e.matmul(acc_b[cc * 2 + tb][0],
                                               lhsT=Ypm[:, par, jj, cc * 128:(cc + 1) * 128],
                                               rhs=iv[:, tb * 512:(tb + 1) * 512], start=(jj == 0),
                                               stop=(jj == 15))
                        return ins
                    S.op("pe", mm, reads=[ivn, "Ypm"], writes=[b_[1] for b_ in acc_b])
                    nxt = jj + PI
                    if nxt < 16:
                        slots.append(load_i(par, nxt))
                    elif par == 0:
                        slots.append(load_i(1, nxt - 16))
                for cc in range(2):
                    for tb in range(2):
                        bank, bname = acc_b[cc * 2 + tb]
                        t0_ = tb * 1024 + par
                        g_ = gst[(cc * 2 + tb) % 2]
                        gn_ = f"gst{(cc * 2 + tb) % 2}"
                        dst = vT if o == 0 else yhT
                        hyp = "hy2e" if par == 0 else "hy2o"
                        dn = hyp if o == 0 else "yhT"

                        def sel(tn, cc=cc, t0_=t0_):
                            a_ = tn[:, cc, t0_:t0_ + 1]
                            return bass.AP(tensor=a_.tensor, offset=a_.offset,
                                           ap=[[a_.ap[0][0], 128], [2, 512]])
                        S.op("dve", lambda e, cc=cc, bank=bank, g_=g_: e.scalar_tensor_tensor(
                            out=g_[:, :], in0=sel(src), scalar=sm(lo + C_HB + o * 2 + cc), in1=bank,
                            op0=ALU.mult, op1=ALU.add), reads=[hyp, "small", bname], writes=[gn_])
                        S.op("pool", lambda e, g_=g_, dst=dst: e.tensor_tensor(
                            out=sel(dst), in0=g_[:, :], in1=sel(gate), op=ALU.mult),
                            reads=[gn_, gname], writes=[dn])
        wout_partial(l, yhT, "yhT", 256, 2, 16, wo=wo_pre)
        S.barrier()

    def mlp_phase(l):
        A.reset()
        act = A.s("act", [128, 8, NT], BF16)
        ring = [A.s(f"wr{i}", [128, 8, 512], BF16) for i in range(4)]
        rl = [A.s(f"rl{i}", [128, 512], BF16) for i in range(2)]
        nld = 0
        for fg in range(4):
            ups = []
            for hf in range(2):
                if fg == 0 and hf == 0:
                    ups.append((WN, "wnext"))
                    continue
                wt, wn_ = ring[nld % 4], f"wr{nld % 4}"
                nld += 1
                c0 = fg * 1024 + hf * 512
                load_w(wt[:, :, :], w_up[l], (0, D), (c0, c0 + 512), wn_)
                ups.append((wt, wn_))
            dns = []
            for hf in range(2):
                wt, wn_ = ring[nld % 4], f"wr{nld % 4}"
                nld += 1
                load_w(wt[:, :, :], w_down[l], (fg * 1024, (fg + 1) * 1024), (hf * 512, (hf + 1) * 512),
                       wn_)
                dns.append((wt, wn_))
            n = 0
            for fc in range(8):
                wt, wn_ = ups[fc // 4]
                for b in range(4):
                    bs = slice(b * 512, (b + 1) * 512)
                    bank, bname = BANKS[n % 4]
                    r_, rn_ = rl[n % 2], f"rl{n % 2}"
                    n += 1

                    def mm(e, fc=fc, bs=bs, bank=bank, wt=wt):
                        ins = None
                        for k in range(8):
                            ins = e.matmul(bank, lhsT=wt[:, k, (fc % 4) * 128:(fc % 4 + 1) * 128],
                                           rhs=hT[:, k, bs], start=(k == 0), stop=(k == 7))
                        return ins
                    S.op("pe", mm, reads=[wn_] + HTN(b), writes=[bname])
                    S.op("act", lambda e, bank=bank, r_=r_: e.activation(out=r_[:, :], in_=bank,
                                                                         func=AF.Relu),
                         reads=[bname], writes=[rn_])
                    S.op("pool", lambda e, fc=fc, bs=bs, r_=r_: e.tensor_tensor(
                        out=act[:, fc, bs], in0=r_[:, :], in1=r_[:, :], op=ALU.mult),
                        reads=[rn_], writes=[f"act{b}"])
            if fg == 0 and l + 1 < 2:
                prefetch("pool", l + 1)
            last = (l == 1 and fg == 3)
            order = [(dc, b) for b in range(4) for dc in range(8)] if last else \
                    [(dc, b) for dc in range(8) for b in range(4)]
            for (dc, b) in order:
                wt, wn_ = dns[dc // 4]
                bs = slice(b * 512, (b + 1) * 512)
                bank, bname = BANKS[n % 4]
                n += 1

                def mm(e, dc=dc, bs=bs, bank=bank, wt=wt):
                    ins = None
                    for fc in range(8):
                        ins = e.matmul(bank, lhsT=wt[:, fc, (dc % 4) * 128:(dc % 4 + 1) * 128],
                                       rhs=act[:, fc, bs], start=(fc == 0), stop=(fc == 7))
                    return ins
                S.op("pe", mm, reads=[wn_, f"act{b}"], writes=[bname])
                S.op("dve", lambda e, dc=dc, bs=bs, bank=bank: e.scalar_tensor_tensor(
                    out=xT[:, dc, bs], in0=bank, scalar=modv[:, l, 40 + dc:41 + dc], in1=xT[:, dc, bs],
                    op0=ALU.mult, op1=ALU.add), reads=[bname, "modv", f"xT{b}"],
                    writes=[f"xT{b}"])
                if last and dc == 7:
                    if b == 0:
                        yst_l = [A.s(f"yst{i}", [128, D], F32) for i in range(2)]
                        Y_DONE[0] = True
                    for i in range(4 * b, 4 * b + 4):
                        store_y_tile(i, yst_l[i % 2], f"yst{i % 2}", (BANKS[5], BANKS[6]))
        S.barrier()

    phases = []
    for l in range(2):
        phases += [("norm1", lambda l=l: rmsnorm(l, 0)), ("pool", lambda l=l: pool_phase(l)),
                   ("attn0", lambda l=l: attn_phase(l, 0)), ("attn1", lambda l=l: attn_phase(l, 1)),
                   ("hyena", lambda l=l: hyena_phase(l)), ("norm2", lambda l=l: rmsnorm(l, 1)),
                   ("mlp", lambda l=l: mlp_phase(l))]
    for idx, (pname, fn) in enumerate(phases):
        if stop is not None and idx >= stop:
            break
        fn()

    if not Y_DONE[0]:
        A.reset()
        yst = [A.s(f"yst{i}", [128, D], F32) for i in range(2)]
        for i in range(NTILE):
            store_y_tile(i, yst[i % 2], f"yst{i % 2}", (BANKS[0], BANKS[1]))
    S.barrier()
    return nc


_PERM = np.concatenate([np.arange(0, NT, 2), np.arange(1, NT, 2)])


def _dft_tables(L, nblk):
    N = 2 * L
    T = L * nblk
    pos = np.arange(L, dtype=np.float64)
    fwd = np.zeros((16, T, 128), np.float32)
    inv = np.zeros((32 * 128, T), np.float32)
    for j in range(8):
        for p in range(128):
            bq, f = (0, j * 128 + p) if nblk == 1 else (j, p)
            sl = slice(bq * L, (bq + 1) * L)
            th = 2.0 * np.pi * (f + 0.5) * pos / N
            thm = 2.0 * np.pi * (L - 1 - f + 0.5) * pos / N
            fwd[j, sl, p] = np.cos(th)
            fwd[8 + j, sl, p] = -np.sin(th)
            inv[j * 128 + p, sl] = (2.0 / N) * np.cos(th)
            inv[(8 + j) * 128 + p, sl] = (2.0 / N) * np.cos(thm)
            inv[(16 + j) * 128 + p, sl] = -(2.0 / N) * np.sin(th)
            inv[(24 + j) * 128 + p, sl] = (2.0 / N) * np.sin(thm)
    fwd = fwd[:, _PERM, :]
    fwd_b = fwd.reshape(16, 16, 128, 128).transpose(0, 2, 1, 3).reshape(16, 128, 16 * 128)
    invp = np.zeros((2, 16 * 128, T // 2), np.float32)
    for j in range(8):
        for p in range(128):
            bq, f = (0, j * 128 + p) if nblk == 1 else (j, p)
            for par in range(2):
                tl = np.arange(par, L, 2, dtype=np.float64)
                cols = ((bq * L + tl - par) // 2).astype(np.int64)
                th = 2.0 * np.pi * (f + 0.5) * tl / N
                invp[par, j * 128 + p, cols] = (2.0 / N) * np.cos(th)
                invp[par, (8 + j) * 128 + p, cols] = -(2.0 / N) * np.sin(th)
    inv_b = invp.reshape(2, 16, 128, 1024)
    return (np.ascontiguousarray(fwd_b).astype(NPBF), np.ascontiguousarray(inv_b).astype(NPBF))


def _hyena_consts(L, nblk):
    t = np.linspace(0.0, 1.0, L, dtype=np.float32)[:, None]
    w = (2.0 * np.pi * np.arange(L, dtype=np.float32)[:, None] / L).astype(np.float32)
    f = np.linspace(1e-4, 15, 16, dtype=np.float32)[None, :]
    z = np.concatenate([t, np.cos(f * w), -np.sin(f * w)], axis=-1).astype(np.float32)
    deltas = np.abs(np.linspace(np.log(1e-2) / 1.5, np.log(1e-2) / 0.3, 256, dtype=np.float32))
    decay = np.exp(-t * deltas[None, :]).astype(np.float32)
    zT = np.ascontiguousarray(np.tile(z, (nblk, 1))[_PERM].T)
    dec = np.tile(decay, (nblk, 1))[_PERM].reshape(16, 128, 256).transpose(1, 0, 2)
    pos = np.arange(NT)[_PERM]
    wn = (pos < L).astype(np.float32).reshape(16, 128).T
    wn = np.repeat(wn[:, :, None], 128, axis=2)
    tau0 = (pos % L != 0).astype(np.float32).reshape(16, 128).T
    return zT, np.ascontiguousarray(dec), np.ascontiguousarray(wn), np.ascontiguousarray(tau0)


def _pool_consts(L, nblk):
    T = L * nblk
    out = np.zeros((128, 4, 3, 4, 128), np.float32)
    tl = np.arange(L)
    for g, wd in enumerate((2, 4, 8, 16)):
        lo = np.clip(tl - wd // 2, 0, L - 1)
        hi = np.clip(tl + (wd - 1 - wd // 2), 0, L - 1)
        M1 = np.zeros((L, L), np.float64)
        for t_ in range(L):
            M1[t_, lo[t_]:hi[t_] + 1] = 1.0 / (hi[t_] - lo[t_] + 1)
        M1 -= np.eye(L)
        M = np.zeros((T, T), np.float64)
        for b in range(nblk):
            M[b * L:(b + 1) * L, b * L:(b + 1) * L] = M1
        for pat, j in enumerate((0, 2, 3, 15)):
            for ri, r in enumerate((-1, 0, 1)):
                if 0 <= j + r < 16:
                    blk = M[j * 128:(j + 1) * 128, (j + r) * 128:(j + r + 1) * 128]
                    out[:, pat, ri, g, :] = blk.T
    return out.reshape(128, -1).astype(NPBF)


def _bias_bank(rel_bias, sample):
    out = np.full((2, 8, 128, 30, 128), NEG, np.float32)
    reps = (0, 1, 2, 3, 14, 15)
    if sample:
        r_all = np.arange(32)
        c_all = np.arange(64)
        row_start = np.clip(r_all - 4, 0, 24)
        col_start = np.clip(c_all - 8, 0, 48)
        for pat, i in enumerate(reps):
            st = start_attn(i)
            q = i * 128 + np.arange(128)
            qr, qc = q // 64, q % 64
            for s_ in range(5):
                k = (st + s_) * 128 + np.arange(128)
                kr, kc = k // 64, k % 64
                vr = (kr[None, :] >= row_start[qr][:, None]) & (kr[None, :] < row_start[qr][:, None] + 8)
                vc = (kc[None, :] >= col_start[qc][:, None]) & (kc[None, :] < col_start[qc][:, None] + 16)
                valid = vr & vc
                dr = np.clip(kr[None, :] - qr[:, None] + 7, 0, 14)
                dc = np.clip(kc[None, :] - qc[:, None], -15, 15) + 15
                vals = rel_bias[:, :, dr, dc]
                out[:, :, :, pat * 5 + s_, :] = np.where(valid[None, None], vals, NEG)
    else:
        for pat, i in enumerate(reps):
            st = start_attn(i)
            for s_ in range(5):
                j = st + s_
                if j // 2 == i // 2:
                    out[:, :, :, pat * 5 + s_, :] = 0.0
    out = np.ascontiguousarray(out.transpose(0, 1, 4, 3, 2))
    return out.reshape(2, 8, 128, 30 * 128).astype(NPBF)


def _colvec(v):
    return np.ascontiguousarray(np.asarray(v, np.float32).reshape(-1, 128).T)


_CACHE = {}


def _consts(sample):
    key = ("c", sample)
    if key not in _CACHE:
        L, nblk = (2048, 1) if sample else (256, 8)
        fwd_b, inv_b = _dft_tables(L, nblk)
        zT, dec, wn, tau0 = _hyena_consts(L, nblk)
        _CACHE[key] = dict(fwd=fwd_b, inv=inv_b, zT=zT, decay=dec, wn=wn, tau0=tau0,
                           apool=_pool_consts(L, nblk))
    return _CACHE[key]


def _get_nc(stop=None):
    key = ("nc", stop)
    if key not in _CACHE:
        _CACHE[key] = build_nc(stop=stop)
    return _CACHE[key]


def make_in_maps(inp):
    g = {k: np.asarray(v) for k, v in inp.items()}
    shared = dict(
        w_mod=g["w_mod"], w_in=g["w_in"], w_out=g["w_out"], w_up=g["w_up"], w_down=g["w_down"],
        pool_w=g["pool_w"], f1_w=g["hy_f1_w"], f2_w=g["hy_f2_w"], f3_w=g["hy_f3_w"],
        identb=np.eye(128, dtype=np.float32).astype(NPBF), identf=np.eye(128, dtype=np.float32),
        onesm=np.full((128, 128), 1.0 / 1024.0, np.float32).astype(NPBF))
    small64 = np.zeros((64, NV64), np.float32)
    for l in range(2):
        small64[:, l * 4 + 0] = g["hy_f1_b"][l]
        small64[:, l * 4 + 1] = g["hy_f1_freq"][l]
        small64[:, l * 4 + 2] = g["hy_f2_b"][l]
        small64[:, l * 4 + 3] = g["hy_f2_freq"][l]
    shared["small64"] = small64
    gqk = np.zeros((2, 128, 1024), np.float32)
    for l in range(2):
        gqk[l, :, 0:512] = np.tile(g["q_norm_g"][l], 8)[None, :]
        gqk[l, :, 512:1024] = np.tile(g["k_norm_g"][l], 8)[None, :]
    shared["gqk"] = gqk
    gcol = np.zeros((2, 128, 2), np.float32)
    for l in range(2):
        gcol[l, :, 0] = np.tile(g["q_norm_g"][l], 2)
        gcol[l, :, 1] = np.tile(g["k_norm_g"][l], 2)
    shared["gcol"] = gcol
    bias_s = _bias_bank(g["rel_bias"], True)
    bias_p = _bias_bank(g["rel_bias"], False)
    maps = []
    for core in range(8):
        sample = core >= 4
        cst = _consts(sample)
        m = dict(shared)
        small = np.zeros((128, NV), np.float32)
        for l in range(2):
            lo = l * PL
            small[:, lo + C_N1G:lo + C_N1G + 8] = _colvec(g["norm1_g"][l])
            small[:, lo + C_N2G:lo + C_N2G + 8] = _colvec(g["norm2_g"][l])
            small[:, lo + C_BMOD:lo + C_BMOD + 48] = _colvec(g["b_mod"][l])
            small[:, lo + C_PSC:lo + C_PSC + 2] = _colvec(g["pool_scale"][l])
            for tap in range(3):
                small[:, lo + C_CW + tap * 6:lo + C_CW + tap * 6 + 6] = _colvec(g["hy_conv_w"][l, tap])
            small[:, lo + C_CB:lo + C_CB + 6] = _colvec(g["hy_conv_b"][l])
            for o in range(2):
                small[:, lo + C_HB + o * 2:lo + C_HB + o * 2 + 2] = _colvec(g["hy_bias"][l, o])
        small[:, C_EPS] = 1e-6
        small[:, C_SGN] = np.where(np.arange(128) % 2 == 0, 1.0, -1.0)
        small[:, C_TAU0:C_TAU0 + 16] = cst["tau0"]
        if sample:
            b = core - 4
            m["x_in"] = np.ascontiguousarray(g["x_sample"][b])
            small[:, C_COND:C_COND + 8] = _colvec(g["c"][b])
            small[:, C_FLAG] = 0.0
            m["cachek"] = np.ascontiguousarray(g["cache_k"][b].reshape(2, 256, 512))
            cv = np.ones((2, 256, 8, 65), np.float32)
            cv[..., :64] = g["cache_v"][b]
            m["cachev"] = cv.reshape(2, 256, 8 * 65)
            m["biasbank"] = bias_s
        else:
            m["x_in"] = np.ascontiguousarray(g["x_prompt"][core * 8:(core + 1) * 8].reshape(NT, D))
            small[:, C_COND:C_COND + 8] = _colvec(g["c_ctx"])
            small[:, C_FLAG] = 1.0
            m["cachek"] = np.zeros((2, 256, 512), np.float32)
            m["cachev"] = np.zeros((2, 256, 8 * 65), np.float32)
            m["biasbank"] = bias_p
        m["small"] = small
        m["zT"], m["decay"], m["wn"], m["apool"] = cst["zT"], cst["decay"], cst["wn"], cst["apool"]
        m["fwd"], m["inv"] = cst["fwd"], cst["inv"]
        maps.append(m)
    return maps


def assemble(results):
    y_prompt = np.stack([results[c]["y"] for c in range(4)]).reshape(32, 256, D)
    y_sample = np.stack([results[c]["y"] for c in range(4, 8)])
    nk = np.zeros((32, 2, 256, 8, 64), np.float32)
    nv = np.zeros((32, 2, 256, 8, 64), np.float32)
    for c in range(4):
        ko = results[c]["kout"].reshape(2, 8, 256, 8, 64)
        vo = results[c]["vout"].reshape(2, 8, 256, 8, 64)
        nk[c * 8:(c + 1) * 8] = ko.transpose(1, 0, 2, 3, 4)
        nv[c * 8:(c + 1) * 8] = vo.transpose(1, 0, 2, 3, 4)
    return (y_prompt.astype(np.float32), y_sample.astype(np.float32), nk, nv)


def kernel(**inputs):
    nc = _get_nc()
    maps = make_in_maps(inputs)
    res = run_bass_kernel_spmd(nc, maps, core_ids=list(range(8)))
    return assemble(res.results)
```

```python
import numpy as np
import ml_dtypes
import concourse.bass as bass
import concourse.mybir as mybir
from concourse.bass_utils import run_bass_kernel_spmd

F32 = mybir.dt.float32
BF16 = mybir.dt.bfloat16
ALU = mybir.AluOpType
AF = mybir.ActivationFunctionType
AX = mybir.AxisListType
NPBF = ml_dtypes.bfloat16

D = 1024
NT = 2048
NTILE = 16
NCH = 8
L_DEPTH = 2
IN_W = 2560
DFF = 4096
NEG = -30000.0
PI_C = 3.14159
MAGIC = 12582912.0
TWO_PI = 6.283185307179586

C_N1G, C_N2G, C_BMOD, C_PSC, C_CW, C_CB, C_HB = 0, 8, 16, 64, 66, 84, 90
PL = 94
C_COND = 2 * PL
C_FLAG = C_COND + 8
C_EPS = C_FLAG + 1
C_TAU0 = C_EPS + 1
C_SGN = C_TAU0 + 16
NV = C_SGN + 1
NV64 = 8


class Sched:
    NDMA_SEM = 8

    def __init__(self, nc):
        self.nc = nc
        self.eng = {"pe": nc.tensor, "act": nc.scalar, "dve": nc.vector, "pool": nc.gpsimd,
                    "sp": nc.sync}
        self.sem, self.cnt = {}, {}
        for e in ("pe", "act", "dve", "pool"):
            self.sem[e] = nc.alloc_semaphore(name=f"s_{e}")
            self.cnt[e] = 0
        self.dsem, self.dcnt = {}, {}
        for q in ("sp", "pool"):
            self.dsem[q] = [nc.alloc_semaphore(name=f"d_{q}{i}") for i in range(self.NDMA_SEM)]
            self.dcnt[q] = 0
        self.known = {e: {} for e in ("pe", "act", "dve", "pool", "sp")}
        self.last_w, self.readers = {}, {}

    def _tok_wait(self, tok):
        if tok[0] == "c":
            return ("c", tok[1]), self.sem[tok[1]], tok[2]
        q, m = tok[1], tok[2]
        r, j = m % self.NDMA_SEM, m // self.NDMA_SEM
        return ("d", q, r), self.dsem[q][r], 16 * (j + 1)

    @staticmethod
    def _excl(reads, writes):
        return list(writes) + [r for r in reads if r.startswith("ps")]

    def _deps(self, reads, writes):
        writes = self._excl(reads, writes)
        deps = []
        for r in reads:
            t = self.last_w.get(r)
            if t is not None:
                deps.append(t)
        for r in writes:
            t = self.last_w.get(r)
            if t is not None:
                deps.append(t)
            deps.extend(self.readers.get(r, ()))
        return deps

    def _emit_waits(self, waiter, deps, self_n=None):
        h = self.eng[waiter]
        best = {}
        for tok in deps:
            if tok[0] == "c" and tok[1] == waiter:
                if waiter == "pe":
                    continue
                if self_n is not None and tok[2] < self_n - 2:
                    continue
            key, sem, val = self._tok_wait(tok)
            if self.known[waiter].get(key, 0) >= val:
                continue
            if key not in best or best[key][1] < val:
                best[key] = (sem, val)
        for key, (sem, val) in best.items():
            h.wait_ge(sem, val)
            self.known[waiter][key] = val

    def _record(self, tok, reads, writes):
        writes = self._excl(reads, writes)
        for r in reads:
            self.readers.setdefault(r, []).append(tok)
        for r in writes:
            self.last_w[r] = tok
            self.readers[r] = []

    def op(self, eng, fn, reads=(), writes=()):
        reads, writes = list(reads), list(writes)
        deps = self._deps(reads, writes)
        n = self.cnt[eng] + 1
        self._emit_waits(eng, deps, self_n=n)
        ins = fn(self.eng[eng])
        ins.then_inc(self.sem[eng], 1)
        self.cnt[eng] = n
        self._record(("c", eng, n), reads, writes)

    def dma(self, q, out, in_, reads=(), writes=()):
        reads, writes = list(reads), list(writes)
        deps = self._deps(reads, writes)
        m = self.dcnt[q]
        if m >= self.NDMA_SEM:
            deps.append(("d", q, m - self.NDMA_SEM))
        self._emit_waits(q, deps)
        r = m % self.NDMA_SEM
        self.eng[q].dma_start(out=out, in_=in_).then_inc(self.dsem[q][r], 16)
        self.dcnt[q] = m + 1
        self._record(("d", q, m), reads, writes)

    def barrier(self):
        toks = []
        for e in ("pe", "act", "dve", "pool"):
            if self.cnt[e] > 0:
                toks.append(("c", e, self.cnt[e]))
        for q in ("sp", "pool"):
            for m in range(max(0, self.dcnt[q] - self.NDMA_SEM), self.dcnt[q]):
                toks.append(("d", q, m))
        for w in ("pe", "act", "dve", "pool", "sp"):
            h = self.eng[w]
            for tok in toks:
                if tok[0] == "c" and tok[1] == w:
                    continue
                key, sem, val = self._tok_wait(tok)
                if self.known[w].get(key, 0) >= val:
                    continue
                h.wait_ge(sem, val)
                self.known[w][key] = val
        self.last_w.clear()
        self.readers.clear()


class Arena:
    BASE, TOP = 16512, 229344

    def __init__(self, nc):
        self.nc = nc
        self.persist = self.BASE
        self.cur = self.BASE
        self.n = 0

    def _alloc(self, name, shape, dt, off):
        self.n += 1
        return self.nc.alloc_sbuf_tensor_at(f"{name}_{self.n}", list(shape), dt, offset=off)

    @staticmethod
    def _bytes(shape, dt):
        n = 1
        for s in shape[1:]:
            n *= s
        return (n * (2 if dt == BF16 else 4) + 31) // 32 * 32

    def p(self, name, shape, dt):
        assert self.cur == self.persist, "persistent alloc after scratch"
        t = self._alloc(name, shape, dt, self.persist)
        self.persist += self._bytes(shape, dt)
        self.cur = self.persist
        assert self.persist <= self.TOP
        return t

    def s(self, name, shape, dt):
        t = self._alloc(name, shape, dt, self.cur)
        self.cur += self._bytes(shape, dt)
        assert self.cur <= self.TOP, f"SBUF overflow at {name}: {self.cur}"
        return t

    def reset(self, to=None):
        self.cur = self.persist if to is None else to

    def at(self, name, shape, dt, off):
        return self._alloc(name, shape, dt, off)

    def mark(self):
        return self.cur


def pid_attn(i):
    return {0: 0, 1: 1, 14: 4, 15: 5}.get(i, 2 if i % 2 == 0 else 3)


def pid_pool(j):
    return 0 if j == 0 else (3 if j == 15 else (1 if j % 2 == 0 else 2))


def start_attn(i):
    return min(max(i - 2, 0), 11)


def build_nc(stop=None, debug=False):
    nc = bass.Bass("TRN2", target_bir_lowering=False)

    def din(name, shape, dt=F32):
        return nc.dram_tensor(name, list(shape), dt, kind="ExternalInput").ap()

    x_in = din("x_in", [NT, D])
    w_mod = din("w_mod", [2, D, 6 * D])
    w_in = din("w_in", [2, D, IN_W])
    w_out = din("w_out", [2, D, D])
    w_up = din("w_up", [2, D, DFF])
    w_down = din("w_down", [2, DFF, D])
    pool_w = din("pool_w", [2, 4, 64, 64])
    f1_w = din("f1_w", [2, 33, 64])
    f2_w = din("f2_w", [2, 64, 64])
    f3_w = din("f3_w", [2, 64, 1024])
    small = din("small", [128, NV])
    small64 = din("small64", [64, NV64])
    gqk = din("gqk", [2, 128, 1024])
    gcol = din("gcol", [2, 128, 2])
    cachek = din("cachek", [2, 256, 512])
    cachev = din("cachev", [2, 256, 8 * 65])
    zT_d = din("zT", [33, NT])
    decay_d = din("decay", [128, 16, 256])
    wn_d = din("wn", [128, 16, 128])
    apool_d = din("apool", [128, 4 * 3 * 4 * 128], BF16)
    bias_d = din("biasbank", [2, 8, 128, 30 * 128], BF16)
    fwd_d = din("fwd", [16, 128, 16 * 128], BF16)
    inv_d = din("inv", [2, 16, 128, 1024], BF16)
    identb_d = din("identb", [128, 128], BF16)
    identf_d = din("identf", [128, 128])
    onesm_d = din("onesm", [128, 128], BF16)

    y_out = nc.dram_tensor("y", [NT, D], F32, kind="ExternalOutput").ap()
    k_out = nc.dram_tensor("kout", [2, NT, 512], F32, kind="ExternalOutput").ap()
    v_out = nc.dram_tensor("vout", [2, NT, 512], F32, kind="ExternalOutput").ap()
    khat = nc.dram_tensor("khat", [2, 32, 128, 512], F32, kind="Internal").ap()

    S = Sched(nc)
    A = Arena(nc)

    psS0 = nc.alloc_psum_tensor("psS0", [128, 1024], F32)
    psS1 = nc.alloc_psum_tensor("psS1", [128, 1024], F32)
    psO = nc.alloc_psum_tensor("psO", [128, 512], F32)
    psG0 = nc.alloc_psum_tensor("psG0", [128, 512], F32)
    psG1 = nc.alloc_psum_tensor("psG1", [128, 512], F32)
    psT = nc.alloc_psum_tensor("psT", [128, 1024], BF16)
    psT_f32 = psS1
    BANKS = [(psG0[:, :], "psG0"), (psG1[:, :], "psG1"), (psO[:, :], "psO"),
             (psS0[:, 0:512], "psS0.0"), (psS0[:, 512:1024], "psS0.1"),
             (psS1[:, 0:512], "psS1.0"), (psS1[:, 512:1024], "psS1.1")]

    smallt = A.p("small", [128, NV], F32)
    small64t = A.p("small64", [64, NV64], F32)
    identb = A.p("identb", [128, 128], BF16)
    identf = A.p("identf", [128, 128], F32)
    onesm = A.p("onesm", [128, 128], BF16)
    modv = A.p("modv", [128, 2, 48], F32)
    modA = A.p("modA", [128, 2, 16], F32)
    nw = A.p("nw", [128, 2, 12], F32)
    PRE_X = A.mark()
    xT = A.p("xT", [128, NCH, NT], F32)
    HT_OFF = A.mark()
    hT = A.p("hT", [128, NCH, NT], BF16)
    WN = A.p("wnext", [128, 8, 768], BF16)

    def prefetch(kind, l, hh=0):
        if kind == "pool":
            load_w(WN[:, :, 0:256], w_in[l], (0, D), (0, 256), "wnext")
        elif kind == "attn":
            for part in range(3):
                c0 = 1024 + part * 512 + hh * 256
                load_w(WN[:, :, part * 256:(part + 1) * 256], w_in[l], (0, D), (c0, c0 + 256), "wnext")
        elif kind == "hyena":
            load_w(WN[:, :, :], w_in[l], (0, D), (256, 1024), "wnext")
        elif kind == "mlp":
            load_w(WN[:, :, 0:512], w_up[l], (0, D), (0, 512), "wnext")

    def sm(col, n=1):
        return smallt[:, col:col + n]

    S.dma("sp", smallt[:, :], small, writes=["small"])
    S.dma("sp", small64t[:, :], small64, writes=["small64"])
    S.dma("sp", identb[:, :], identb_d, writes=["identb"])
    S.dma("sp", identf[:, :], identf_d, writes=["identf"])
    S.dma("sp", onesm[:, :], onesm_d, writes=["onesm"])
    CONSTS = ["small", "small64", "identb", "identf", "onesm"]
    eps_ap = sm(C_EPS)

    def copy_op(eng, out, in_, reads, writes):
        if eng == "act":
            S.op("act", lambda e: e.copy(out=out, in_=in_), reads, writes)
        elif eng == "dve":
            S.op("dve", lambda e: e.tensor_copy(out=out, in_=in_), reads, writes)
        else:
            S.op("pool", lambda e: e.tensor_copy(out=out, in_=in_), reads, writes)

    def load_w(dst, src_ap, rows, cols, wname):
        r0, r1 = rows
        c0, c1 = cols
        src = src_ap[r0:r1, c0:c1].rearrange("(k p) c -> p k c", p=128)
        S.dma("pool", dst, src, writes=[wname])

    def prologue(l):
        A.reset(PRE_X)
        lo = l * PL
        silu_c = A.s("silu_c", [128, 8], BF16)
        sig = A.s("sig", [128, 8], F32)
        wslabs = [A.s(f"wm{i}", [128, 8, 512], BF16) for i in range(3)]

        def adaln_steps():
            S.op("act", lambda e: e.activation(out=sig[:, :], in_=sm(C_COND, 8), func=AF.Sigmoid),
                 reads=["small"], writes=["sig"])
            S.op("dve", lambda e: e.tensor_tensor(out=silu_c[:, :], in0=sig[:, :], in1=sm(C_COND, 8),
                                                  op=ALU.mult), reads=["sig", "small"],
                 writes=["silu_c"])
            pm = psT_f32
            for sl in range(3):
                load_w(wslabs[sl][:, :, :], w_mod[l], (0, D), (sl * 512, (sl + 1) * 512), f"wm{sl}")
            yield
            for sl in range(12):
                wt = wslabs[sl % 3]
                wn_ = f"wm{sl % 3}"

                def mm(e, sl=sl, wt=wt):
                    ins = None
                    for jc in range(4):
                        j = sl * 4 + jc
                        for k in range(8):
                            ins = e.matmul(pm[:, j:j + 1], lhsT=wt[:, k, jc * 128:(jc + 1) * 128],
                                           rhs=silu_c[:, k:k + 1], start=(k == 0), stop=(k == 7))
                    return ins
                S.op("pe", mm, reads=[wn_, "silu_c"], writes=["psS1.0"])
                if sl + 3 < 12:
                    load_w(wt[:, :, :], w_mod[l], (0, D), ((sl + 3) * 512, (sl + 4) * 512), wn_)
                yield
            S.op("dve", lambda e: e.tensor_tensor(out=modv[:, l, :], in0=pm[:, 0:48],
                                                  in1=sm(lo + C_BMOD, 48), op=ALU.add),
                 reads=["psS1.0", "small"], writes=["modv"])
            for which, (cg, cs) in enumerate(((C_N1G, 8), (C_N2G, 32))):
                S.op("dve", lambda e, which=which, cg=cg, cs=cs: e.scalar_tensor_tensor(
                    out=modA[:, l, which * 8:(which + 1) * 8], in0=modv[:, l, cs:cs + 8], scalar=1.0,
                    in1=sm(lo + cg, 8), op0=ALU.add, op1=ALU.mult),
                    reads=["modv", "small"], writes=["modA"])
            for which, tap in enumerate((0, 2)):
                S.op("dve", lambda e, which=which, tap=tap: e.tensor_scalar(
                    out=nw[:, l, which * 6:(which + 1) * 6], in0=sm(lo + C_CW + tap * 6, 6),
                    scalar1=sm(C_FLAG), scalar2=-1.0, op0=ALU.mult, op1=ALU.mult),
                    reads=["small"], writes=["nw"])
            yield

        ada = adaln_steps()

        def ada_step():
            try:
                next(ada)
            except StopIteration:
                pass

        ada_step()
        h2 = A.s("h2", [64, NT], BF16)
        w1 = A.s("w1", [33, 64], F32)
        w2 = A.s("w2", [64, 64], F32)
        w3 = A.s("w3", [64, 1024], BF16)
        decay = A.s("decay", [128, 16, 256], F32)
        wn = A.s("wn", [128, 16, 128], BF16)
        fb = A.s("fb", [64, 2], F32)
        ovl = A.mark()
        zT = A.s("zT", [33, NT], F32)
        pre = A.s("pre", [64, NT], F32)
        tmp = A.s("tmp", [64, NT], F32)
        h1 = A.s("h1", [64, NT], F32)
        S.dma("sp", zT[:, :], zT_d, writes=["zT"])
        S.dma("sp", w1[:, :], f1_w[l], writes=["w1"])
        S.dma("sp", w2[:, :], f2_w[l], writes=["w2"])
        S.dma("pool", w3[:, :], f3_w[l], writes=["w3"])
        S.dma("sp", decay[:, :, :], decay_d, writes=["decay"])
        S.dma("pool", wn[:, :, :], wn_d, writes=["wn"])
        for li in range(2):
            S.op("dve", lambda e, li=li: e.tensor_tensor(
                out=fb[:, li:li + 1], in0=small64t[:, l * 4 + 2 * li:l * 4 + 2 * li + 1],
                in1=small64t[:, l * 4 + 2 * li + 1:l * 4 + 2 * li + 2], op=ALU.mult),
                reads=["small64"], writes=["fb"])

        def sine_layer(li, wmat, kdim, src, dst, srcname, dstname):
            for b in range(4):
                bank, bname = BANKS[b % 2]
                S.op("pe", lambda e, b=b, bank=bank: e.matmul(
                    bank[0:64, :], lhsT=wmat[0:kdim, :], rhs=src[0:kdim, b * 512:(b + 1) * 512],
                    start=True, stop=True), reads=[srcname, f"w{li + 1}"], writes=[bname])
                S.op("dve", lambda e, b=b, bank=bank: e.tensor_scalar(
                    out=pre[:, b * 512:(b + 1) * 512], in0=bank[0:64, :],
                    scalar1=small64t[:, l * 4 + 2 * li + 1:l * 4 + 2 * li + 2],
                    scalar2=fb[:, li:li + 1], op0=ALU.mult, op1=ALU.add),
                    reads=[bname, "small64", "fb"], writes=["pre"])
            S.op("dve", lambda e: e.tensor_scalar(out=tmp[:, :], in0=pre[:, :], scalar1=1.0 / TWO_PI,
                                                  scalar2=MAGIC, op0=ALU.mult, op1=ALU.add),
                 reads=["pre"], writes=["tmp"])
            S.op("dve", lambda e: e.tensor_scalar(out=tmp[:, :], in0=tmp[:, :], scalar1=MAGIC,
                                                  scalar2=-TWO_PI, op0=ALU.subtract, op1=ALU.mult),
                 reads=["tmp"], writes=["tmp"])
            S.op("dve", lambda e: e.tensor_tensor(out=tmp[:, :], in0=tmp[:, :], in1=pre[:, :],
                                                  op=ALU.add), reads=["tmp", "pre"], writes=["tmp"])
            S.op("dve", lambda e: e.tensor_scalar(out=tmp[:, :], in0=tmp[:, :], scalar1=PI_C,
                                                  scalar2=-PI_C, op0=ALU.min, op1=ALU.max),
                 reads=["tmp"], writes=["tmp"])
            S.op("act", lambda e: e.activation(out=dst[:, :], in_=tmp[:, :], func=AF.Sin),
                 reads=["tmp"], writes=[dstname])

        sine_layer(0, w1, 33, zT, h1, "zT", "h1")
        ada_step()
        sine_layer(1, w2, 64, h1, h2, "h1", "h2")
        ada_step()
        ada_step()

        A.reset(ovl)
        a_t = A.s("a_t", [128, 16, 512], BF16)
        d_t = A.s("d_t", [128, 16, 512], BF16)
        KD_OFF = A.mark()
        kd = A.s("kd", [128, 16, 1024], BF16)
        absk = [A.s(f"absk{i}", [128, 1024], BF16) for i in range(2)]
        def kd_mm(t):
            for cb in range(2):
                bank, bname = BANKS[cb] if t % 2 == 0 else BANKS[4 + 2 * cb]
                S.op("pe", lambda e, cb=cb, bank=bank: e.matmul(
                    bank, lhsT=h2[:, t * 128:(t + 1) * 128], rhs=w3[:, cb * 512:(cb + 1) * 512],
                    start=True, stop=True), reads=["h2", "w3"], writes=[bname])

        def kd_ew(t):
            for cb in range(2):
                bank, bname = BANKS[cb] if t % 2 == 0 else BANKS[4 + 2 * cb]
                dcy = decay[:, t, :]
                dcy_b = bass.AP(tensor=dcy.tensor, offset=dcy.offset,
                                ap=[[dcy.ap[0][0], 128], [0, 2], [1, 256]])
                S.op("dve", lambda e, cb=cb, bank=bank, dcy_b=dcy_b: e.tensor_tensor(
                    out=kd[:, t, cb * 512:(cb + 1) * 512].rearrange("p (h c) -> p h c", h=2),
                    in0=bank.rearrange("p (h c) -> p h c", h=2), in1=dcy_b, op=ALU.mult),
                    reads=[bname, "decay"], writes=[f"kd{t}"])
            ak = absk[t % 2]
            S.op("act", lambda e, ak=ak: e.activation(out=ak[:, :], in_=kd[:, t, :], func=AF.Abs),
                 reads=[f"kd{t}"], writes=[f"absk{t % 2}"])

        def kd_norm(t):
            ak = absk[t % 2]
            for cb in range(2):
                bank, bname = BANKS[2 + cb]
                S.op("pe", lambda e, cb=cb, bank=bank, ak=ak: e.matmul(
                    bank, lhsT=wn[:, t, :], rhs=ak[:, cb * 512:(cb + 1) * 512],
                    start=(t == 0), stop=(t == 15)), reads=[f"absk{t % 2}", "wn"], writes=[bname])

        kd_mm(0)
        for t in range(16):
            if t + 1 < 16:
                kd_mm(t + 1)
            kd_ew(t)
            if t >= 1:
                kd_norm(t - 1)
            if t % 4 == 3:
                ada_step()
        kd_norm(15)
        rnf = A.s("rnf", [128, 1024], F32)
        rn = A.s("rn", [128, 1024], BF16)
        for cb in range(2):
            bank, bname = BANKS[2 + cb]
            S.op("dve", lambda e, cb=cb, bank=bank: e.tensor_scalar(
                out=rnf[:, cb * 512:(cb + 1) * 512], in0=bank, scalar1=1e-6, scalar2=None,
                op0=ALU.add), reads=[bname], writes=["rnf"])
        S.op("dve", lambda e: e.reciprocal(out=rnf[:, :], in_=rnf[:, :]), reads=["rnf"], writes=["rnf"])
        copy_op("act", rn[:, :], rnf[:, :], ["rnf"], ["rn"])
        t1 = [A.s(f"t1{i}", [128, 512], BF16) for i in range(2)]
        t2 = [A.s(f"t2{i}", [128, 512], BF16) for i in range(2)]
        for t in range(16):
            u1, u2 = t1[t % 2], t2[t % 2]
            n1, n2 = f"t1{t % 2}", f"t2{t % 2}"
            S.op("dve", lambda e, t=t, u1=u1: e.tensor_tensor(out=u1[:, :], in0=kd[:, t, 0:512],
                                                             in1=rn[:, 0:512], op=ALU.mult),
                 reads=[f"kd{t}", "rn"], writes=[n1])
            S.op("dve", lambda e, t=t, u2=u2: e.scalar_tensor_tensor(
                out=u2[:, :], in0=kd[:, t, 512:1024], scalar=sm(C_TAU0 + t), in1=rn[:, 512:1024],
                op0=ALU.mult, op1=ALU.mult), reads=[f"kd{t}", "rn", "small"], writes=[n2])
            S.op("dve", lambda e, t=t, u1=u1, u2=u2: e.tensor_tensor(
                out=a_t[:, t, :], in0=u1[:, :], in1=u2[:, :], op=ALU.add), reads=[n1, n2],
                writes=["a_t"])
            S.op("dve", lambda e, t=t, u1=u1, u2=u2: e.tensor_tensor(
                out=d_t[:, t, :], in0=u1[:, :], in1=u2[:, :], op=ALU.subtract), reads=[n1, n2],
                writes=["d_t"])
            if t % 4 == 3:
                ada_step()
        NR = 6
        ring = [A.s(f"fw{i}", [128, 16, 128], BF16) for i in range(NR)]
        kst = [A.s(f"kst{i}", [128, 512], F32) for i in range(4)]
        est = [A.s(f"est{i}", [128, 512], F32) for i in range(2)]

        def load_fw(jt):
            S.dma("sp", ring[jt % NR][:, :, :], fwd_d[jt].rearrange("p (s f) -> p s f", s=16),
                  writes=[f"fw{jt % NR}"])
        for jt in range(NR):
            load_fw(jt)
        for jt in range(16):
            ri, j = jt // 8, jt % 8
            fw = ring[jt % NR]
            fn_ = f"fw{jt % NR}"
            (bE, nE), (bO, nO) = (BANKS[0], BANKS[1]) if jt % 2 == 0 else (BANKS[3], BANKS[4])
            src, sname = (a_t, "a_t") if ri == 0 else (d_t, "d_t")
            for half, (bank, bname) in enumerate(((bE, nE), (bO, nO))):
                def mm(e, fw=fw, bank=bank, src=src, half=half):
                    ins = None
                    for s_ in range(8):
                        ins = e.matmul(bank, lhsT=fw[:, half * 8 + s_, :], rhs=src[:, half * 8 + s_, :],
                                       start=(s_ == 0), stop=(s_ == 7))
                    return ins
                S.op("pe", mm, reads=[fn_, sname], writes=[bname])
            if jt + NR < 16:
                load_fw(jt + NR)
            es, esn = est[jt % 2], f"est{jt % 2}"
            copy_op("act", es[:, :], bE, [nE], [esn])
            kb_, kbn_ = kst[(jt % 2) * 2], f"kst{(jt % 2) * 2}"
            km_, kmn_ = kst[(jt % 2) * 2 + 1], f"kst{(jt % 2) * 2 + 1}"
            S.op("dve", lambda e, bO=bO, es=es, kb_=kb_: e.tensor_tensor(
                out=kb_[:, :], in0=bO, in1=es[:, :], op=ALU.add), reads=[nO, esn], writes=[kbn_])
            S.op("dve", lambda e, bO=bO, es=es, km_=km_: e.scalar_tensor_tensor(
                out=km_[:, :], in0=bO, scalar=-1.0, in1=es[:, :], op0=ALU.mult, op1=ALU.add),
                reads=[nO, esn], writes=[kmn_])
            S.dma("sp", khat[l, ri * 16 + j], kb_[:, :], reads=[kbn_], writes=["khat"])
            S.dma("sp", khat[l, ri * 16 + 8 + j], km_[:, :], reads=[kmn_], writes=["khat"])
            ada_step()
        for _ in range(20):
            ada_step()
        S.barrier()

    for l in range(2):
        prologue(l)
    A.reset()

    prefetch("pool", 0)
    xst = [A.s(f"xst{i}", [128, D], F32) for i in range(2)]
    for i in range(NTILE):
        st = xst[i % 2]
        sn = f"xst{i % 2}"
        S.dma("sp", st[:, :], x_in[i * 128:(i + 1) * 128, :], writes=[sn])
        for hb in range(2):
            bank, bname = BANKS[hb]

            def tr(e, hb=hb, bank=bank, st=st):
                ins = None
                for kk in range(4):
                    k = hb * 4 + kk
                    ins = e.transpose(out=bank[:, kk * 128:(kk + 1) * 128],
                                      in_=st[:, k * 128:(k + 1) * 128], identity=identf[:, :])
                return ins
            S.op("pe", tr, reads=[sn, "identf"], writes=[bname])
            copy_op("act" if hb == 0 else "dve",
                    xT[:, hb * 4:(hb + 1) * 4, i * 128:(i + 1) * 128],
                    bank.rearrange("p (k t) -> p k t", k=4), [bname], [f"xT{i // 4}"])
    S.barrier()

    def HTN(b):
        return [f"hT{b}.{k}" for k in range(8)]

    def rmsnorm(l, which):
        A.reset()
        rstd_l = [A.s(f"rstd{i}", [128, 512], F32) for i in range(2)]
        lnv_l = [A.s(f"lnv{i}", [128, 512], F32) for i in range(2)]
        tmpn = [A.s(f"tmpn{i}", [128, 512], F32) for i in range(4)]
        cB = 0 if which == 0 else 24
        nt_ = [0]

        def stage1(b):
            bs = slice(b * 512, (b + 1) * 512)
            for k in range(8):
                if k % 2 == 0:
                    S.op("act", lambda e, k=k: e.activation(out=hT[:, k, bs], in_=xT[:, k, bs],
                                                            func=AF.Square),
                         reads=[f"xT{b}"], writes=[f"hT{b}.{k}"])
                else:
                    S.op("dve", lambda e, k=k: e.tensor_tensor(out=hT[:, k, bs], in0=xT[:, k, bs],
                                                               in1=xT[:, k, bs], op=ALU.mult),
                         reads=[f"xT{b}"], writes=[f"hT{b}.{k}"])
            bank, bname = BANKS[b % 2]

            def mm(e):
                ins = None
                for k in range(8):
                    ins = e.matmul(bank, lhsT=onesm[:, :], rhs=hT[:, k, bs], start=(k == 0),
                                   stop=(k == 7))
                return ins
            S.op("pe", mm, reads=HTN(b) + ["onesm"], writes=[bname])

        def stage2(b):
            bs = slice(b * 512, (b + 1) * 512)
            rstd, lnv = rstd_l[b % 2], lnv_l[b % 2]
            Nr, Nl = f"rstd{b % 2}", f"lnv{b % 2}"
            bank, bname = BANKS[b % 2]
            S.op("act", lambda e: e.activation(out=lnv[:, :], in_=bank, func=AF.Ln, bias=eps_ap),
                 reads=[bname, "small"], writes=[Nl])
            S.op("act", lambda e: e.activation(out=rstd[:, :], in_=lnv[:, :], func=AF.Exp, scale=-0.5),
                 reads=[Nl], writes=[Nr])
            for k in range(8):
                tn = tmpn[nt_[0] % 4]
                tnn = f"tmpn{nt_[0] % 4}"
                nt_[0] += 1
                S.op("dve", lambda e, k=k, tn=tn: e.scalar_tensor_tensor(
                    out=tn[:, :], in0=xT[:, k, bs], scalar=modA[:, l, which * 8 + k:which * 8 + k + 1],
                    in1=rstd[:, :], op0=ALU.mult, op1=ALU.mult),
                    reads=[f"xT{b}", "modA", Nr], writes=[tnn])
                S.op("act", lambda e, k=k, tn=tn: e.activation(
                    out=hT[:, k, bs], in_=tn[:, :], func=AF.Identity,
                    bias=modv[:, l, cB + k:cB + k + 1]), reads=[tnn, "modv"],
                    writes=[f"hT{b}.{k}"])

        stage1(0)
        for b in range(4):
            if b + 1 < 4:
                stage1(b + 1)
            stage2(b)
        S.barrier()

    def wout_load(l, row0, nk):
        wo = A.s("wo", [128, nk, D], BF16)
        load_w(wo[:, :, :], w_out[l], (row0, row0 + nk * 128), (0, D), "wo")
        return wo

    def wout_partial(l, yT, yname, row0, nk, gate_col, wo=None, rhs_fn=None, ynames=None):
        if wo is None:
            wo = wout_load(l, row0, nk)
        if rhs_fn is None:
            rhs_fn = lambda kc, b: yT[:, kc, b * 512:(b + 1) * 512]
        n = 0
        for dc in range(8):
            for b in range(4):
                bs = slice(b * 512, (b + 1) * 512)
                bank, bname = BANKS[n % 4]
                n += 1

                def mm(e, dc=dc, b=b, bank=bank):
                    ins = None
                    for kc in range(nk):
                        ins = e.matmul(bank, lhsT=wo[:, kc, dc * 128:(dc + 1) * 128], rhs=rhs_fn(kc, b),
                                       start=(kc == 0), stop=(kc == nk - 1))
                    return ins
                S.op("pe", mm, reads=["wo"] + (ynames(b) if ynames else [yname]), writes=[bname])
                S.op("dve", lambda e, dc=dc, bs=bs, bank=bank: e.scalar_tensor_tensor(
                    out=xT[:, dc, bs], in0=bank, scalar=modv[:, l, gate_col + dc:gate_col + dc + 1],
                    in1=xT[:, dc, bs], op0=ALU.mult, op1=ALU.add),
                    reads=[bname, "modv", f"xT{b}"], writes=[f"xT{b}"])

    Y_DONE = [False]

    def store_y_tile(i, st, sn, banks2):
        for hb in range(2):
            bank, bname = banks2[hb]

            def tr(e, hb=hb, bank=bank):
                ins = None
                for kk in range(4):
                    k = hb * 4 + kk
                    ins = e.transpose(out=bank[:, kk * 128:(kk + 1) * 128],
                                      in_=xT[:, k, i * 128:(i + 1) * 128], identity=identf[:, :])
                return ins
            S.op("pe", tr, reads=[f"xT{i // 4}", "identf"], writes=[bname])
            copy_op("act", st[:, hb * 512:(hb + 1) * 512], bank, [bname], [sn])
        S.dma("sp", y_out[i * 128:(i + 1) * 128, :], st[:, :], reads=[sn])

    def pool_phase(l):
        A.reset()
        lo = l * PL
        wp = WN
        wo_pre = wout_load(l, 0, 2)
        apool = A.s("apool", [128, 4, 3, 4, 128], BF16)
        S.dma("sp", apool[:, :, :, :, :],
              apool_d.rearrange("p (a r g t) -> p a r g t", a=4, r=3, g=4), writes=["apool"])
        wblk = A.s("wblk", [128, 2, 128], BF16)
        S.op("dve", lambda e: e.memset(wblk[:, :, :], 0.0), writes=["wblk"])
        for g in range(4):
            gp, gl = g // 2, g % 2
            S.dma("pool", wblk[gl * 64:(gl + 1) * 64, gp, gl * 64:(gl + 1) * 64], pool_w[l, g],
                  writes=["wblk"])
        upool = A.s("upool", [128, 16, 256], BF16)
        for i in range(16):
            bank, bname = BANKS[i % 2]

            def mm(e, i=i, bank=bank):
                ins = None
                for k in range(8):
                    ins = e.matmul(bank[:, 0:256], lhsT=hT[:, k, i * 128:(i + 1) * 128],
                                   rhs=wp[:, k, 0:256], start=(k == 0), stop=(k == 7))
                return ins
            S.op("pe", mm, reads=HTN(i // 4) + ["wnext"], writes=[bname])
            copy_op("act" if i % 2 == 0 else "dve", upool[:, i, :], bank[:, 0:256], [bname],
                    [f"upool{i}"])
        prefetch("attn", l, 0)
        pooledT = A.s("pooledT", [128, 2, NT], BF16)
        n = 0
        for b in range(4):
            for gp in range(2):
                for gl in range(2):
                    g = gp * 2 + gl
                    bank, bname = BANKS[n % 4]
                    n += 1

                    def mm(e, b=b, gp=gp, g=g, bank=bank):
                        ins = None
                        for jl in range(4):
                            j = b * 4 + jl
                            rs = [r for r in (-1, 0, 1) if 0 <= j + r < 16]
                            for r in rs:
                                ins = e.matmul(bank[:, jl * 128:(jl + 1) * 128],
                                               lhsT=upool[:, j + r, gp * 128:(gp + 1) * 128],
                                               rhs=apool[:, pid_pool(j), r + 1, g, :],
                                               start=(r == rs[0]), stop=(r == rs[-1]))
                        return ins
                    rd = [f"upool{j}" for j in range(max(0, b * 4 - 1), min(16, b * 4 + 5))]
                    S.op("pe", mm, reads=rd + ["apool"], writes=[bname])
                    copy_op("act" if gl == 0 else "dve",
                            pooledT[gl * 64:(gl + 1) * 64, gp, b * 512:(b + 1) * 512],
                            bank[gl * 64:(gl + 1) * 64, :], [bname], [f"pooledT{b}.{gp}.{gl}"])
        ypT = A.s("ypT", [128, 2, NT], BF16)
        for b in range(4):
            for gp in range(2):
                bank, bname = BANKS[(b * 2 + gp) % 4]
                S.op("pe", lambda e, b=b, gp=gp, bank=bank: e.matmul(
                    bank, lhsT=wblk[:, gp, :], rhs=pooledT[:, gp, b * 512:(b + 1) * 512],
                    start=True, stop=True),
                    reads=["wblk", f"pooledT{b}.{gp}.0", f"pooledT{b}.{gp}.1"], writes=[bname])
                S.op("act", lambda e, b=b, gp=gp, bank=bank: e.activation(
                    out=ypT[:, gp, b * 512:(b + 1) * 512], in_=bank, func=AF.Identity,
                    scale=sm(lo + C_PSC + gp)), reads=[bname, "small"], writes=["ypT"])
        wout_partial(l, ypT, "ypT", 0, 2, 16, wo=wo_pre)
        S.barrier()

    def attn_phase(l, hh):
        A.reset()
        QT = A.s("QT", [128, 2, NT], BF16)
        KT = A.s("KT", [128, 2, NT], BF16)
        Vaug = A.s("Vaug", [128, 16, 4, 65], BF16)
        kc_tok = A.s("kc_tok", [128, 2, 256], BF16)
        KcT = A.s("KcT", [128, 2, 256], BF16)
        Vc = A.s("Vc", [128, 2, 4, 65], BF16)
        ytok = A.s("ytok", [128, 16, 256], BF16)
        gprod = A.s("gprod", [128, 1], F32)
        gkrow = A.s("gkrow", [128, 256], F32)
        wo_pre = wout_load(l, 512 + hh * 256, 2)
        ebt = [A.s(f"ebt{i}", [128, 30, 128], BF16) for i in range(2)]
        S.dma("sp", ebt[0][:, :, :], bias_d[l, hh * 4].rearrange("p (a k) -> p a k", a=30),
              writes=["ebt0"])
        markB = A.mark()
        wqkv = WN
        S.dma("sp", gkrow[:, :], gqk[l, :, 512:768], writes=["gkrow"])
        gtmp = A.s("gtmp", [128, 2], F32)
        S.dma("sp", gtmp[:, :], gcol[l], writes=["gtmp"])
        S.op("dve", lambda e: e.tensor_tensor(out=gprod[:, :], in0=gtmp[:, 0:1], in1=gtmp[:, 1:2],
                                              op=ALU.mult), reads=["gtmp"], writes=["gprod"])
        S.op("dve", lambda e: e.memset(Vaug[:, :, :, 64:65], 1.0), writes=["Vones"])
        S.dma("pool", kc_tok[:, :, :],
              cachek[l].rearrange("(t p) c -> p t c", p=128)[:, :, hh * 256:(hh + 1) * 256],
              writes=["kc_tok"])
        S.dma("pool", Vc[:, :, :, :],
              cachev[l].rearrange("(t p) (h c) -> p t h c", p=128, c=65)[:, :, hh * 4:(hh + 1) * 4, :],
              writes=["Vc"])

        qkf_l = [A.s(f"qkf{i}", [128, 512], F32) for i in range(2)]
        sq_l = [A.s(f"sq{i}", [128, 512], F32) for i in range(2)]
        ss_l = [A.s(f"ss{i}", [128, 8], F32) for i in range(2)]
        rs_l = [A.s(f"rs{i}", [128, 8], F32) for i in range(2)]
        qn = [A.s(f"qn{i}", [128, 512], F32) for i in range(2)]
        qkb_l = [A.s(f"qkb{i}", [128, 512], BF16) for i in range(3)]
        vf = [A.s(f"vf{i}", [128, 256], F32) for i in range(2)]

        def bc64(t):
            a_ = t[:, :]
            return bass.AP(tensor=a_.tensor, offset=a_.offset, ap=[[8, 128], [1, 8], [0, 64]])

        def stage_mm(i):
            ts_ = slice(i * 128, (i + 1) * 128)
            bqk, nqk = BANKS[0] if i % 2 == 0 else BANKS[3]
            bv, nv = BANKS[1] if i % 2 == 0 else BANKS[4]

            def mmqk(e):
                ins = None
                for k in range(8):
                    ins = e.matmul(bqk, lhsT=hT[:, k, ts_], rhs=wqkv[:, k, 0:512], start=(k == 0),
                                   stop=(k == 7))
                return ins
            S.op("pe", mmqk, reads=HTN(i // 4) + ["wnext"], writes=[nqk])

            def mmv(e):
                ins = None
                for k in range(8):
                    ins = e.matmul(bv[:, 0:256], lhsT=hT[:, k, ts_], rhs=wqkv[:, k, 512:768],
                                   start=(k == 0), stop=(k == 7))
                return ins
            S.op("pe", mmv, reads=HTN(i // 4) + ["wnext"], writes=[nv])

        def stage_ew1(i):
            ts_ = slice(i * 128, (i + 1) * 128)
            bqk, nqk = BANKS[0] if i % 2 == 0 else BANKS[3]
            bv, nv = BANKS[1] if i % 2 == 0 else BANKS[4]
            p2 = i % 2
            qkf, sq, ss = qkf_l[p2], sq_l[p2], ss_l[p2]
            Nq, Ns, Nss = f"qkf{p2}", f"sq{p2}", f"ss{p2}"
            vfi = vf[p2]
            copy_op("act", qkf[:, :], bqk, [nqk], [Nq])
            copy_op("act", vfi[:, :], bv[:, 0:256], [nv], [f"vf{p2}"])
            copy_op("dve", Vaug[:, i, :, 0:64], bv[:, 0:256].rearrange("p (h c) -> p h c", h=4),
                    [nv], [f"Vaug{i}"])
            S.dma("sp", v_out[l, ts_, hh * 256:(hh + 1) * 256], vfi[:, :], reads=[f"vf{p2}"])
            S.op("act", lambda e: e.activation(out=sq[:, :], in_=bqk, func=AF.Square),
                 reads=[nqk], writes=[Ns])
            S.op("dve", lambda e: e.tensor_reduce(
                out=ss[:, :], in_=sq[:, :].rearrange("p (h c) -> p h c", h=8), axis=AX.X, op=ALU.add),
                reads=[Ns], writes=[Nss])

        def stage_ew2(i):
            ts_ = slice(i * 128, (i + 1) * 128)
            p2 = i % 2
            qkf, sq, ss, rs_ = qkf_l[p2], sq_l[p2], ss_l[p2], rs_l[p2]
            qkb = qkb_l[i % 3]
            Nq, Ns, Nss, Nrs, Nqb = f"qkf{p2}", f"sq{p2}", f"ss{p2}", f"rs{p2}", f"qkb{i % 3}"
            S.op("act", lambda e: e.activation(out=rs_[:, :], in_=ss[:, :], func=AF.Ln,
                                               scale=1.0 / 64.0, bias=eps_ap),
                 reads=[Nss, "small"], writes=[Nrs])
            S.op("act", lambda e: e.activation(out=rs_[:, :], in_=rs_[:, :], func=AF.Exp, scale=-0.5),
                 reads=[Nrs], writes=[Nrs])
            S.op("dve", lambda e: e.tensor_tensor(
                out=qkb[:, :].rearrange("p (h c) -> p h c", h=8),
                in0=qkf[:, :].rearrange("p (h c) -> p h c", h=8), in1=bc64(rs_), op=ALU.mult),
                reads=[Nq, Nrs], writes=[Nqb])
            qni = qn[p2]
            S.op("pool", lambda e: e.tensor_tensor(
                out=sq[:, 0:256].rearrange("p (h c) -> p h c", h=4),
                in0=qkf[:, 256:512].rearrange("p (h c) -> p h c", h=4),
                in1=bass.AP(tensor=rs_[:, :].tensor, offset=rs_[:, :].offset + 4,
                            ap=[[8, 128], [1, 4], [0, 64]]), op=ALU.mult),
                reads=[Nq, Nrs, Ns], writes=[Ns])
            S.op("pool", lambda e: e.tensor_tensor(out=qni[:, 0:256], in0=sq[:, 0:256], in1=gkrow[:, :],
                                                   op=ALU.mult), reads=[Ns, "gkrow"],
                 writes=[f"qn{p2}"])
            S.dma("sp", k_out[l, ts_, hh * 256:(hh + 1) * 256], qni[:, 0:256], reads=[f"qn{p2}"])

        def stage_tr(i):
            ts_ = slice(i * 128, (i + 1) * 128)
            qkb = qkb_l[i % 3]
            Nqb = f"qkb{i % 3}"

            def trq(e):
                ins = None
                for c4 in range(4):
                    ins = e.transpose(out=psT[:, c4 * 128:(c4 + 1) * 128],
                                      in_=qkb[:, c4 * 128:(c4 + 1) * 128], identity=identb[:, :])
                return ins
            S.op("pe", trq, reads=[Nqb, "identb"], writes=["psT"])
            copy_op("dve", QT[:, :, ts_], psT[:, 0:256].rearrange("p (c t) -> p c t", c=2), ["psT"],
                    [f"QT{i}"])
            S.op("act", lambda e: e.activation(
                out=KT[:, :, ts_], in_=psT[:, 256:512].rearrange("p (c t) -> p c t", c=2),
                func=AF.Identity, scale=gprod[:, 0:1]), reads=["psT", "gprod"], writes=[f"KT{i}"])

        stage_mm(0)
        stage_mm(1)
        def trc(e):
            ins = None
            for t in range(2):
                for c in range(2):
                    ins = e.transpose(out=psT[:, (t * 2 + c) * 128:(t * 2 + c + 1) * 128],
                                      in_=kc_tok[:, t, c * 128:(c + 1) * 128], identity=identb[:, :])
            return ins
        S.op("pe", trc, reads=["kc_tok", "identb"], writes=["psT"])
        for t in range(2):
            S.op("act", lambda e, t=t: e.activation(
                out=KcT[:, :, t * 128:(t + 1) * 128],
                in_=psT[:, t * 256:(t + 1) * 256].rearrange("p (c k) -> p c k", c=2),
                func=AF.Identity, scale=gtmp[:, 0:1]), reads=["psT", "gtmp"], writes=["KcT"])

        stage_ew1(0)
        for i in range(16):
            if i + 2 < 16:
                stage_mm(i + 2)
            if i + 1 < 16:
                stage_ew1(i + 1)
            stage_ew2(i)
            if i >= 1:
                stage_tr(i - 1)
        stage_tr(15)
        S.op("act", lambda e: e.activation(out=ebt[0][:, :, :], in_=ebt[0][:, :, :], func=AF.Exp),
             reads=["ebt0"], writes=["ebt0"])
        S.barrier()

        A.reset(markB)
        if hh == 0:
            prefetch("attn", l, 1)
        else:
            prefetch("hyena", l)
        PT = A.s("PT", [128, 16, 5, 128], BF16)
        PTc = [A.s(f"PTc{i}", [128, 2, NT], BF16) for i in range(2)]
        rcp = [A.s(f"rcp{i}", [128, 1], F32) for i in range(2)]
        psS = [(psS0, ["psS0.0", "psS0.1"]), (psS1, ["psS1.0", "psS1.1"])]
        psOs = [(psO, "psO"), (psG1, "psG1")]
        users = [[i for i in range(16) if start_attn(i) <= j <= start_attn(i) + 4] for j in range(16)]
        done_at = [[i for i in range(16) if start_attn(i) + 4 == j] for j in range(16)]
        PT_ap = PT[:, :, :, :]

        def pt_out(i0, n, j):
            o0 = i0 * 640 + (j - start_attn(i0)) * 128
            if n == 1:
                stride = 640
            else:
                o1 = (i0 + 1) * 640 + (j - start_attn(i0 + 1)) * 128
                stride = o1 - o0
            return bass.AP(tensor=PT_ap.tensor, offset=PT_ap.offset + o0,
                           ap=[[PT_ap.ap[0][0], 128], [stride, n], [1, 128]])

        def runs(us, j):
            out, cur = [], [us[0]]
            for i in us[1:]:
                d_new = (i * 640 + (j - start_attn(i)) * 128) - (cur[-1] * 640 + (j - start_attn(cur[-1])) * 128)
                if len(cur) >= 2:
                    d_old = (cur[1] * 640 + (j - start_attn(cur[1])) * 128) - (cur[0] * 640 + (j - start_attn(cur[0])) * 128)
                    if d_new != d_old:
                        out.append(cur)
                        cur = [i]
                        continue
                cur.append(i)
            out.append(cur)
            return out

        pvn = [0]

        def head_ctx_steps(hl, two_banks=False):
            h = hh * 4 + hl
            c, hp = hl // 2, hl % 2
            pr = slice(hp * 64, (hp + 1) * 64)
            eb, ebn = ebt[hl % 2], f"ebt{hl % 2}"
            ptc, ptcn = PTc[hl % 2], f"PTc{hl % 2}"
            if hl > 0:
                S.dma("sp", eb[:, :, :], bias_d[l, h].rearrange("p (a k) -> p a k", a=30), writes=[ebn])
                yield
                for pc in range(6):
                    S.op("act", lambda e, pc=pc: e.activation(
                        out=eb[:, pc * 5:pc * 5 + 5, :], in_=eb[:, pc * 5:pc * 5 + 5, :], func=AF.Exp),
                        reads=[ebn], writes=[ebn])
                    yield
            n_ = 0
            for t in range(2):
                for b_ in range(4):
                    cb, cbn = (psG1, "psG1") if (two_banks and n_ % 2 == 1) else (psG0, "psG0")
                    n_ += 1
                    S.op("pe", lambda e, cb=cb: e.matmul(
                        cb[:, :], lhsT=KcT[pr, c, t * 128:(t + 1) * 128],
                        rhs=QT[pr, c, b_ * 512:(b_ + 1) * 512], start=True, stop=True),
                        reads=["KcT"] + [f"QT{i}" for i in range(b_ * 4, b_ * 4 + 4)], writes=[cbn])
                    S.op("act", lambda e, cb=cb: e.activation(
                        out=ptc[:, t, b_ * 512:(b_ + 1) * 512], in_=cb[:, :], func=AF.Exp, scale=0.125),
                        reads=[cbn], writes=[f"{ptcn}.{b_}"])
                    yield

        g0 = head_ctx_steps(0, two_banks=True)
        for _ in g0:
            pass
        for hl in range(4):
            h = hh * 4 + hl
            c, hp = hl // 2, hl % 2
            pr = slice(hp * 64, (hp + 1) * 64)
            eb, ebn = ebt[hl % 2], f"ebt{hl % 2}"
            ptc, ptcn = PTc[hl % 2], f"PTc{hl % 2}"
            gnext = head_ctx_steps(hl + 1) if hl + 1 < 4 else iter(())

            def mm_s(j):
                us = users[j]
                i0, n = us[0], len(us)
                pS, pSn = psS[j % 2]

                def f(e):
                    ins = None
                    for c0 in range(0, n * 128, 512):
                        c1 = min(n * 128, c0 + 512)
                        ins = e.matmul(pS[:, c0:c1], lhsT=KT[pr, c, j * 128:(j + 1) * 128],
                                       rhs=QT[pr, c, i0 * 128 + c0:i0 * 128 + c1], start=True, stop=True)
                    return ins
                S.op("pe", f, reads=[f"KT{j}"] + [f"QT{i}" for i in us], writes=pSn)

            def exp_s(j):
                us = users[j]
                i0 = us[0]
                pS, pSn = psS[j % 2]
                for run in runs(us, j):
                    r0, n = run[0], len(run)
                    S.op("act", lambda e, r0=r0, n=n: e.activation(
                        out=pt_out(r0, n, j),
                        in_=pS[:, (r0 - i0) * 128:(r0 - i0 + n) * 128].rearrange("p (i q) -> p i q", i=n),
                        func=AF.Exp, scale=0.125), reads=pSn, writes=[f"PT{i}" for i in run])

            def bias_mul(i):
                S.op("dve", lambda e: e.tensor_tensor(out=PT[:, i, :, :], in0=PT[:, i, :, :],
                                                      in1=eb[:, pid_attn(i) * 5:pid_attn(i) * 5 + 5, :],
                                                      op=ALU.mult), reads=[f"PT{i}", ebn],
                     writes=[f"PT{i}"])

            def pv(i):
                st = start_attn(i)
                pO, pOn = psOs[pvn[0] % 2]
                rc, rcn = rcp[pvn[0] % 2], f"rcp{pvn[0] % 2}"
                pvn[0] += 1

                def f(e):
                    ins = None
                    for s_ in range(7):
                        if s_ < 5:
                            lhsT, rhs = PT[:, i, s_, :], Vaug[:, st + s_, hl, :]
                        else:
                            lhsT, rhs = ptc[:, s_ - 5, i * 128:(i + 1) * 128], Vc[:, s_ - 5, hl, :]
                        ins = e.matmul(pO[:, 0:65], lhsT=lhsT, rhs=rhs, start=(s_ == 0), stop=(s_ == 6))
                    return ins
                S.op("pe", f, reads=[f"PT{i}", f"{ptcn}.{i // 4}", "Vc", "Vones"] +
                     [f"Vaug{j}" for j in range(st, st + 5)], writes=[pOn])
                S.op("dve", lambda e: e.reciprocal(out=rc[:, :], in_=pO[:, 64:65]), reads=[pOn],
                     writes=[rcn])
                S.op("dve", lambda e: e.tensor_scalar(
                    out=ytok[:, i, hl * 64:(hl + 1) * 64], in0=pO[:, 0:64], scalar1=rc[:, 0:1],
                    scalar2=None, op0=ALU.mult), reads=[pOn, rcn], writes=[f"ytok{i}"])

            def tr_y(i):
                def f(e):
                    ins = None
                    for c_ in range(2):
                        ins = e.transpose(out=psT[:, c_ * 128:(c_ + 1) * 128],
                                          in_=ytok[:, i, c_ * 128:(c_ + 1) * 128], identity=identb[:, :])
                    return ins
                S.op("pe", f, reads=[f"ytok{i}", "identb"], writes=["psT"])
                copy_op("dve", PT[:, i, 0:2, :], psT[:, 0:256].rearrange("p (c t) -> p c t", c=2),
                        ["psT"], [f"PT{i}"])

            pend, ydone = [], []
            mm_s(0)
            for j in range(16):
                if j + 1 < 16:
                    mm_s(j + 1)
                exp_s(j)
                if hl == 3:
                    for i in ydone:
                        tr_y(i)
                    ydone = []
                for i in pend:
                    pv(i)
                    ydone.append(i)
                pend = done_at[j]
                for i in pend:
                    bias_mul(i)
                if 1 <= j:
                    next(gnext, None)
            for i in pend:
                pv(i)
                ydone.append(i)
            for _ in gnext:
                pass
            if hl == 3:
                for i in ydone:
                    tr_y(i)
        def yna_rhs(kc, b):
            a_ = PT[:, 4 * b, kc, :]
            return bass.AP(tensor=a_.tensor, offset=a_.offset, ap=[[a_.ap[0][0], 128], [640, 4], [1, 128]])
        wout_partial(l, None, None, 512 + hh * 256, 2, 16, wo=wo_pre, rhs_fn=yna_rhs,
                     ynames=lambda b: [f"PT{i}" for i in range(4 * b, 4 * b + 4)])
        S.barrier()

    def hyena_phase(l):
        A.reset()
        lo = l * PL
        x1T = A.s("x1T", [128, 2, NT], BF16)
        x2T = A.s("x2T", [128, 2, NT], BF16)
        vT = A.s("vT", [128, 2, NT], BF16)
        wo_pre = wout_load(l, 256, 2)
        mark = A.mark()
        whyp = WN
        ust_l = [A.s(f"ust{i}", [128, NT], F32) for i in range(2)]
        acc_l = [A.s(f"acc{i}", [128, NT], F32) for i in range(2)]
        dsts = [x1T, x1T, x2T, x2T, vT, vT]
        HYW = [["hy0"], ["hy1"], ["hy2e", "hy2o"]]
        for fc in range(6):
            ust, acc = ust_l[fc % 2], acc_l[fc % 2]
            Nu, Na = f"ust{fc % 2}", f"acc{fc % 2}"
            for b in range(4):
                bank, bname = BANKS[b % 2]

                def mm(e, fc=fc, b=b, bank=bank):
                    ins = None
                    for k in range(8):
                        ins = e.matmul(bank, lhsT=whyp[:, k, fc * 128:(fc + 1) * 128],
                                       rhs=hT[:, k, b * 512:(b + 1) * 512], start=(k == 0), stop=(k == 7))
                    return ins
                S.op("pe", mm, reads=["wnext"] + HTN(b), writes=[bname])
                copy_op("act", ust[:, b * 512:(b + 1) * 512], bank, [bname], [Nu])
                S.op("act", lambda e, fc=fc, b=b, bank=bank, acc=acc: e.activation(
                    out=acc[:, b * 512:(b + 1) * 512], in_=bank, func=AF.Identity,
                    scale=sm(lo + C_CW + 6 + fc), bias=sm(lo + C_CB + fc)),
                    reads=[bname, "small"], writes=[Na])
            cw = lambda tap, fc=fc: sm(lo + C_CW + tap * 6 + fc)
            dst = dsts[fc]
            S.op("dve", lambda e, fc=fc, ust=ust, acc=acc: e.scalar_tensor_tensor(
                out=acc[:, 1:NT], in0=ust[:, 0:NT - 1], scalar=cw(0), in1=acc[:, 1:NT],
                op0=ALU.mult, op1=ALU.add), reads=[Nu, "small", Na], writes=[Na])
            a_hi = acc[:, 256:NT].rearrange("p (s t) -> p s t", t=256)[:, :, 0:1]
            u_lo = ust[:, 0:NT - 256].rearrange("p (s t) -> p s t", t=256)[:, :, 255:256]
            S.op("dve", lambda e, fc=fc, ust=ust, acc=acc: e.scalar_tensor_tensor(
                out=a_hi, in0=u_lo, scalar=nw[:, l, fc:fc + 1], in1=a_hi, op0=ALU.mult, op1=ALU.add),
                reads=[Nu, "nw", Na], writes=[Na])
            a_lo = acc[:, 0:NT - 256].rearrange("p (s t) -> p s t", t=256)[:, :, 255:256]
            u_hi = ust[:, 256:NT].rearrange("p (s t) -> p s t", t=256)[:, :, 0:1]
            S.op("dve", lambda e, fc=fc, ust=ust, acc=acc: e.scalar_tensor_tensor(
                out=a_lo, in0=u_hi, scalar=nw[:, l, 6 + fc:7 + fc], in1=a_lo, op0=ALU.mult,
                op1=ALU.add), reads=[Nu, "nw", Na], writes=[Na])
            S.op("dve", lambda e, fc=fc, ust=ust, acc=acc, dst=dst: e.scalar_tensor_tensor(
                out=dst[:, fc % 2, 0:NT - 1], in0=ust[:, 1:NT], scalar=cw(2), in1=acc[:, 0:NT - 1],
                op0=ALU.mult, op1=ALU.add), reads=[Nu, "small", Na], writes=HYW[fc // 2])
            S.op("dve", lambda e, fc=fc, acc=acc, dst=dst: e.tensor_copy(
                out=dst[:, fc % 2, NT - 1:NT], in_=acc[:, NT - 1:NT]), reads=[Na],
                writes=HYW[fc // 2])
        S.barrier()
        A.reset(mark)
        prefetch("mlp", l)
        zt = A.s("zt", [128, 16, 256], BF16)
        Ypm = A.s("Ypm", [128, 2, 16, 256], BF16)
        UV = [A.s(f"uv{i}", [128, 256], F32) for i in range(4)]
        NF, NKB, NI, PF, PI = 6, 4, 8, 2, 7
        fring = [A.at(f"fwr{i}", [128, 16, 128], BF16, HT_OFF + i * 4096) for i in range(NF)]
        ztp = A.at("ztp", [128, 8, 256], BF16, HT_OFF + NF * 4096)
        iring = [A.s(f"ivr{i}", [128, 1024], BF16) for i in range(NI - 2)] + \
                [A.at(f"ivr{NI - 2 + i}", [128, 1024], BF16, HT_OFF + NF * 4096 + 4096 + i * 2048)
                 for i in range(2)]
        kb = [A.s(f"kb{i}", [128, 2, 256], F32) for i in range(NKB)]
        tt = [A.s(f"tt{i}", [128, 256], F32) for i in range(8)]
        gst = [A.s(f"gst{i}", [128, 512], F32) for i in range(2)]
        yhT = A.s("yhT", [128, 2, NT], BF16)
        ninv = [0]

        def load_f(j, o):
            for ri in range(2):
                q_ = 2 * j + ri
                S.dma("sp", fring[q_ % NF][:, :, :],
                      fwd_d[j + 8 * ri].rearrange("p (s f) -> p s f", s=16), writes=[f"fwr{q_ % NF}"])

        def load_k(q, o):
            j, m = q // 2, q % 2
            for ri in range(2):
                S.dma("sp", kb[q % NKB][:, ri, :],
                      khat[l, ri * 16 + j + 8 * m, :, o * 256:(o + 1) * 256], writes=[f"kb{q % NKB}"])

        def load_i(half, jj):
            n_ = ninv[0]
            ninv[0] += 1
            S.dma("sp", iring[n_ % NI][:, :], inv_d[half, jj], writes=[f"ivr{n_ % NI}"])
            return n_ % NI

        for o in range(2):
            src = vT
            gate = x1T if o == 0 else x2T
            gname = "hy0" if o == 0 else "hy1"
            for j in range(PF):
                load_f(j, o)
            for q in range(2):
                load_k(q, o)
            for half in range(2):
                for g2 in range(2):

                    def trz(e, half=half, g2=g2):
                        ins = None
                        for tl in range(4):
                            t = half * 8 + g2 * 4 + tl
                            for c in range(2):
                                a_ = src[:, c, 256 * (t % 8) + t // 8:256 * (t % 8) + t // 8 + 1]
                                sel_ = bass.AP(tensor=a_.tensor, offset=a_.offset,
                                               ap=[[a_.ap[0][0], 128], [2, 128]])
                                ins = e.transpose(
                                    out=psT[:, (tl * 2 + c) * 128:(tl * 2 + c + 1) * 128],
                                    in_=sel_, identity=identb[:, :])
                        return ins
                    S.op("pe", trz, reads=["hy2e" if half == 0 else "hy2o", "identb"], writes=["psT"])
                    t0 = half * 8 + g2 * 4
                    copy_op("dve" if g2 % 2 == 0 else "act", zt[:, t0:t0 + 4, :],
                            psT[:, 0:1024].rearrange("p (t c) -> p t c", t=4), ["psT"], ["zt"])
            S.op("dve", lambda e: e.tensor_scalar(out=ztp[:, :, :], in0=zt[:, 8:16, :], scalar1=-1.0,
                                                  scalar2=None, op0=ALU.mult), reads=["zt"],
                 writes=["ztp"])
            for q in range(16):
                j, m = q // 2, q % 2
                zname = "zt" if m == 0 else "ztp"
                kbt, kbn = kb[q % NKB], f"kb{q % NKB}"
                banks = (BANKS[(q % 2) * 2], BANKS[(q % 2) * 2 + 1])
                for ri in range(2):
                    q_ = 2 * j + ri
                    fw = fring[q_ % NF]
                    bank, bname = banks[ri]

                    def mm(e, fw=fw, bank=bank, m=m):
                        ins = None
                        for s_ in range(16):
                            rhs = ztp[:, s_ - 8, :] if (m == 1 and s_ >= 8) else zt[:, s_, :]
                            ins = e.matmul(bank[:, 0:256], lhsT=fw[:, s_, :], rhs=rhs,
                                           start=(s_ == 0), stop=(s_ == 15))
                        return ins
                    S.op("pe", mm, reads=[f"fwr{q_ % NF}", "zt", zname], writes=[bname])
                if m == 1 and j + PF < 8:
                    load_f(j + PF, o)
                if q + 2 < 16:
                    load_k(q + 2, o)
                (bR, nR), (bI, nI) = banks
                sR, sI = j + 8 * m, 16 + j + 8 * m
                tb_ = (q % 2) * 4
                T0, T1, T2, T3 = tt[tb_], tt[tb_ + 1], tt[tb_ + 2], tt[tb_ + 3]
                N0, N1, N2, N3 = f"tt{tb_}", f"tt{tb_ + 1}", f"tt{tb_ + 2}", f"tt{tb_ + 3}"
                S.op("dve", lambda e, bR=bR, kbt=kbt, T0=T0: e.tensor_tensor(
                    out=T0[:, :], in0=bR[:, 0:256], in1=kbt[:, 0, :], op=ALU.mult),
                    reads=[nR, kbn], writes=[N0])
                S.op("dve", lambda e, bR=bR, kbt=kbt, T2=T2: e.tensor_tensor(
                    out=T2[:, :], in0=bR[:, 0:256], in1=kbt[:, 1, :], op=ALU.mult),
                    reads=[nR, kbn], writes=[N2])
                S.op("dve", lambda e, bI=bI, kbt=kbt, T1=T1: e.tensor_tensor(
                    out=T1[:, :], in0=bI[:, 0:256], in1=kbt[:, 1, :], op=ALU.mult),
                    reads=[nI, kbn], writes=[N1])
                S.op("dve", lambda e, bI=bI, kbt=kbt, T3=T3: e.tensor_tensor(
                    out=T3[:, :], in0=bI[:, 0:256], in1=kbt[:, 0, :], op=ALU.mult),
                    reads=[nI, kbn], writes=[N3])
                U0, U1 = (UV[0], UV[1]) if m == 0 else (UV[2], UV[3])
                n0, n1 = ("uv0", "uv1") if m == 0 else ("uv2", "uv3")
                S.op("pool", lambda e, T0=T0, T1=T1, U0=U0: e.tensor_tensor(
                    out=U0[:, :], in0=T0[:, :], in1=T1[:, :], op=ALU.subtract),
                    reads=[N0, N1], writes=[n0])
                S.op("pool", lambda e, T2=T2, T3=T3, U1=U1: e.tensor_tensor(
                    out=U1[:, :], in0=T2[:, :], in1=T3[:, :], op=ALU.add),
                    reads=[N2, N3], writes=[n1])
                if m == 1:
                    for ri in range(2):
                        S.op("dve", lambda e, ri=ri: e.tensor_tensor(
                            out=Ypm[:, 0, ri * 8 + j, :], in0=UV[ri][:, :], in1=UV[2 + ri][:, :],
                            op=ALU.add), reads=[f"uv{ri}", f"uv{2 + ri}"], writes=["Ypm"])
                        S.op("dve", lambda e, ri=ri: e.tensor_tensor(
                            out=Ypm[:, 1, ri * 8 + j, :], in0=UV[ri][:, :], in1=UV[2 + ri][:, :],
                            op=ALU.subtract), reads=[f"uv{ri}", f"uv{2 + ri}"], writes=["Ypm"])
                if q == 8:
                    pre_slots = [load_i(0, jj) for jj in range(PI)]
            slots = list(pre_slots)
            for par in range(2):
                acc_b = [BANKS[3], BANKS[4], BANKS[5], BANKS[6]] if par == 0 else \
                        [BANKS[0], BANKS[1], BANKS[2], BANKS[3]]
                for jj in range(16):
                    sl_ = slots.pop(0)
                    iv = iring[sl_]
                    ivn = f"ivr{sl_}"

                    def mm(e, jj=jj, iv=iv, acc_b=acc_b, par=par):
                        ins = None
                        for cc in range(2):
                            for tb in range(2):
                                ins = e.matmul(acc_b[cc * 2 + tb][0],
                                               lhsT=Ypm[:, par, jj, cc * 128:(cc + 1) * 128],
                                               rhs=iv[:, tb * 512:(tb + 1) * 512], start=(jj == 0),
                                               stop=(jj == 15))
                        return ins
                    S.op("pe", mm, reads=[ivn, "Ypm"], writes=[b_[1] for b_ in acc_b])
                    nxt = jj + PI
                    if nxt < 16:
                        slots.append(load_i(par, nxt))
                    elif par == 0:
                        slots.append(load_i(1, nxt - 16))
                for cc in range(2):
                    for tb in range(2):
                        bank, bname = acc_b[cc * 2 + tb]
                        t0_ = tb * 1024 + par
                        g_ = gst[(cc * 2 + tb) % 2]
                        gn_ = f"gst{(cc * 2 + tb) % 2}"
                        dst = vT if o == 0 else yhT
                        hyp = "hy2e" if par == 0 else "hy2o"
                        dn = hyp if o == 0 else "yhT"

                        def sel(tn, cc=cc, t0_=t0_):
                            a_ = tn[:, cc, t0_:t0_ + 1]
                            return bass.AP(tensor=a_.tensor, offset=a_.offset,
                                           ap=[[a_.ap[0][0], 128], [2, 512]])
                        S.op("dve", lambda e, cc=cc, bank=bank, g_=g_: e.scalar_tensor_tensor(
                            out=g_[:, :], in0=sel(src), scalar=sm(lo + C_HB + o * 2 + cc), in1=bank,
                            op0=ALU.mult, op1=ALU.add), reads=[hyp, "small", bname], writes=[gn_])
                        S.op("pool", lambda e, g_=g_, dst=dst: e.tensor_tensor(
                            out=sel(dst), in0=g_[:, :], in1=sel(gate), op=ALU.mult),
                            reads=[gn_, gname], writes=[dn])
        wout_partial(l, yhT, "yhT", 256, 2, 16, wo=wo_pre)
        S.barrier()

    def mlp_phase(l):
        A.reset()
        act = A.s("act", [128, 8, NT], BF16)
        ring = [A.s(f"wr{i}", [128, 8, 512], BF16) for i in range(4)]
        rl = [A.s(f"rl{i}", [128, 512], BF16) for i in range(2)]
        nld = 0
        for fg in range(4):
            ups = []
            for hf in range(2):
                if fg == 0 and hf == 0:
                    ups.append((WN, "wnext"))
                    continue
                wt, wn_ = ring[nld % 4], f"wr{nld % 4}"
                nld += 1
                c0 = fg * 1024 + hf * 512
                load_w(wt[:, :, :], w_up[l], (0, D), (c0, c0 + 512), wn_)
                ups.append((wt, wn_))
            dns = []
            for hf in range(2):
                wt, wn_ = ring[nld % 4], f"wr{nld % 4}"
                nld += 1
                load_w(wt[:, :, :], w_down[l], (fg * 1024, (fg + 1) * 1024), (hf * 512, (hf + 1) * 512),
                       wn_)
                dns.append((wt, wn_))
            n = 0
            for fc in range(8):
                wt, wn_ = ups[fc // 4]
                for b in range(4):
                    bs = slice(b * 512, (b + 1) * 512)
                    bank, bname = BANKS[n % 4]
                    r_, rn_ = rl[n % 2], f"rl{n % 2}"
                    n += 1

                    def mm(e, fc=fc, bs=bs, bank=bank, wt=wt):
                        ins = None
                        for k in range(8):
                            ins = e.matmul(bank, lhsT=wt[:, k, (fc % 4) * 128:(fc % 4 + 1) * 128],
                                           rhs=hT[:, k, bs], start=(k == 0), stop=(k == 7))
                        return ins
                    S.op("pe", mm, reads=[wn_] + HTN(b), writes=[bname])
                    S.op("act", lambda e, bank=bank, r_=r_: e.activation(out=r_[:, :], in_=bank,
                                                                         func=AF.Relu),
                         reads=[bname], writes=[rn_])
                    S.op("pool", lambda e, fc=fc, bs=bs, r_=r_: e.tensor_tensor(
                        out=act[:, fc, bs], in0=r_[:, :], in1=r_[:, :], op=ALU.mult),
                        reads=[rn_], writes=[f"act{b}"])
            if fg == 0 and l + 1 < 2:
                prefetch("pool", l + 1)
            last = (l == 1 and fg == 3)
            order = [(dc, b) for b in range(4) for dc in range(8)] if last else \
                    [(dc, b) for dc in range(8) for b in range(4)]
            for (dc, b) in order:
                wt, wn_ = dns[dc // 4]
                bs = slice(b * 512, (b + 1) * 512)
                bank, bname = BANKS[n % 4]
                n += 1

                def mm(e, dc=dc, bs=bs, bank=bank, wt=wt):
                    ins = None
                    for fc in range(8):
                        ins = e.matmul(bank, lhsT=wt[:, fc, (dc % 4) * 128:(dc % 4 + 1) * 128],
                                       rhs=act[:, fc, bs], start=(fc == 0), stop=(fc == 7))
                    return ins
                S.op("pe", mm, reads=[wn_, f"act{b}"], writes=[bname])
                S.op("dve", lambda e, dc=dc, bs=bs, bank=bank: e.scalar_tensor_tensor(
                    out=xT[:, dc, bs], in0=bank, scalar=modv[:, l, 40 + dc:41 + dc], in1=xT[:, dc, bs],
                    op0=ALU.mult, op1=ALU.add), reads=[bname, "modv", f"xT{b}"],
                    writes=[f"xT{b}"])
                if last and dc == 7:
                    if b == 0:
                        yst_l = [A.s(f"yst{i}", [128, D], F32) for i in range(2)]
                        Y_DONE[0] = True
                    for i in range(4 * b, 4 * b + 4):
                        store_y_tile(i, yst_l[i % 2], f"yst{i % 2}", (BANKS[5], BANKS[6]))
        S.barrier()

    phases = []
    for l in range(2):
        phases += [("norm1", lambda l=l: rmsnorm(l, 0)), ("pool", lambda l=l: pool_phase(l)),
                   ("attn0", lambda l=l: attn_phase(l, 0)), ("attn1", lambda l=l: attn_phase(l, 1)),
                   ("hyena", lambda l=l: hyena_phase(l)), ("norm2", lambda l=l: rmsnorm(l, 1)),
                   ("mlp", lambda l=l: mlp_phase(l))]
    for idx, (pname, fn) in enumerate(phases):
        if stop is not None and idx >= stop:
            break
        fn()

    if not Y_DONE[0]:
        A.reset()
        yst = [A.s(f"yst{i}", [128, D], F32) for i in range(2)]
        for i in range(NTILE):
            store_y_tile(i, yst[i % 2], f"yst{i % 2}", (BANKS[0], BANKS[1]))
    S.barrier()
    return nc


_PERM = np.concatenate([np.arange(0, NT, 2), np.arange(1, NT, 2)])


def _dft_tables(L, nblk):
    N = 2 * L
    T = L * nblk
    pos = np.arange(L, dtype=np.float64)
    fwd = np.zeros((16, T, 128), np.float32)
    inv = np.zeros((32 * 128, T), np.float32)
    for j in range(8):
        for p in range(128):
            bq, f = (0, j * 128 + p) if nblk == 1 else (j, p)
            sl = slice(bq * L, (bq + 1) * L)
            th = 2.0 * np.pi * (f + 0.5) * pos / N
            thm = 2.0 * np.pi * (L - 1 - f + 0.5) * pos / N
            fwd[j, sl, p] = np.cos(th)
            fwd[8 + j, sl, p] = -np.sin(th)
            inv[j * 128 + p, sl] = (2.0 / N) * np.cos(th)
            inv[(8 + j) * 128 + p, sl] = (2.0 / N) * np.cos(thm)
            inv[(16 + j) * 128 + p, sl] = -(2.0 / N) * np.sin(th)
            inv[(24 + j) * 128 + p, sl] = (2.0 / N) * np.sin(thm)
    fwd = fwd[:, _PERM, :]
    fwd_b = fwd.reshape(16, 16, 128, 128).transpose(0, 2, 1, 3).reshape(16, 128, 16 * 128)
    invp = np.zeros((2, 16 * 128, T // 2), np.float32)
    for j in range(8):
        for p in range(128):
            bq, f = (0, j * 128 + p) if nblk == 1 else (j, p)
            for par in range(2):
                tl = np.arange(par, L, 2, dtype=np.float64)
                cols = ((bq * L + tl - par) // 2).astype(np.int64)
                th = 2.0 * np.pi * (f + 0.5) * tl / N
                invp[par, j * 128 + p, cols] = (2.0 / N) * np.cos(th)
                invp[par, (8 + j) * 128 + p, cols] = -(2.0 / N) * np.sin(th)
    inv_b = invp.reshape(2, 16, 128, 1024)
    return (np.ascontiguousarray(fwd_b).astype(NPBF), np.ascontiguousarray(inv_b).astype(NPBF))


def _hyena_consts(L, nblk):
    t = np.linspace(0.0, 1.0, L, dtype=np.float32)[:, None]
    w = (2.0 * np.pi * np.arange(L, dtype=np.float32)[:, None] / L).astype(np.float32)
    f = np.linspace(1e-4, 15, 16, dtype=np.float32)[None, :]
    z = np.concatenate([t, np.cos(f * w), -np.sin(f * w)], axis=-1).astype(np.float32)
    deltas = np.abs(np.linspace(np.log(1e-2) / 1.5, np.log(1e-2) / 0.3, 256, dtype=np.float32))
    decay = np.exp(-t * deltas[None, :]).astype(np.float32)
    zT = np.ascontiguousarray(np.tile(z, (nblk, 1))[_PERM].T)
    dec = np.tile(decay, (nblk, 1))[_PERM].reshape(16, 128, 256).transpose(1, 0, 2)
    pos = np.arange(NT)[_PERM]
    wn = (pos < L).astype(np.float32).reshape(16, 128).T
    wn = np.repeat(wn[:, :, None], 128, axis=2)
    tau0 = (pos % L != 0).astype(np.float32).reshape(16, 128).T
    return zT, np.ascontiguousarray(dec), np.ascontiguousarray(wn), np.ascontiguousarray(tau0)


def _pool_consts(L, nblk):
    T = L * nblk
    out = np.zeros((128, 4, 3, 4, 128), np.float32)
    tl = np.arange(L)
    for g, wd in enumerate((2, 4, 8, 16)):
        lo = np.clip(tl - wd // 2, 0, L - 1)
        hi = np.clip(tl + (wd - 1 - wd // 2), 0, L - 1)
        M1 = np.zeros((L, L), np.float64)
        for t_ in range(L):
            M1[t_, lo[t_]:hi[t_] + 1] = 1.0 / (hi[t_] - lo[t_] + 1)
        M1 -= np.eye(L)
        M = np.zeros((T, T), np.float64)
        for b in range(nblk):
            M[b * L:(b + 1) * L, b * L:(b + 1) * L] = M1
        for pat, j in enumerate((0, 2, 3, 15)):
            for ri, r in enumerate((-1, 0, 1)):
                if 0 <= j + r < 16:
                    blk = M[j * 128:(j + 1) * 128, (j + r) * 128:(j + r + 1) * 128]
                    out[:, pat, ri, g, :] = blk.T
    return out.reshape(128, -1).astype(NPBF)


def _bias_bank(rel_bias, sample):
    out = np.full((2, 8, 128, 30, 128), NEG, np.float32)
    reps = (0, 1, 2, 3, 14, 15)
    if sample:
        r_all = np.arange(32)
        c_all = np.arange(64)
        row_start = np.clip(r_all - 4, 0, 24)
        col_start = np.clip(c_all - 8, 0, 48)
        for pat, i in enumerate(reps):
            st = start_attn(i)
            q = i * 128 + np.arange(128)
            qr, qc = q // 64, q % 64
            for s_ in range(5):
                k = (st + s_) * 128 + np.arange(128)
                kr, kc = k // 64, k % 64
                vr = (kr[None, :] >= row_start[qr][:, None]) & (kr[None, :] < row_start[qr][:, None] + 8)
                vc = (kc[None, :] >= col_start[qc][:, None]) & (kc[None, :] < col_start[qc][:, None] + 16)
                valid = vr & vc
                dr = np.clip(kr[None, :] - qr[:, None] + 7, 0, 14)
                dc = np.clip(kc[None, :] - qc[:, None], -15, 15) + 15
                vals = rel_bias[:, :, dr, dc]
                out[:, :, :, pat * 5 + s_, :] = np.where(valid[None, None], vals, NEG)
    else:
        for pat, i in enumerate(reps):
            st = start_attn(i)
            for s_ in range(5):
                j = st + s_
                if j // 2 == i // 2:
                    out[:, :, :, pat * 5 + s_, :] = 0.0
    out = np.ascontiguousarray(out.transpose(0, 1, 4, 3, 2))
    return out.reshape(2, 8, 128, 30 * 128).astype(NPBF)


def _colvec(v):
    return np.ascontiguousarray(np.asarray(v, np.float32).reshape(-1, 128).T)


_CACHE = {}


def _consts(sample):
    key = ("c", sample)
    if key not in _CACHE:
        L, nblk = (2048, 1) if sample else (256, 8)
        fwd_b, inv_b = _dft_tables(L, nblk)
        zT, dec, wn, tau0 = _hyena_consts(L, nblk)
        _CACHE[key] = dict(fwd=fwd_b, inv=inv_b, zT=zT, decay=dec, wn=wn, tau0=tau0,
                           apool=_pool_consts(L, nblk))
    return _CACHE[key]


def _get_nc(stop=None):
    key = ("nc", stop)
    if key not in _CACHE:
        _CACHE[key] = build_nc(stop=stop)
    return _CACHE[key]


def make_in_maps(inp):
    g = {k: np.asarray(v) for k, v in inp.items()}
    shared = dict(
        w_mod=g["w_mod"], w_in=g["w_in"], w_out=g["w_out"], w_up=g["w_up"], w_down=g["w_down"],
        pool_w=g["pool_w"], f1_w=g["hy_f1_w"], f2_w=g["hy_f2_w"], f3_w=g["hy_f3_w"],
        identb=np.eye(128, dtype=np.float32).astype(NPBF), identf=np.eye(128, dtype=np.float32),
        onesm=np.full((128, 128), 1.0 / 1024.0, np.float32).astype(NPBF))
    small64 = np.zeros((64, NV64), np.float32)
    for l in range(2):
        small64[:, l * 4 + 0] = g["hy_f1_b"][l]
        small64[:, l * 4 + 1] = g["hy_f1_freq"][l]
        small64[:, l * 4 + 2] = g["hy_f2_b"][l]
        small64[:, l * 4 + 3] = g["hy_f2_freq"][l]
    shared["small64"] = small64
    gqk = np.zeros((2, 128, 1024), np.float32)
    for l in range(2):
        gqk[l, :, 0:512] = np.tile(g["q_norm_g"][l], 8)[None, :]
        gqk[l, :, 512:1024] = np.tile(g["k_norm_g"][l], 8)[None, :]
    shared["gqk"] = gqk
    gcol = np.zeros((2, 128, 2), np.float32)
    for l in range(2):
        gcol[l, :, 0] = np.tile(g["q_norm_g"][l], 2)
        gcol[l, :, 1] = np.tile(g["k_norm_g"][l], 2)
    shared["gcol"] = gcol
    bias_s = _bias_bank(g["rel_bias"], True)
    bias_p = _bias_bank(g["rel_bias"], False)
    maps = []
    for core in range(8):
        sample = core >= 4
        cst = _consts(sample)
        m = dict(shared)
        small = np.zeros((128, NV), np.float32)
        for l in range(2):
            lo = l * PL
            small[:, lo + C_N1G:lo + C_N1G + 8] = _colvec(g["norm1_g"][l])
            small[:, lo + C_N2G:lo + C_N2G + 8] = _colvec(g["norm2_g"][l])
            small[:, lo + C_BMOD:lo + C_BMOD + 48] = _colvec(g["b_mod"][l])
            small[:, lo + C_PSC:lo + C_PSC + 2] = _colvec(g["pool_scale"][l])
            for tap in range(3):
                small[:, lo + C_CW + tap * 6:lo + C_CW + tap * 6 + 6] = _colvec(g["hy_conv_w"][l, tap])
            small[:, lo + C_CB:lo + C_CB + 6] = _colvec(g["hy_conv_b"][l])
            for o in range(2):
                small[:, lo + C_HB + o * 2:lo + C_HB + o * 2 + 2] = _colvec(g["hy_bias"][l, o])
        small[:, C_EPS] = 1e-6
        small[:, C_SGN] = np.where(np.arange(128) % 2 == 0, 1.0, -1.0)
        small[:, C_TAU0:C_TAU0 + 16] = cst["tau0"]
        if sample:
            b = core - 4
            m["x_in"] = np.ascontiguousarray(g["x_sample"][b])
            small[:, C_COND:C_COND + 8] = _colvec(g["c"][b])
            small[:, C_FLAG] = 0.0
            m["cachek"] = np.ascontiguousarray(g["cache_k"][b].reshape(2, 256, 512))
            cv = np.ones((2, 256, 8, 65), np.float32)
            cv[..., :64] = g["cache_v"][b]
            m["cachev"] = cv.reshape(2, 256, 8 * 65)
            m["biasbank"] = bias_s
        else:
            m["x_in"] = np.ascontiguousarray(g["x_prompt"][core * 8:(core + 1) * 8].reshape(NT, D))
            small[:, C_COND:C_COND + 8] = _colvec(g["c_ctx"])
            small[:, C_FLAG] = 1.0
            m["cachek"] = np.zeros((2, 256, 512), np.float32)
            m["cachev"] = np.zeros((2, 256, 8 * 65), np.float32)
            m["biasbank"] = bias_p
        m["small"] = small
        m["zT"], m["decay"], m["wn"], m["apool"] = cst["zT"], cst["decay"], cst["wn"], cst["apool"]
        m["fwd"], m["inv"] = cst["fwd"], cst["inv"]
        maps.append(m)
    return maps


def assemble(results):
    y_prompt = np.stack([results[c]["y"] for c in range(4)]).reshape(32, 256, D)
    y_sample = np.stack([results[c]["y"] for c in range(4, 8)])
    nk = np.zeros((32, 2, 256, 8, 64), np.float32)
    nv = np.zeros((32, 2, 256, 8, 64), np.float32)
    for c in range(4):
        ko = results[c]["kout"].reshape(2, 8, 256, 8, 64)
        vo = results[c]["vout"].reshape(2, 8, 256, 8, 64)
        nk[c * 8:(c + 1) * 8] = ko.transpose(1, 0, 2, 3, 4)
        nv[c * 8:(c + 1) * 8] = vo.transpose(1, 0, 2, 3, 4)
    return (y_prompt.astype(np.float32), y_sample.astype(np.float32), nk, nv)


def kernel(**inputs):
    nc = _get_nc()
    maps = make_in_maps(inputs)
    res = run_bass_kernel_spmd(nc, maps, core_ids=list(range(8)))
    return assemble(res.results)
```

```python
import numpy as np
import ml_dtypes
import concourse.bass as bass
import concourse.mybir as mybir
from concourse.bass_utils import run_bass_kernel_spmd

F32 = mybir.dt.float32
BF16 = mybir.dt.bfloat16
ALU = mybir.AluOpType
AF = mybir.ActivationFunctionType
AX = mybir.AxisListType
NPBF = ml_dtypes.bfloat16

D = 1024
NT = 2048
NTILE = 16
NCH = 8
L_DEPTH = 2
IN_W = 2560
DFF = 4096
NEG = -30000.0
PI_C = 3.14159
MAGIC = 12582912.0
TWO_PI = 6.283185307179586

C_N1G, C_N2G, C_BMOD, C_PSC, C_CW, C_CB, C_HB = 0, 8, 16, 64, 66, 84, 90
PL = 94
C_COND = 2 * PL
C_FLAG = C_COND + 8
C_EPS = C_FLAG + 1
C_TAU0 = C_EPS + 1
C_SGN = C_TAU0 + 16
NV = C_SGN + 1
NV64 = 8


class Sched:
    NDMA_SEM = 8

    def __init__(self, nc):
        self.nc = nc
        self.eng = {"pe": nc.tensor, "act": nc.scalar, "dve": nc.vector, "pool": nc.gpsimd,
                    "sp": nc.sync}
        self.sem, self.cnt = {}, {}
        for e in ("pe", "act", "dve", "pool"):
            self.sem[e] = nc.alloc_semaphore(name=f"s_{e}")
            self.cnt[e] = 0
        self.dsem, self.dcnt = {}, {}
        for q in ("sp", "pool"):
            self.dsem[q] = [nc.alloc_semaphore(name=f"d_{q}{i}") for i in range(self.NDMA_SEM)]
            self.dcnt[q] = 0
        self.known = {e: {} for e in ("pe", "act", "dve", "pool", "sp")}
        self.last_w, self.readers = {}, {}

    def _tok_wait(self, tok):
        if tok[0] == "c":
            return ("c", tok[1]), self.sem[tok[1]], tok[2]
        q, m = tok[1], tok[2]
        r, j = m % self.NDMA_SEM, m // self.NDMA_SEM
        return ("d", q, r), self.dsem[q][r], 16 * (j + 1)

    @staticmethod
    def _excl(reads, writes):
        return list(writes) + [r for r in reads if r.startswith("ps")]

    def _deps(self, reads, writes):
        writes = self._excl(reads, writes)
        deps = []
        for r in reads:
            t = self.last_w.get(r)
            if t is not None:
                deps.append(t)
        for r in writes:
            t = self.last_w.get(r)
            if t is not None:
                deps.append(t)
            deps.extend(self.readers.get(r, ()))
        return deps

    def _emit_waits(self, waiter, deps, self_n=None):
        h = self.eng[waiter]
        best = {}
        for tok in deps:
            if tok[0] == "c" and tok[1] == waiter:
                if waiter == "pe":
                    continue
                if self_n is not None and tok[2] < self_n - 2:
                    continue
            key, sem, val = self._tok_wait(tok)
            if self.known[waiter].get(key, 0) >= val:
                continue
            if key not in best or best[key][1] < val:
                best[key] = (sem, val)
        for key, (sem, val) in best.items():
            h.wait_ge(sem, val)
            self.known[waiter][key] = val

    def _record(self, tok, reads, writes):
        writes = self._excl(reads, writes)
        for r in reads:
            self.readers.setdefault(r, []).append(tok)
        for r in writes:
            self.last_w[r] = tok
            self.readers[r] = []

    def op(self, eng, fn, reads=(), writes=()):
        reads, writes = list(reads), list(writes)
        deps = self._deps(reads, writes)
        n = self.cnt[eng] + 1
        self._emit_waits(eng, deps, self_n=n)
        ins = fn(self.eng[eng])
        ins.then_inc(self.sem[eng], 1)
        self.cnt[eng] = n
        self._record(("c", eng, n), reads, writes)

    def dma(self, q, out, in_, reads=(), writes=()):
        reads, writes = list(reads), list(writes)
        deps = self._deps(reads, writes)
        m = self.dcnt[q]
        if m >= self.NDMA_SEM:
            deps.append(("d", q, m - self.NDMA_SEM))
        self._emit_waits(q, deps)
        r = m % self.NDMA_SEM
        self.eng[q].dma_start(out=out, in_=in_).then_inc(self.dsem[q][r], 16)
        self.dcnt[q] = m + 1
        self._record(("d", q, m), reads, writes)

    def barrier(self):
        toks = []
        for e in ("pe", "act", "dve", "pool"):
            if self.cnt[e] > 0:
                toks.append(("c", e, self.cnt[e]))
        for q in ("sp", "pool"):
            for m in range(max(0, self.dcnt[q] - self.NDMA_SEM), self.dcnt[q]):
                toks.append(("d", q, m))
        for w in ("pe", "act", "dve", "pool", "sp"):
            h = self.eng[w]
            for tok in toks:
                if tok[0] == "c" and tok[1] == w:
                    continue
                key, sem, val = self._tok_wait(tok)
                if self.known[w].get(key, 0) >= val:
                    continue
                h.wait_ge(sem, val)
                self.known[w][key] = val
        self.last_w.clear()
        self.readers.clear()


class Arena:
    BASE, TOP = 16512, 229344

    def __init__(self, nc):
        self.nc = nc
        self.persist = self.BASE
        self.cur = self.BASE
        self.n = 0

    def _alloc(self, name, shape, dt, off):
        self.n += 1
        return self.nc.alloc_sbuf_tensor_at(f"{name}_{self.n}", list(shape), dt, offset=off)

    @staticmethod
    def _bytes(shape, dt):
        n = 1
        for s in shape[1:]:
            n *= s
        return (n * (2 if dt == BF16 else 4) + 31) // 32 * 32

    def p(self, name, shape, dt):
        assert self.cur == self.persist, "persistent alloc after scratch"
        t = self._alloc(name, shape, dt, self.persist)
        self.persist += self._bytes(shape, dt)
        self.cur = self.persist
        assert self.persist <= self.TOP
        return t

    def s(self, name, shape, dt):
        t = self._alloc(name, shape, dt, self.cur)
        self.cur += self._bytes(shape, dt)
        assert self.cur <= self.TOP, f"SBUF overflow at {name}: {self.cur}"
        return t

    def reset(self, to=None):
        self.cur = self.persist if to is None else to

    def at(self, name, shape, dt, off):
        return self._alloc(name, shape, dt, off)

    def mark(self):
        return self.cur


def pid_attn(i):
    return {0: 0, 1: 1, 14: 4, 15: 5}.get(i, 2 if i % 2 == 0 else 3)


def pid_pool(j):
    return 0 if j == 0 else (3 if j == 15 else (1 if j % 2 == 0 else 2))


def start_attn(i):
    return min(max(i - 2, 0), 11)


def build_nc(stop=None, debug=False):
    nc = bass.Bass("TRN2", target_bir_lowering=False)

    def din(name, shape, dt=F32):
        return nc.dram_tensor(name, list(shape), dt, kind="ExternalInput").ap()

    x_in = din("x_in", [NT, D])
    w_mod = din("w_mod", [2, D, 6 * D])
    w_in = din("w_in", [2, D, IN_W])
    w_out = din("w_out", [2, D, D])
    w_up = din("w_up", [2, D, DFF])
    w_down = din("w_down", [2, DFF, D])
    pool_w = din("pool_w", [2, 4, 64, 64])
    f1_w = din("f1_w", [2, 33, 64])
    f2_w = din("f2_w", [2, 64, 64])
    f3_w = din("f3_w", [2, 64, 1024])
    small = din("small", [128, NV])
    small64 = din("small64", [64, NV64])
    gqk = din("gqk", [2, 128, 1024])
    gcol = din("gcol", [2, 128, 2])
    cachek = din("cachek", [2, 256, 512])
    cachev = din("cachev", [2, 256, 8 * 65])
    zT_d = din("zT", [33, NT])
    decay_d = din("decay", [128, 16, 256])
    wn_d = din("wn", [128, 16, 128])
    apool_d = din("apool", [128, 4 * 3 * 4 * 128], BF16)
    bias_d = din("biasbank", [2, 8, 128, 30 * 128], BF16)
    fwd_d = din("fwd", [16, 128, 16 * 128], BF16)
    inv_d = din("inv", [2, 16, 128, 1024], BF16)
    identb_d = din("identb", [128, 128], BF16)
    identf_d = din("identf", [128, 128])
    onesm_d = din("onesm", [128, 128], BF16)

    y_out = nc.dram_tensor("y", [NT, D], F32, kind="ExternalOutput").ap()
    k_out = nc.dram_tensor("kout", [2, NT, 512], F32, kind="ExternalOutput").ap()
    v_out = nc.dram_tensor("vout", [2, NT, 512], F32, kind="ExternalOutput").ap()
    khat = nc.dram_tensor("khat", [2, 32, 128, 512], F32, kind="Internal").ap()

    S = Sched(nc)
    A = Arena(nc)

    psS0 = nc.alloc_psum_tensor("psS0", [128, 1024], F32)
    psS1 = nc.alloc_psum_tensor("psS1", [128, 1024], F32)
    psO = nc.alloc_psum_tensor("psO", [128, 512], F32)
    psG0 = nc.alloc_psum_tensor("psG0", [128, 512], F32)
    psG1 = nc.alloc_psum_tensor("psG1", [128, 512], F32)
    psT = nc.alloc_psum_tensor("psT", [128, 1024], BF16)
    psT_f32 = psS1
    BANKS = [(psG0[:, :], "psG0"), (psG1[:, :], "psG1"), (psO[:, :], "psO"),
             (psS0[:, 0:512], "psS0.0"), (psS0[:, 512:1024], "psS0.1"),
             (psS1[:, 0:512], "psS1.0"), (psS1[:, 512:1024], "psS1.1")]

    smallt = A.p("small", [128, NV], F32)
    small64t = A.p("small64", [64, NV64], F32)
    identb = A.p("identb", [128, 128], BF16)
    identf = A.p("identf", [128, 128], F32)
    onesm = A.p("onesm", [128, 128], BF16)
    modv = A.p("modv", [128, 2, 48], F32)
    modA = A.p("modA", [128, 2, 16], F32)
    nw = A.p("nw", [128, 2, 12], F32)
    PRE_X = A.mark()
    xT = A.p("xT", [128, NCH, NT], F32)
    HT_OFF = A.mark()
    hT = A.p("hT", [128, NCH, NT], BF16)
    WN = A.p("wnext", [128, 8, 768], BF16)

    def prefetch(kind, l, hh=0):
        if kind == "pool":
            load_w(WN[:, :, 0:256], w_in[l], (0, D), (0, 256), "wnext")
        elif kind == "attn":
            for part in range(3):
                c0 = 1024 + part * 512 + hh * 256
                load_w(WN[:, :, part * 256:(part + 1) * 256], w_in[l], (0, D), (c0, c0 + 256), "wnext")
        elif kind == "hyena":
            load_w(WN[:, :, :], w_in[l], (0, D), (256, 1024), "wnext")
        elif kind == "mlp":
            load_w(WN[:, :, 0:512], w_up[l], (0, D), (0, 512), "wnext")

    def sm(col, n=1):
        return smallt[:, col:col + n]

    S.dma("sp", smallt[:, :], small, writes=["small"])
    S.dma("sp", small64t[:, :], small64, writes=["small64"])
    S.dma("sp", identb[:, :], identb_d, writes=["identb"])
    S.dma("sp", identf[:, :], identf_d, writes=["identf"])
    S.dma("sp", onesm[:, :], onesm_d, writes=["onesm"])
    CONSTS = ["small", "small64", "identb", "identf", "onesm"]
    eps_ap = sm(C_EPS)

    def copy_op(eng, out, in_, reads, writes):
        if eng == "act":
            S.op("act", lambda e: e.copy(out=out, in_=in_), reads, writes)
        elif eng == "dve":
            S.op("dve", lambda e: e.tensor_copy(out=out, in_=in_), reads, writes)
        else:
            S.op("pool", lambda e: e.tensor_copy(out=out, in_=in_), reads, writes)

    def load_w(dst, src_ap, rows, cols, wname):
        r0, r1 = rows
        c0, c1 = cols
        src = src_ap[r0:r1, c0:c1].rearrange("(k p) c -> p k c", p=128)
        S.dma("pool", dst, src, writes=[wname])

    def prologue(l):
        A.reset(PRE_X)
        lo = l * PL
        silu_c = A.s("silu_c", [128, 8], BF16)
        sig = A.s("sig", [128, 8], F32)
        wslabs = [A.s(f"wm{i}", [128, 8, 512], BF16) for i in range(3)]

        def adaln_steps():
            S.op("act", lambda e: e.activation(out=sig[:, :], in_=sm(C_COND, 8), func=AF.Sigmoid),
                 reads=["small"], writes=["sig"])
            S.op("dve", lambda e: e.tensor_tensor(out=silu_c[:, :], in0=sig[:, :], in1=sm(C_COND, 8),
                                                  op=ALU.mult), reads=["sig", "small"],
                 writes=["silu_c"])
            pm = psT_f32
            for sl in range(3):
                load_w(wslabs[sl][:, :, :], w_mod[l], (0, D), (sl * 512, (sl + 1) * 512), f"wm{sl}")
            yield
            for sl in range(12):
                wt = wslabs[sl % 3]
                wn_ = f"wm{sl % 3}"

                def mm(e, sl=sl, wt=wt):
                    ins = None
                    for jc in range(4):
                        j = sl * 4 + jc
                        for k in range(8):
                            ins = e.matmul(pm[:, j:j + 1], lhsT=wt[:, k, jc * 128:(jc + 1) * 128],
                                           rhs=silu_c[:, k:k + 1], start=(k == 0), stop=(k == 7))
                    return ins
                S.op("pe", mm, reads=[wn_, "silu_c"], writes=["psS1.0"])
                if sl + 3 < 12:
                    load_w(wt[:, :, :], w_mod[l], (0, D), ((sl + 3) * 512, (sl + 4) * 512), wn_)
                yield
            S.op("dve", lambda e: e.tensor_tensor(out=modv[:, l, :], in0=pm[:, 0:48],
                                                  in1=sm(lo + C_BMOD, 48), op=ALU.add),
                 reads=["psS1.0", "small"], writes=["modv"])
            for which, (cg, cs) in enumerate(((C_N1G, 8), (C_N2G, 32))):
                S.op("dve", lambda e, which=which, cg=cg, cs=cs: e.scalar_tensor_tensor(
                    out=modA[:, l, which * 8:(which + 1) * 8], in0=modv[:, l, cs:cs + 8], scalar=1.0,
                    in1=sm(lo + cg, 8), op0=ALU.add, op1=ALU.mult),
                    reads=["modv", "small"], writes=["modA"])
            for which, tap in enumerate((0, 2)):
                S.op("dve", lambda e, which=which, tap=tap: e.tensor_scalar(
                    out=nw[:, l, which * 6:(which + 1) * 6], in0=sm(lo + C_CW + tap * 6, 6),
                    scalar1=sm(C_FLAG), scalar2=-1.0, op0=ALU.mult, op1=ALU.mult),
                    reads=["small"], writes=["nw"])
            yield

        ada = adaln_steps()

        def ada_step():
            try:
                next(ada)
            except StopIteration:
                pass

        ada_step()
        h2 = A.s("h2", [64, NT], BF16)
        w1 = A.s("w1", [33, 64], F32)
        w2 = A.s("w2", [64, 64], F32)
        w3 = A.s("w3", [64, 1024], BF16)
        decay = A.s("decay", [128, 16, 256], F32)
        wn = A.s("wn", [128, 16, 128], BF16)
        fb = A.s("fb", [64, 2], F32)
        ovl = A.mark()
        zT = A.s("zT", [33, NT], F32)
        pre = A.s("pre", [64, NT], F32)
        tmp = A.s("tmp", [64, NT], F32)
        h1 = A.s("h1", [64, NT], F32)
        S.dma("sp", zT[:, :], zT_d, writes=["zT"])
        S.dma("sp", w1[:, :], f1_w[l], writes=["w1"])
        S.dma("sp", w2[:, :], f2_w[l], writes=["w2"])
        S.dma("pool", w3[:, :], f3_w[l], writes=["w3"])
        S.dma("sp", decay[:, :, :], decay_d, writes=["decay"])
        S.dma("pool", wn[:, :, :], wn_d, writes=["wn"])
        for li in range(2):
            S.op("dve", lambda e, li=li: e.tensor_tensor(
                out=fb[:, li:li + 1], in0=small64t[:, l * 4 + 2 * li:l * 4 + 2 * li + 1],
                in1=small64t[:, l * 4 + 2 * li + 1:l * 4 + 2 * li + 2], op=ALU.mult),
                reads=["small64"], writes=["fb"])

        def sine_layer(li, wmat, kdim, src, dst, srcname, dstname):
            for b in range(4):
                bank, bname = BANKS[b % 2]
                S.op("pe", lambda e, b=b, bank=bank: e.matmul(
                    bank[0:64, :], lhsT=wmat[0:kdim, :], rhs=src[0:kdim, b * 512:(b + 1) * 512],
                    start=True, stop=True), reads=[srcname, f"w{li + 1}"], writes=[bname])
                S.op("dve", lambda e, b=b, bank=bank: e.tensor_scalar(
                    out=pre[:, b * 512:(b + 1) * 512], in0=bank[0:64, :],
                    scalar1=small64t[:, l * 4 + 2 * li + 1:l * 4 + 2 * li + 2],
                    scalar2=fb[:, li:li + 1], op0=ALU.mult, op1=ALU.add),
                    reads=[bname, "small64", "fb"], writes=["pre"])
            S.op("dve", lambda e: e.tensor_scalar(out=tmp[:, :], in0=pre[:, :], scalar1=1.0 / TWO_PI,
                                                  scalar2=MAGIC, op0=ALU.mult, op1=ALU.add),
                 reads=["pre"], writes=["tmp"])
            S.op("dve", lambda e: e.tensor_scalar(out=tmp[:, :], in0=tmp[:, :], scalar1=MAGIC,
                                                  scalar2=-TWO_PI, op0=ALU.subtract, op1=ALU.mult),
                 reads=["tmp"], writes=["tmp"])
            S.op("dve", lambda e: e.tensor_tensor(out=tmp[:, :], in0=tmp[:, :], in1=pre[:, :],
                                                  op=ALU.add), reads=["tmp", "pre"], writes=["tmp"])
            S.op("dve", lambda e: e.tensor_scalar(out=tmp[:, :], in0=tmp[:, :], scalar1=PI_C,
                                                  scalar2=-PI_C, op0=ALU.min, op1=ALU.max),
                 reads=["tmp"], writes=["tmp"])
            S.op("act", lambda e: e.activation(out=dst[:, :], in_=tmp[:, :], func=AF.Sin),
                 reads=["tmp"], writes=[dstname])

        sine_layer(0, w1, 33, zT, h1, "zT", "h1")
        ada_step()
        sine_layer(1, w2, 64, h1, h2, "h1", "h2")
        ada_step()
        ada_step()

        A.reset(ovl)
        a_t = A.s("a_t", [128, 16, 512], BF16)
        d_t = A.s("d_t", [128, 16, 512], BF16)
        KD_OFF = A.mark()
        kd = A.s("kd", [128, 16, 1024], BF16)
        absk = [A.s(f"absk{i}", [128, 1024], BF16) for i in range(2)]
        def kd_mm(t):
            for cb in range(2):
                bank, bname = BANKS[cb] if t % 2 == 0 else BANKS[4 + 2 * cb]
                S.op("pe", lambda e, cb=cb, bank=bank: e.matmul(
                    bank, lhsT=h2[:, t * 128:(t + 1) * 128], rhs=w3[:, cb * 512:(cb + 1) * 512],
                    start=True, stop=True), reads=["h2", "w3"], writes=[bname])

        def kd_ew(t):
            for cb in range(2):
                bank, bname = BANKS[cb] if t % 2 == 0 else BANKS[4 + 2 * cb]
                dcy = decay[:, t, :]
                dcy_b = bass.AP(tensor=dcy.tensor, offset=dcy.offset,
                                ap=[[dcy.ap[0][0], 128], [0, 2], [1, 256]])
                S.op("dve", lambda e, cb=cb, bank=bank, dcy_b=dcy_b: e.tensor_tensor(
                    out=kd[:, t, cb * 512:(cb + 1) * 512].rearrange("p (h c) -> p h c", h=2),
                    in0=bank.rearrange("p (h c) -> p h c", h=2), in1=dcy_b, op=ALU.mult),
                    reads=[bname, "decay"], writes=[f"kd{t}"])
            ak = absk[t % 2]
            S.op("act", lambda e, ak=ak: e.activation(out=ak[:, :], in_=kd[:, t, :], func=AF.Abs),
                 reads=[f"kd{t}"], writes=[f"absk{t % 2}"])

        def kd_norm(t):
            ak = absk[t % 2]
            for cb in range(2):
                bank, bname = BANKS[2 + cb]
                S.op("pe", lambda e, cb=cb, bank=bank, ak=ak: e.matmul(
                    bank, lhsT=wn[:, t, :], rhs=ak[:, cb * 512:(cb + 1) * 512],
                    start=(t == 0), stop=(t == 15)), reads=[f"absk{t % 2}", "wn"], writes=[bname])

        kd_mm(0)
        for t in range(16):
            if t + 1 < 16:
                kd_mm(t + 1)
            kd_ew(t)
            if t >= 1:
                kd_norm(t - 1)
            if t % 4 == 3:
                ada_step()
        kd_norm(15)
        rnf = A.s("rnf", [128, 1024], F32)
        rn = A.s("rn", [128, 1024], BF16)
        for cb in range(2):
            bank, bname = BANKS[2 + cb]
            S.op("dve", lambda e, cb=cb, bank=bank: e.tensor_scalar(
                out=rnf[:, cb * 512:(cb + 1) * 512], in0=bank, scalar1=1e-6, scalar2=None,
                op0=ALU.add), reads=[bname], writes=["rnf"])
        S.op("dve", lambda e: e.reciprocal(out=rnf[:, :], in_=rnf[:, :]), reads=["rnf"], writes=["rnf"])
        copy_op("act", rn[:, :], rnf[:, :], ["rnf"], ["rn"])
        t1 = [A.s(f"t1{i}", [128, 512], BF16) for i in range(2)]
        t2 = [A.s(f"t2{i}", [128, 512], BF16) for i in range(2)]
        for t in range(16):
            u1, u2 = t1[t % 2], t2[t % 2]
            n1, n2 = f"t1{t % 2}", f"t2{t % 2}"
            S.op("dve", lambda e, t=t, u1=u1: e.tensor_tensor(out=u1[:, :], in0=kd[:, t, 0:512],
                                                             in1=rn[:, 0:512], op=ALU.mult),
                 reads=[f"kd{t}", "rn"], writes=[n1])
            S.op("dve", lambda e, t=t, u2=u2: e.scalar_tensor_tensor(
                out=u2[:, :], in0=kd[:, t, 512:1024], scalar=sm(C_TAU0 + t), in1=rn[:, 512:1024],
                op0=ALU.mult, op1=ALU.mult), reads=[f"kd{t}", "rn", "small"], writes=[n2])
            S.op("dve", lambda e, t=t, u1=u1, u2=u2: e.tensor_tensor(
                out=a_t[:, t, :], in0=u1[:, :], in1=u2[:, :], op=ALU.add), reads=[n1, n2],
                writes=["a_t"])
            S.op("dve", lambda e, t=t, u1=u1, u2=u2: e.tensor_tensor(
                out=d_t[:, t, :], in0=u1[:, :], in1=u2[:, :], op=ALU.subtract), reads=[n1, n2],
                writes=["d_t"])
            if t % 4 == 3:
                ada_step()
        NR = 6
        ring = [A.s(f"fw{i}", [128, 16, 128], BF16) for i in range(NR)]
        kst = [A.s(f"kst{i}", [128, 512], F32) for i in range(4)]
        est = [A.s(f"est{i}", [128, 512], F32) for i in range(2)]

        def load_fw(jt):
            S.dma("sp", ring[jt % NR][:, :, :], fwd_d[jt].rearrange("p (s f) -> p s f", s=16),
                  writes=[f"fw{jt % NR}"])
        for jt in range(NR):
            load_fw(jt)
        for jt in range(16):
            ri, j = jt // 8, jt % 8
            fw = ring[jt % NR]
            fn_ = f"fw{jt % NR}"
            (bE, nE), (bO, nO) = (BANKS[0], BANKS[1]) if jt % 2 == 0 else (BANKS[3], BANKS[4])
            src, sname = (a_t, "a_t") if ri == 0 else (d_t, "d_t")
            for half, (bank, bname) in enumerate(((bE, nE), (bO, nO))):
                def mm(e, fw=fw, bank=bank, src=src, half=half):
                    ins = None
                    for s_ in range(8):
                        ins = e.matmul(bank, lhsT=fw[:, half * 8 + s_, :], rhs=src[:, half * 8 + s_, :],
                                       start=(s_ == 0), stop=(s_ == 7))
                    return ins
                S.op("pe", mm, reads=[fn_, sname], writes=[bname])
            if jt + NR < 16:
                load_fw(jt + NR)
            es, esn = est[jt % 2], f"est{jt % 2}"
            copy_op("act", es[:, :], bE, [nE], [esn])
            kb_, kbn_ = kst[(jt % 2) * 2], f"kst{(jt % 2) * 2}"
            km_, kmn_ = kst[(jt % 2) * 2 + 1], f"kst{(jt % 2) * 2 + 1}"
            S.op("dve", lambda e, bO=bO, es=es, kb_=kb_: e.tensor_tensor(
                out=kb_[:, :], in0=bO, in1=es[:, :], op=ALU.add), reads=[nO, esn], writes=[kbn_])
            S.op("dve", lambda e, bO=bO, es=es, km_=km_: e.scalar_tensor_tensor(
                out=km_[:, :], in0=bO, scalar=-1.0, in1=es[:, :], op0=ALU.mult, op1=ALU.add),
                reads=[nO, esn], writes=[kmn_])
            S.dma("sp", khat[l, ri * 16 + j], kb_[:, :], reads=[kbn_], writes=["khat"])
            S.dma("sp", khat[l, ri * 16 + 8 + j], km_[:, :], reads=[kmn_], writes=["khat"])
            ada_step()
        for _ in range(20):
            ada_step()
        S.barrier()

    for l in range(2):
        prologue(l)
    A.reset()

    prefetch("pool", 0)
    xst = [A.s(f"xst{i}", [128, D], F32) for i in range(2)]
    for i in range(NTILE):
        st = xst[i % 2]
        sn = f"xst{i % 2}"
        S.dma("sp", st[:, :], x_in[i * 128:(i + 1) * 128, :], writes=[sn])
        for hb in range(2):
            bank, bname = BANKS[hb]

            def tr(e, hb=hb, bank=bank, st=st):
                ins = None
                for kk in range(4):
                    k = hb * 4 + kk
                    ins = e.transpose(out=bank[:, kk * 128:(kk + 1) * 128],
                                      in_=st[:, k * 128:(k + 1) * 128], identity=identf[:, :])
                return ins
            S.op("pe", tr, reads=[sn, "identf"], writes=[bname])
            copy_op("act" if hb == 0 else "dve",
                    xT[:, hb * 4:(hb + 1) * 4, i * 128:(i + 1) * 128],
                    bank.rearrange("p (k t) -> p k t", k=4), [bname], [f"xT{i // 4}"])
    S.barrier()

    def HTN(b):
        return [f"hT{b}.{k}" for k in range(8)]

    def rmsnorm(l, which):
        A.reset()
        rstd_l = [A.s(f"rstd{i}", [128, 512], F32) for i in range(2)]
        lnv_l = [A.s(f"lnv{i}", [128, 512], F32) for i in range(2)]
        tmpn = [A.s(f"tmpn{i}", [128, 512], F32) for i in range(4)]
        cB = 0 if which == 0 else 24
        nt_ = [0]

        def stage1(b):
            bs = slice(b * 512, (b + 1) * 512)
            for k in range(8):
                if k % 2 == 0:
                    S.op("act", lambda e, k=k: e.activation(out=hT[:, k, bs], in_=xT[:, k, bs],
                                                            func=AF.Square),
                         reads=[f"xT{b}"], writes=[f"hT{b}.{k}"])
                else:
                    S.op("dve", lambda e, k=k: e.tensor_tensor(out=hT[:, k, bs], in0=xT[:, k, bs],
                                                               in1=xT[:, k, bs], op=ALU.mult),
                         reads=[f"xT{b}"], writes=[f"hT{b}.{k}"])
            bank, bname = BANKS[b % 2]

            def mm(e):
                ins = None
                for k in range(8):
                    ins = e.matmul(bank, lhsT=onesm[:, :], rhs=hT[:, k, bs], start=(k == 0),
                                   stop=(k == 7))
                return ins
            S.op("pe", mm, reads=HTN(b) + ["onesm"], writes=[bname])

        def stage2(b):
            bs = slice(b * 512, (b + 1) * 512)
            rstd, lnv = rstd_l[b % 2], lnv_l[b % 2]
            Nr, Nl = f"rstd{b % 2}", f"lnv{b % 2}"
            bank, bname = BANKS[b % 2]
            S.op("act", lambda e: e.activation(out=lnv[:, :], in_=bank, func=AF.Ln, bias=eps_ap),
                 reads=[bname, "small"], writes=[Nl])
            S.op("act", lambda e: e.activation(out=rstd[:, :], in_=lnv[:, :], func=AF.Exp, scale=-0.5),
                 reads=[Nl], writes=[Nr])
            for k in range(8):
                tn = tmpn[nt_[0] % 4]
                tnn = f"tmpn{nt_[0] % 4}"
                nt_[0] += 1
                S.op("dve", lambda e, k=k, tn=tn: e.scalar_tensor_tensor(
                    out=tn[:, :], in0=xT[:, k, bs], scalar=modA[:, l, which * 8 + k:which * 8 + k + 1],
                    in1=rstd[:, :], op0=ALU.mult, op1=ALU.mult),
                    reads=[f"xT{b}", "modA", Nr], writes=[tnn])
                S.op("act", lambda e, k=k, tn=tn: e.activation(
                    out=hT[:, k, bs], in_=tn[:, :], func=AF.Identity,
                    bias=modv[:, l, cB + k:cB + k + 1]), reads=[tnn, "modv"],
                    writes=[f"hT{b}.{k}"])

        stage1(0)
        for b in range(4):
            if b + 1 < 4:
                stage1(b + 1)
            stage2(b)
        S.barrier()

    def wout_load(l, row0, nk):
        wo = A.s("wo", [128, nk, D], BF16)
        load_w(wo[:, :, :], w_out[l], (row0, row0 + nk * 128), (0, D), "wo")
        return wo

    def wout_partial(l, yT, yname, row0, nk, gate_col, wo=None, rhs_fn=None, ynames=None):
        if wo is None:
            wo = wout_load(l, row0, nk)
        if rhs_fn is None:
            rhs_fn = lambda kc, b: yT[:, kc, b * 512:(b + 1) * 512]
        n = 0
        for dc in range(8):
            for b in range(4):
                bs = slice(b * 512, (b + 1) * 512)
                bank, bname = BANKS[n % 4]
                n += 1

                def mm(e, dc=dc, b=b, bank=bank):
                    ins = None
                    for kc in range(nk):
                        ins = e.matmul(bank, lhsT=wo[:, kc, dc * 128:(dc + 1) * 128], rhs=rhs_fn(kc, b),
                                       start=(kc == 0), stop=(kc == nk - 1))
                    return ins
                S.op("pe", mm, reads=["wo"] + (ynames(b) if ynames else [yname]), writes=[bname])
                S.op("dve", lambda e, dc=dc, bs=bs, bank=bank: e.scalar_tensor_tensor(
                    out=xT[:, dc, bs], in0=bank, scalar=modv[:, l, gate_col + dc:gate_col + dc + 1],
                    in1=xT[:, dc, bs], op0=ALU.mult, op1=ALU.add),
                    reads=[bname, "modv", f"xT{b}"], writes=[f"xT{b}"])

    Y_DONE = [False]

    def store_y_tile(i, st, sn, banks2):
        for hb in range(2):
            bank, bname = banks2[hb]

            def tr(e, hb=hb, bank=bank):
                ins = None
                for kk in range(4):
                    k = hb * 4 + kk
                    ins = e.transpose(out=bank[:, kk * 128:(kk + 1) * 128],
                                      in_=xT[:, k, i * 128:(i + 1) * 128], identity=identf[:, :])
                return ins
            S.op("pe", tr, reads=[f"xT{i // 4}", "identf"], writes=[bname])
            copy_op("act", st[:, hb * 512:(hb + 1) * 512], bank, [bname], [sn])
        S.dma("sp", y_out[i * 128:(i + 1) * 128, :], st[:, :], reads=[sn])

    def pool_phase(l):
        A.reset()
        lo = l * PL
        wp = WN
        wo_pre = wout_load(l, 0, 2)
        apool = A.s("apool", [128, 4, 3, 4, 128], BF16)
        S.dma("sp", apool[:, :, :, :, :],
              apool_d.rearrange("p (a r g t) -> p a r g t", a=4, r=3, g=4), writes=["apool"])
        wblk = A.s("wblk", [128, 2, 128], BF16)
        S.op("dve", lambda e: e.memset(wblk[:, :, :], 0.0), writes=["wblk"])
        for g in range(4):
            gp, gl = g // 2, g % 2
            S.dma("pool", wblk[gl * 64:(gl + 1) * 64, gp, gl * 64:(gl + 1) * 64], pool_w[l, g],
                  writes=["wblk"])
        upool = A.s("upool", [128, 16, 256], BF16)
        for i in range(16):
            bank, bname = BANKS[i % 2]

            def mm(e, i=i, bank=bank):
                ins = None
                for k in range(8):
                    ins = e.matmul(bank[:, 0:256], lhsT=hT[:, k, i * 128:(i + 1) * 128],
                                   rhs=wp[:, k, 0:256], start=(k == 0), stop=(k == 7))
                return ins
            S.op("pe", mm, reads=HTN(i // 4) + ["wnext"], writes=[bname])
            copy_op("act" if i % 2 == 0 else "dve", upool[:, i, :], bank[:, 0:256], [bname],
                    [f"upool{i}"])
        prefetch("attn", l, 0)
        pooledT = A.s("pooledT", [128, 2, NT], BF16)
        n = 0
        for b in range(4):
            for gp in range(2):
                for gl in range(2):
                    g = gp * 2 + gl
                    bank, bname = BANKS[n % 4]
                    n += 1

                    def mm(e, b=b, gp=gp, g=g, bank=bank):
                        ins = None
                        for jl in range(4):
                            j = b * 4 + jl
                            rs = [r for r in (-1, 0, 1) if 0 <= j + r < 16]
                            for r in rs:
                                ins = e.matmul(bank[:, jl * 128:(jl + 1) * 128],
                                               lhsT=upool[:, j + r, gp * 128:(gp + 1) * 128],
                                               rhs=apool[:, pid_pool(j), r + 1, g, :],
                                               start=(r == rs[0]), stop=(r == rs[-1]))
                        return ins
                    rd = [f"upool{j}" for j in range(max(0, b * 4 - 1), min(16, b * 4 + 5))]
                    S.op("pe", mm, reads=rd + ["apool"], writes=[bname])
                    copy_op("act" if gl == 0 else "dve",
                            pooledT[gl * 64:(gl + 1) * 64, gp, b * 512:(b + 1) * 512],
                            bank[gl * 64:(gl + 1) * 64, :], [bname], [f"pooledT{b}.{gp}.{gl}"])
        ypT = A.s("ypT", [128, 2, NT], BF16)
        for b in range(4):
            for gp in range(2):
                bank, bname = BANKS[(b * 2 + gp) % 4]
                S.op("pe", lambda e, b=b, gp=gp, bank=bank: e.matmul(
                    bank, lhsT=wblk[:, gp, :], rhs=pooledT[:, gp, b * 512:(b + 1) * 512],
                    start=True, stop=True),
                    reads=["wblk", f"pooledT{b}.{gp}.0", f"pooledT{b}.{gp}.1"], writes=[bname])
                S.op("act", lambda e, b=b, gp=gp, bank=bank: e.activation(
                    out=ypT[:, gp, b * 512:(b + 1) * 512], in_=bank, func=AF.Identity,
                    scale=sm(lo + C_PSC + gp)), reads=[bname, "small"], writes=["ypT"])
        wout_partial(l, ypT, "ypT", 0, 2, 16, wo=wo_pre)
        S.barrier()

    def attn_phase(l, hh):
        A.reset()
        QT = A.s("QT", [128, 2, NT], BF16)
        KT = A.s("KT", [128, 2, NT], BF16)
        Vaug = A.s("Vaug", [128, 16, 4, 65], BF16)
        kc_tok = A.s("kc_tok", [128, 2, 256], BF16)
        KcT = A.s("KcT", [128, 2, 256], BF16)
        Vc = A.s("Vc", [128, 2, 4, 65], BF16)
        ytok = A.s("ytok", [128, 16, 256], BF16)
        gprod = A.s("gprod", [128, 1], F32)
        gkrow = A.s("gkrow", [128, 256], F32)
        wo_pre = wout_load(l, 512 + hh * 256, 2)
        ebt = [A.s(f"ebt{i}", [128, 30, 128], BF16) for i in range(2)]
        S.dma("sp", ebt[0][:, :, :], bias_d[l, hh * 4].rearrange("p (a k) -> p a k", a=30),
              writes=["ebt0"])
        markB = A.mark()
        wqkv = WN
        S.dma("sp", gkrow[:, :], gqk[l, :, 512:768], writes=["gkrow"])
        gtmp = A.s("gtmp", [128, 2], F32)
        S.dma("sp", gtmp[:, :], gcol[l], writes=["gtmp"])
        S.op("dve", lambda e: e.tensor_tensor(out=gprod[:, :], in0=gtmp[:, 0:1], in1=gtmp[:, 1:2],
                                              op=ALU.mult), reads=["gtmp"], writes=["gprod"])
        S.op("dve", lambda e: e.memset(Vaug[:, :, :, 64:65], 1.0), writes=["Vones"])
        S.dma("pool", kc_tok[:, :, :],
              cachek[l].rearrange("(t p) c -> p t c", p=128)[:, :, hh * 256:(hh + 1) * 256],
              writes=["kc_tok"])
        S.dma("pool", Vc[:, :, :, :],
              cachev[l].rearrange("(t p) (h c) -> p t h c", p=128, c=65)[:, :, hh * 4:(hh + 1) * 4, :],
              writes=["Vc"])

        qkf_l = [A.s(f"qkf{i}", [128, 512], F32) for i in range(2)]
        sq_l = [A.s(f"sq{i}", [128, 512], F32) for i in range(2)]
        ss_l = [A.s(f"ss{i}", [128, 8], F32) for i in range(2)]
        rs_l = [A.s(f"rs{i}", [128, 8], F32) for i in range(2)]
        qn = [A.s(f"qn{i}", [128, 512], F32) for i in range(2)]
        qkb_l = [A.s(f"qkb{i}", [128, 512], BF16) for i in range(3)]
        vf = [A.s(f"vf{i}", [128, 256], F32) for i in range(2)]

        def bc64(t):
            a_ = t[:, :]
            return bass.AP(tensor=a_.tensor, offset=a_.offset, ap=[[8, 128], [1, 8], [0, 64]])

        def stage_mm(i):
            ts_ = slice(i * 128, (i + 1) * 128)
            bqk, nqk = BANKS[0] if i % 2 == 0 else BANKS[3]
            bv, nv = BANKS[1] if i % 2 == 0 else BANKS[4]

            def mmqk(e):
                ins = None
                for k in range(8):
                    ins = e.matmul(bqk, lhsT=hT[:, k, ts_], rhs=wqkv[:, k, 0:512], start=(k == 0),
                                   stop=(k == 7))
                return ins
            S.op("pe", mmqk, reads=HTN(i // 4) + ["wnext"], writes=[nqk])

            def mmv(e):
                ins = None
                for k in range(8):
                    ins = e.matmul(bv[:, 0:256], lhsT=hT[:, k, ts_], rhs=wqkv[:, k, 512:768],
                                   start=(k == 0), stop=(k == 7))
                return ins
            S.op("pe", mmv, reads=HTN(i // 4) + ["wnext"], writes=[nv])

        def stage_ew1(i):
            ts_ = slice(i * 128, (i + 1) * 128)
            bqk, nqk = BANKS[0] if i % 2 == 0 else BANKS[3]
            bv, nv = BANKS[1] if i % 2 == 0 else BANKS[4]
            p2 = i % 2
            qkf, sq, ss = qkf_l[p2], sq_l[p2], ss_l[p2]
            Nq, Ns, Nss = f"qkf{p2}", f"sq{p2}", f"ss{p2}"
            vfi = vf[p2]
            copy_op("act", qkf[:, :], bqk, [nqk], [Nq])
            copy_op("dve", vfi[:, :], bv[:, 0:256], [nv], [f"vf{p2}"])
            copy_op("dve", Vaug[:, i, :, 0:64], bv[:, 0:256].rearrange("p (h c) -> p h c", h=4),
                    [nv], [f"Vaug{i}"])
            S.dma("sp", v_out[l, ts_, hh * 256:(hh + 1) * 256], vfi[:, :], reads=[f"vf{p2}"])
            S.op("act", lambda e: e.activation(out=sq[:, :], in_=bqk, func=AF.Square),
                 reads=[nqk], writes=[Ns])
            S.op("dve", lambda e: e.tensor_reduce(
                out=ss[:, :], in_=sq[:, :].rearrange("p (h c) -> p h c", h=8), axis=AX.X, op=ALU.add),
                reads=[Ns], writes=[Nss])

        def stage_ew2(i):
            ts_ = slice(i * 128, (i + 1) * 128)
            p2 = i % 2
            qkf, sq, ss, rs_ = qkf_l[p2], sq_l[p2], ss_l[p2], rs_l[p2]
            qkb = qkb_l[i % 3]
            Nq, Ns, Nss, Nrs, Nqb = f"qkf{p2}", f"sq{p2}", f"ss{p2}", f"rs{p2}", f"qkb{i % 3}"
            S.op("act", lambda e: e.activation(out=rs_[:, :], in_=ss[:, :], func=AF.Ln,
                                               scale=1.0 / 64.0, bias=eps_ap),
                 reads=[Nss, "small"], writes=[Nrs])
            S.op("act", lambda e: e.activation(out=rs_[:, :], in_=rs_[:, :], func=AF.Exp, scale=-0.5),
                 reads=[Nrs], writes=[Nrs])
            S.op("dve", lambda e: e.tensor_tensor(
                out=qkb[:, :].rearrange("p (h c) -> p h c", h=8),
                in0=qkf[:, :].rearrange("p (h c) -> p h c", h=8), in1=bc64(rs_), op=ALU.mult),
                reads=[Nq, Nrs], writes=[Nqb])
            qni = qn[p2]
            S.op("pool", lambda e: e.tensor_tensor(
                out=sq[:, 0:256].rearrange("p (h c) -> p h c", h=4),
                in0=qkf[:, 256:512].rearrange("p (h c) -> p h c", h=4),
                in1=bass.AP(tensor=rs_[:, :].tensor, offset=rs_[:, :].offset + 4,
                            ap=[[8, 128], [1, 4], [0, 64]]), op=ALU.mult),
                reads=[Nq, Nrs, Ns], writes=[Ns])
            S.op("pool", lambda e: e.tensor_tensor(out=qni[:, 0:256], in0=sq[:, 0:256], in1=gkrow[:, :],
                                                   op=ALU.mult), reads=[Ns, "gkrow"],
                 writes=[f"qn{p2}"])
            S.dma("sp", k_out[l, ts_, hh * 256:(hh + 1) * 256], qni[:, 0:256], reads=[f"qn{p2}"])

        def stage_tr(i):
            ts_ = slice(i * 128, (i + 1) * 128)
            qkb = qkb_l[i % 3]
            Nqb = f"qkb{i % 3}"

            def trq(e):
                ins = None
                for c4 in range(4):
                    ins = e.transpose(out=psT[:, c4 * 128:(c4 + 1) * 128],
                                      in_=qkb[:, c4 * 128:(c4 + 1) * 128], identity=identb[:, :])
                return ins
            S.op("pe", trq, reads=[Nqb, "identb"], writes=["psT"])
            copy_op("dve", QT[:, :, ts_], psT[:, 0:256].rearrange("p (c t) -> p c t", c=2), ["psT"],
                    [f"QT{i}"])
            S.op("act", lambda e: e.activation(
                out=KT[:, :, ts_], in_=psT[:, 256:512].rearrange("p (c t) -> p c t", c=2),
                func=AF.Identity, scale=gprod[:, 0:1]), reads=["psT", "gprod"], writes=[f"KT{i}"])

        stage_mm(0)
        stage_mm(1)
        def trc(e):
            ins = None
            for t in range(2):
                for c in range(2):
                    ins = e.transpose(out=psT[:, (t * 2 + c) * 128:(t * 2 + c + 1) * 128],
                                      in_=kc_tok[:, t, c * 128:(c + 1) * 128], identity=identb[:, :])
            return ins
        S.op("pe", trc, reads=["kc_tok", "identb"], writes=["psT"])
        for t in range(2):
            S.op("act", lambda e, t=t: e.activation(
                out=KcT[:, :, t * 128:(t + 1) * 128],
                in_=psT[:, t * 256:(t + 1) * 256].rearrange("p (c k) -> p c k", c=2),
                func=AF.Identity, scale=gtmp[:, 0:1]), reads=["psT", "gtmp"], writes=["KcT"])

        stage_ew1(0)
        for i in range(16):
            if i + 2 < 16:
                stage_mm(i + 2)
            if i + 1 < 16:
                stage_ew1(i + 1)
            stage_ew2(i)
            if i >= 1:
                stage_tr(i - 1)
        stage_tr(15)
        S.op("act", lambda e: e.activation(out=ebt[0][:, :, :], in_=ebt[0][:, :, :], func=AF.Exp),
             reads=["ebt0"], writes=["ebt0"])
        S.barrier()

        A.reset(markB)
        if hh == 0:
            prefetch("attn", l, 1)
        else:
            prefetch("hyena", l)
        PT = A.s("PT", [128, 16, 5, 128], BF16)
        PTc = [A.s(f"PTc{i}", [128, 2, NT], BF16) for i in range(2)]
        rcp = [A.s(f"rcp{i}", [128, 1], F32) for i in range(2)]
        psS = [(psS0, ["psS0.0", "psS0.1"]), (psS1, ["psS1.0", "psS1.1"])]
        psOs = [(psO, "psO"), (psG1, "psG1")]
        users = [[i for i in range(16) if start_attn(i) <= j <= start_attn(i) + 4] for j in range(16)]
        done_at = [[i for i in range(16) if start_attn(i) + 4 == j] for j in range(16)]
        PT_ap = PT[:, :, :, :]

        def pt_out(i0, n, j):
            o0 = i0 * 640 + (j - start_attn(i0)) * 128
            if n == 1:
                stride = 640
            else:
                o1 = (i0 + 1) * 640 + (j - start_attn(i0 + 1)) * 128
                stride = o1 - o0
            return bass.AP(tensor=PT_ap.tensor, offset=PT_ap.offset + o0,
                           ap=[[PT_ap.ap[0][0], 128], [stride, n], [1, 128]])

        def runs(us, j):
            out, cur = [], [us[0]]
            for i in us[1:]:
                d_new = (i * 640 + (j - start_attn(i)) * 128) - (cur[-1] * 640 + (j - start_attn(cur[-1])) * 128)
                if len(cur) >= 2:
                    d_old = (cur[1] * 640 + (j - start_attn(cur[1])) * 128) - (cur[0] * 640 + (j - start_attn(cur[0])) * 128)
                    if d_new != d_old:
                        out.append(cur)
                        cur = [i]
                        continue
                cur.append(i)
            out.append(cur)
            return out

        pvn = [0]

        def head_ctx_steps(hl, two_banks=False):
            h = hh * 4 + hl
            c, hp = hl // 2, hl % 2
            pr = slice(hp * 64, (hp + 1) * 64)
            eb, ebn = ebt[hl % 2], f"ebt{hl % 2}"
            ptc, ptcn = PTc[hl % 2], f"PTc{hl % 2}"
            if hl > 0:
                S.dma("sp", eb[:, :, :], bias_d[l, h].rearrange("p (a k) -> p a k", a=30), writes=[ebn])
                yield
                for pc in range(6):
                    S.op("act", lambda e, pc=pc: e.activation(
                        out=eb[:, pc * 5:pc * 5 + 5, :], in_=eb[:, pc * 5:pc * 5 + 5, :], func=AF.Exp),
                        reads=[ebn], writes=[ebn])
                    yield
            n_ = 0
            for t in range(2):
                for b_ in range(4):
                    cb, cbn = (psG1, "psG1") if (two_banks and n_ % 2 == 1) else (psG0, "psG0")
                    n_ += 1
                    S.op("pe", lambda e, cb=cb: e.matmul(
                        cb[:, :], lhsT=KcT[pr, c, t * 128:(t + 1) * 128],
                        rhs=QT[pr, c, b_ * 512:(b_ + 1) * 512], start=True, stop=True),
                        reads=["KcT"] + [f"QT{i}" for i in range(b_ * 4, b_ * 4 + 4)], writes=[cbn])
                    S.op("act", lambda e, cb=cb: e.activation(
                        out=ptc[:, t, b_ * 512:(b_ + 1) * 512], in_=cb[:, :], func=AF.Exp, scale=0.125),
                        reads=[cbn], writes=[f"{ptcn}.{b_}"])
                    yield

        g0 = head_ctx_steps(0, two_banks=True)
        for _ in g0:
            pass
        for hl in range(4):
            h = hh * 4 + hl
            c, hp = hl // 2, hl % 2
            pr = slice(hp * 64, (hp + 1) * 64)
            eb, ebn = ebt[hl % 2], f"ebt{hl % 2}"
            ptc, ptcn = PTc[hl % 2], f"PTc{hl % 2}"
            gnext = head_ctx_steps(hl + 1) if hl + 1 < 4 else iter(())

            def mm_s(j):
                us = users[j]
                i0, n = us[0], len(us)
                pS, pSn = psS[j % 2]

                def f(e):
                    ins = None
                    for c0 in range(0, n * 128, 512):
                        c1 = min(n * 128, c0 + 512)
                        ins = e.matmul(pS[:, c0:c1], lhsT=KT[pr, c, j * 128:(j + 1) * 128],
                                       rhs=QT[pr, c, i0 * 128 + c0:i0 * 128 + c1], start=True, stop=True)
                    return ins
                S.op("pe", f, reads=[f"KT{j}"] + [f"QT{i}" for i in us], writes=pSn)

            def exp_s(j):
                us = users[j]
                i0 = us[0]
                pS, pSn = psS[j % 2]
                for run in runs(us, j):
                    r0, n = run[0], len(run)
                    S.op("act", lambda e, r0=r0, n=n: e.activation(
                        out=pt_out(r0, n, j),
                        in_=pS[:, (r0 - i0) * 128:(r0 - i0 + n) * 128].rearrange("p (i q) -> p i q", i=n),
                        func=AF.Exp, scale=0.125), reads=pSn, writes=[f"PT{i}" for i in run])

            def bias_mul(i):
                S.op("dve", lambda e: e.tensor_tensor(out=PT[:, i, :, :], in0=PT[:, i, :, :],
                                                      in1=eb[:, pid_attn(i) * 5:pid_attn(i) * 5 + 5, :],
                                                      op=ALU.mult), reads=[f"PT{i}", ebn],
                     writes=[f"PT{i}"])

            def pv(i):
                st = start_attn(i)
                pO, pOn = psOs[pvn[0] % 2]
                rc, rcn = rcp[pvn[0] % 2], f"rcp{pvn[0] % 2}"
                pvn[0] += 1

                def f(e):
                    ins = None
                    for s_ in range(7):
                        if s_ < 5:
                            lhsT, rhs = PT[:, i, s_, :], Vaug[:, st + s_, hl, :]
                        else:
                            lhsT, rhs = ptc[:, s_ - 5, i * 128:(i + 1) * 128], Vc[:, s_ - 5, hl, :]
                        ins = e.matmul(pO[:, 0:65], lhsT=lhsT, rhs=rhs, start=(s_ == 0), stop=(s_ == 6))
                    return ins
                S.op("pe", f, reads=[f"PT{i}", f"{ptcn}.{i // 4}", "Vc", "Vones"] +
                     [f"Vaug{j}" for j in range(st, st + 5)], writes=[pOn])
                S.op("dve", lambda e: e.reciprocal(out=rc[:, :], in_=pO[:, 64:65]), reads=[pOn],
                     writes=[rcn])
                S.op("dve", lambda e: e.tensor_scalar(
                    out=ytok[:, i, hl * 64:(hl + 1) * 64], in0=pO[:, 0:64], scalar1=rc[:, 0:1],
                    scalar2=None, op0=ALU.mult), reads=[pOn, rcn], writes=[f"ytok{i}"])

            def tr_y(i):
                def f(e):
                    ins = None
                    for c_ in range(2):
                        ins = e.transpose(out=psT[:, c_ * 128:(c_ + 1) * 128],
                                          in_=ytok[:, i, c_ * 128:(c_ + 1) * 128], identity=identb[:, :])
                    return ins
                S.op("pe", f, reads=[f"ytok{i}", "identb"], writes=["psT"])
                copy_op("dve", PT[:, i, 0:2, :], psT[:, 0:256].rearrange("p (c t) -> p c t", c=2),
                        ["psT"], [f"PT{i}"])

            pend, ydone = [], []
            mm_s(0)
            for j in range(16):
                if j + 1 < 16:
                    mm_s(j + 1)
                exp_s(j)
                if hl == 3:
                    for i in ydone:
                        tr_y(i)
                    ydone = []
                for i in pend:
                    pv(i)
                    ydone.append(i)
                pend = done_at[j]
                for i in pend:
                    bias_mul(i)
                if 1 <= j:
                    next(gnext, None)
            for i in pend:
                pv(i)
                ydone.append(i)
            for _ in gnext:
                pass
            if hl == 3:
                for i in ydone:
                    tr_y(i)
        def yna_rhs(kc, b):
            a_ = PT[:, 4 * b, kc, :]
            return bass.AP(tensor=a_.tensor, offset=a_.offset, ap=[[a_.ap[0][0], 128], [640, 4], [1, 128]])
        wout_partial(l, None, None, 512 + hh * 256, 2, 16, wo=wo_pre, rhs_fn=yna_rhs,
                     ynames=lambda b: [f"PT{i}" for i in range(4 * b, 4 * b + 4)])
        S.barrier()

    def hyena_phase(l):
        A.reset()
        lo = l * PL
        x1T = A.s("x1T", [128, 2, NT], BF16)
        x2T = A.s("x2T", [128, 2, NT], BF16)
        vT = A.s("vT", [128, 2, NT], BF16)
        wo_pre = wout_load(l, 256, 2)
        mark = A.mark()
        whyp = WN
        ust_l = [A.s(f"ust{i}", [128, NT], F32) for i in range(2)]
        acc_l = [A.s(f"acc{i}", [128, NT], F32) for i in range(2)]
        dsts = [x1T, x1T, x2T, x2T, vT, vT]
        HYW = [["hy0"], ["hy1"], ["hy2e", "hy2o"]]
        for fc in range(6):
            ust, acc = ust_l[fc % 2], acc_l[fc % 2]
            Nu, Na = f"ust{fc % 2}", f"acc{fc % 2}"
            for b in range(4):
                bank, bname = BANKS[b % 2]

                def mm(e, fc=fc, b=b, bank=bank):
                    ins = None
                    for k in range(8):
                        ins = e.matmul(bank, lhsT=whyp[:, k, fc * 128:(fc + 1) * 128],
                                       rhs=hT[:, k, b * 512:(b + 1) * 512], start=(k == 0), stop=(k == 7))
                    return ins
                S.op("pe", mm, reads=["wnext"] + HTN(b), writes=[bname])
                copy_op("act", ust[:, b * 512:(b + 1) * 512], bank, [bname], [Nu])
                S.op("act", lambda e, fc=fc, b=b, bank=bank, acc=acc: e.activation(
                    out=acc[:, b * 512:(b + 1) * 512], in_=bank, func=AF.Identity,
                    scale=sm(lo + C_CW + 6 + fc), bias=sm(lo + C_CB + fc)),
                    reads=[bname, "small"], writes=[Na])
            cw = lambda tap, fc=fc: sm(lo + C_CW + tap * 6 + fc)
            dst = dsts[fc]
            S.op("dve", lambda e, fc=fc, ust=ust, acc=acc: e.scalar_tensor_tensor(
                out=acc[:, 1:NT], in0=ust[:, 0:NT - 1], scalar=cw(0), in1=acc[:, 1:NT],
                op0=ALU.mult, op1=ALU.add), reads=[Nu, "small", Na], writes=[Na])
            a_hi = acc[:, 256:NT].rearrange("p (s t) -> p s t", t=256)[:, :, 0:1]
            u_lo = ust[:, 0:NT - 256].rearrange("p (s t) -> p s t", t=256)[:, :, 255:256]
            S.op("dve", lambda e, fc=fc, ust=ust, acc=acc: e.scalar_tensor_tensor(
                out=a_hi, in0=u_lo, scalar=nw[:, l, fc:fc + 1], in1=a_hi, op0=ALU.mult, op1=ALU.add),
                reads=[Nu, "nw", Na], writes=[Na])
            a_lo = acc[:, 0:NT - 256].rearrange("p (s t) -> p s t", t=256)[:, :, 255:256]
            u_hi = ust[:, 256:NT].rearrange("p (s t) -> p s t", t=256)[:, :, 0:1]
            S.op("dve", lambda e, fc=fc, ust=ust, acc=acc: e.scalar_tensor_tensor(
                out=a_lo, in0=u_hi, scalar=nw[:, l, 6 + fc:7 + fc], in1=a_lo, op0=ALU.mult,
                op1=ALU.add), reads=[Nu, "nw", Na], writes=[Na])
            S.op("dve", lambda e, fc=fc, ust=ust, acc=acc, dst=dst: e.scalar_tensor_tensor(
                out=dst[:, fc % 2, 0:NT - 1], in0=ust[:, 1:NT], scalar=cw(2), in1=acc[:, 0:NT - 1],
                op0=ALU.mult, op1=ALU.add), reads=[Nu, "small", Na], writes=HYW[fc // 2])
            S.op("dve", lambda e, fc=fc, acc=acc, dst=dst: e.tensor_copy(
                out=dst[:, fc % 2, NT - 1:NT], in_=acc[:, NT - 1:NT]), reads=[Na],
                writes=HYW[fc // 2])
        S.barrier()
        A.reset(mark)
        prefetch("mlp", l)
        zt = A.s("zt", [128, 16, 256], BF16)
        Ypm = A.s("Ypm", [128, 2, 16, 256], BF16)
        UV = [A.s(f"uv{i}", [128, 256], F32) for i in range(4)]
        NF, NKB, NI, PF, PI = 6, 4, 8, 2, 7
        fring = [A.at(f"fwr{i}", [128, 16, 128], BF16, HT_OFF + i * 4096) for i in range(NF)]
        ztp = A.at("ztp", [128, 8, 256], BF16, HT_OFF + NF * 4096)
        iring = [A.s(f"ivr{i}", [128, 1024], BF16) for i in range(NI - 2)] + \
                [A.at(f"ivr{NI - 2 + i}", [128, 1024], BF16, HT_OFF + NF * 4096 + 4096 + i * 2048)
                 for i in range(2)]
        kb = [A.s(f"kb{i}", [128, 2, 256], F32) for i in range(NKB)]
        tt = [A.s(f"tt{i}", [128, 256], F32) for i in range(8)]
        gst = [A.s(f"gst{i}", [128, 512], F32) for i in range(2)]
        yhT = A.s("yhT", [128, 2, NT], BF16)
        ninv = [0]

        def load_f(j, o):
            for ri in range(2):
                q_ = 2 * j + ri
                S.dma("sp", fring[q_ % NF][:, :, :],
                      fwd_d[j + 8 * ri].rearrange("p (s f) -> p s f", s=16), writes=[f"fwr{q_ % NF}"])

        def load_k(q, o):
            j, m = q // 2, q % 2
            for ri in range(2):
                S.dma("sp", kb[q % NKB][:, ri, :],
                      khat[l, ri * 16 + j + 8 * m, :, o * 256:(o + 1) * 256], writes=[f"kb{q % NKB}"])

        def load_i(half, jj):
            n_ = ninv[0]
            ninv[0] += 1
            S.dma("sp", iring[n_ % NI][:, :], inv_d[half, jj], writes=[f"ivr{n_ % NI}"])
            return n_ % NI

        for o in range(2):
            src = vT
            gate = x1T if o == 0 else x2T
            gname = "hy0" if o == 0 else "hy1"
            for j in range(PF):
                load_f(j, o)
            for q in range(2):
                load_k(q, o)
            for half in range(2):
                for g4 in range(4):

                    def trz(e, half=half, g4=g4):
                        ins = None
                        for tl in range(2):
                            t = half * 8 + g4 * 2 + tl
                            for c in range(2):
                                a_ = src[:, c, 256 * (t % 8) + t // 8:256 * (t % 8) + t // 8 + 1]
                                sel_ = bass.AP(tensor=a_.tensor, offset=a_.offset,
                                               ap=[[a_.ap[0][0], 128], [2, 128]])
                                ins = e.transpose(
                                    out=psT[:, (tl * 2 + c) * 128:(tl * 2 + c + 1) * 128],
                                    in_=sel_, identity=identb[:, :])
                        return ins
                    S.op("pe", trz, reads=["hy2e" if half == 0 else "hy2o", "identb"], writes=["psT"])
                    t0 = half * 8 + g4 * 2
                    copy_op("dve" if g4 % 2 == 0 else "act", zt[:, t0:t0 + 2, :],
                            psT[:, 0:512].rearrange("p (t c) -> p t c", t=2), ["psT"], ["zt"])
            S.op("dve", lambda e: e.tensor_scalar(out=ztp[:, :, :], in0=zt[:, 8:16, :], scalar1=-1.0,
                                                  scalar2=None, op0=ALU.mult), reads=["zt"],
                 writes=["ztp"])
            for q in range(16):
                j, m = q // 2, q % 2
                zname = "zt" if m == 0 else "ztp"
                kbt, kbn = kb[q % NKB], f"kb{q % NKB}"
                banks = (BANKS[(q % 2) * 2], BANKS[(q % 2) * 2 + 1])
                for ri in range(2):
                    q_ = 2 * j + ri
                    fw = fring[q_ % NF]
                    bank, bname = banks[ri]

                    def mm(e, fw=fw, bank=bank, m=m):
                        ins = None
                        for s_ in range(16):
                            rhs = ztp[:, s_ - 8, :] if (m == 1 and s_ >= 8) else zt[:, s_, :]
                            ins = e.matmul(bank[:, 0:256], lhsT=fw[:, s_, :], rhs=rhs,
                                           start=(s_ == 0), stop=(s_ == 15))
                        return ins
                    S.op("pe", mm, reads=[f"fwr{q_ % NF}", "zt", zname], writes=[bname])
                if m == 1 and j + PF < 8:
                    load_f(j + PF, o)
                if q + 2 < 16:
                    load_k(q + 2, o)
                (bR, nR), (bI, nI) = banks
                sR, sI = j + 8 * m, 16 + j + 8 * m
                tb_ = (q % 2) * 4
                T0, T1, T2, T3 = tt[tb_], tt[tb_ + 1], tt[tb_ + 2], tt[tb_ + 3]
                N0, N1, N2, N3 = f"tt{tb_}", f"tt{tb_ + 1}", f"tt{tb_ + 2}", f"tt{tb_ + 3}"
                S.op("dve", lambda e, bR=bR, kbt=kbt, T0=T0: e.tensor_tensor(
                    out=T0[:, :], in0=bR[:, 0:256], in1=kbt[:, 0, :], op=ALU.mult),
                    reads=[nR, kbn], writes=[N0])
                S.op("dve", lambda e, bR=bR, kbt=kbt, T2=T2: e.tensor_tensor(
                    out=T2[:, :], in0=bR[:, 0:256], in1=kbt[:, 1, :], op=ALU.mult),
                    reads=[nR, kbn], writes=[N2])
                S.op("dve", lambda e, bI=bI, kbt=kbt, T1=T1: e.tensor_tensor(
                    out=T1[:, :], in0=bI[:, 0:256], in1=kbt[:, 1, :], op=ALU.mult),
                    reads=[nI, kbn], writes=[N1])
                S.op("dve", lambda e, bI=bI, kbt=kbt, T3=T3: e.tensor_tensor(
                    out=T3[:, :], in0=bI[:, 0:256], in1=kbt[:, 0, :], op=ALU.mult),
                    reads=[nI, kbn], writes=[N3])
                U0, U1 = (UV[0], UV[1]) if m == 0 else (UV[2], UV[3])
                n0, n1 = ("uv0", "uv1") if m == 0 else ("uv2", "uv3")
                S.op("pool", lambda e, T0=T0, T1=T1, U0=U0: e.tensor_tensor(
                    out=U0[:, :], in0=T0[:, :], in1=T1[:, :], op=ALU.subtract),
                    reads=[N0, N1], writes=[n0])
                S.op("pool", lambda e, T2=T2, T3=T3, U1=U1: e.tensor_tensor(
                    out=U1[:, :], in0=T2[:, :], in1=T3[:, :], op=ALU.add),
                    reads=[N2, N3], writes=[n1])
                if m == 1:
                    for ri in range(2):
                        S.op("dve", lambda e, ri=ri: e.tensor_tensor(
                            out=Ypm[:, 0, ri * 8 + j, :], in0=UV[ri][:, :], in1=UV[2 + ri][:, :],
                            op=ALU.add), reads=[f"uv{ri}", f"uv{2 + ri}"], writes=["Ypm"])
                        S.op("dve", lambda e, ri=ri: e.tensor_tensor(
                            out=Ypm[:, 1, ri * 8 + j, :], in0=UV[ri][:, :], in1=UV[2 + ri][:, :],
                            op=ALU.subtract), reads=[f"uv{ri}", f"uv{2 + ri}"], writes=["Ypm"])
                if q == 8:
                    pre_slots = [load_i(0, jj) for jj in range(PI)]
            slots = list(pre_slots)
            for par in range(2):
                acc_b = [BANKS[3], BANKS[4], BANKS[5], BANKS[6]] if par == 0 else \
                        [BANKS[0], BANKS[1], BANKS[2], BANKS[3]]
                for jj in range(16):
                    sl_ = slots.pop(0)
                    iv = iring[sl_]
                    ivn = f"ivr{sl_}"

                    def mm(e, jj=jj, iv=iv, acc_b=acc_b, par=par):
                        ins = None
                        for cc in range(2):
                            for tb in range(2):
                                ins = e.matmul(acc_b[cc * 2 + tb][0],
                                               lhsT=Ypm[:, par, jj, cc * 128:(cc + 1) * 128],
                                               rhs=iv[:, tb * 512:(tb + 1) * 512], start=(jj == 0),
                                               stop=(jj == 15))
                        return ins
                    S.op("pe", mm, reads=[ivn, "Ypm"], writes=[b_[1] for b_ in acc_b])
                    nxt = jj + PI
                    if nxt < 16:
                        slots.append(load_i(par, nxt))
                    elif par == 0:
                        slots.append(load_i(1, nxt - 16))
                for cc in range(2):
                    for tb in range(2):
                        bank, bname = acc_b[cc * 2 + tb]
                        t0_ = tb * 1024 + par
                        g_ = gst[(cc * 2 + tb) % 2]
                        gn_ = f"gst{(cc * 2 + tb) % 2}"
                        dst = vT if o == 0 else yhT
                        hyp = "hy2e" if par == 0 else "hy2o"
                        dn = hyp if o == 0 else "yhT"

                        def sel(tn, cc=cc, t0_=t0_):
                            a_ = tn[:, cc, t0_:t0_ + 1]
                            return bass.AP(tensor=a_.tensor, offset=a_.offset,
                                           ap=[[a_.ap[0][0], 128], [2, 512]])
                        S.op("dve", lambda e, cc=cc, bank=bank, g_=g_: e.scalar_tensor_tensor(
                            out=g_[:, :], in0=sel(src), scalar=sm(lo + C_HB + o * 2 + cc), in1=bank,
                            op0=ALU.mult, op1=ALU.add), reads=[hyp, "small", bname], writes=[gn_])
                        S.op("pool", lambda e, g_=g_, dst=dst: e.tensor_tensor(
                            out=sel(dst), in0=g_[:, :], in1=sel(gate), op=ALU.mult),
                            reads=[gn_, gname], writes=[dn])
        wout_partial(l, yhT, "yhT", 256, 2, 16, wo=wo_pre)
        S.barrier()

    def mlp_phase(l):
        A.reset()
        act = A.s("act", [128, 8, NT], BF16)
        ring = [A.s(f"wr{i}", [128, 8, 512], BF16) for i in range(4)]
        rl = [A.s(f"rl{i}", [128, 512], BF16) for i in range(2)]
        nld = 0
        for fg in range(4):
            ups = []
            for hf in range(2):
                if fg == 0 and hf == 0:
                    ups.append((WN, "wnext"))
                    continue
                wt, wn_ = ring[nld % 4], f"wr{nld % 4}"
                nld += 1
                c0 = fg * 1024 + hf * 512
                load_w(wt[:, :, :], w_up[l], (0, D), (c0, c0 + 512), wn_)
                ups.append((wt, wn_))
            dns = []
            for hf in range(2):
                wt, wn_ = ring[nld % 4], f"wr{nld % 4}"
                nld += 1
                load_w(wt[:, :, :], w_down[l], (fg * 1024, (fg + 1) * 1024), (hf * 512, (hf + 1) * 512),
                       wn_)
                dns.append((wt, wn_))
            n = 0
            for fc in range(8):
                wt, wn_ = ups[fc // 4]
                for b in range(4):
                    bs = slice(b * 512, (b + 1) * 512)
                    bank, bname = BANKS[n % 4]
                    r_, rn_ = rl[n % 2], f"rl{n % 2}"
                    n += 1

                    def mm(e, fc=fc, bs=bs, bank=bank, wt=wt):
                        ins = None
                        for k in range(8):
                            ins = e.matmul(bank, lhsT=wt[:, k, (fc % 4) * 128:(fc % 4 + 1) * 128],
                                           rhs=hT[:, k, bs], start=(k == 0), stop=(k == 7))
                        return ins
                    S.op("pe", mm, reads=[wn_] + HTN(b), writes=[bname])
                    S.op("act", lambda e, bank=bank, r_=r_: e.activation(out=r_[:, :], in_=bank,
                                                                         func=AF.Relu),
                         reads=[bname], writes=[rn_])
                    S.op("pool", lambda e, fc=fc, bs=bs, r_=r_: e.tensor_tensor(
                        out=act[:, fc, bs], in0=r_[:, :], in1=r_[:, :], op=ALU.mult),
                        reads=[rn_], writes=[f"act{b}"])
            if fg == 0 and l + 1 < 2:
                prefetch("pool", l + 1)
            last = (l == 1 and fg == 3)
            order = [(dc, b) for b in range(4) for dc in range(8)] if last else \
                    [(dc, b) for dc in range(8) for b in range(4)]
            for (dc, b) in order:
                wt, wn_ = dns[dc // 4]
                bs = slice(b * 512, (b + 1) * 512)
                bank, bname = BANKS[n % 4]
                n += 1

                def mm(e, dc=dc, bs=bs, bank=bank, wt=wt):
                    ins = None
                    for fc in range(8):
                        ins = e.matmul(bank, lhsT=wt[:, fc, (dc % 4) * 128:(dc % 4 + 1) * 128],
                                       rhs=act[:, fc, bs], start=(fc == 0), stop=(fc == 7))
                    return ins
                S.op("pe", mm, reads=[wn_, f"act{b}"], writes=[bname])
                S.op("dve", lambda e, dc=dc, bs=bs, bank=bank: e.scalar_tensor_tensor(
                    out=xT[:, dc, bs], in0=bank, scalar=modv[:, l, 40 + dc:41 + dc], in1=xT[:, dc, bs],
                    op0=ALU.mult, op1=ALU.add), reads=[bname, "modv", f"xT{b}"],
                    writes=[f"xT{b}"])
                if last and dc == 7:
                    if b == 0:
                        yst_l = [A.s(f"yst{i}", [128, D], F32) for i in range(2)]
                        Y_DONE[0] = True
                    for i in range(4 * b, 4 * b + 4):
                        store_y_tile(i, yst_l[i % 2], f"yst{i % 2}", (BANKS[5], BANKS[6]))
        S.barrier()

    phases = []
    for l in range(2):
        phases += [("norm1", lambda l=l: rmsnorm(l, 0)), ("pool", lambda l=l: pool_phase(l)),
                   ("attn0", lambda l=l: attn_phase(l, 0)), ("attn1", lambda l=l: attn_phase(l, 1)),
                   ("hyena", lambda l=l: hyena_phase(l)), ("norm2", lambda l=l: rmsnorm(l, 1)),
                   ("mlp", lambda l=l: mlp_phase(l))]
    for idx, (pname, fn) in enumerate(phases):
        if stop is not None and idx >= stop:
            break
        fn()

    if not Y_DONE[0]:
        A.reset()
        yst = [A.s(f"yst{i}", [128, D], F32) for i in range(2)]
        for i in range(NTILE):
            store_y_tile(i, yst[i % 2], f"yst{i % 2}", (BANKS[0], BANKS[1]))
    S.barrier()
    return nc


_PERM = np.concatenate([np.arange(0, NT, 2), np.arange(1, NT, 2)])


def _dft_tables(L, nblk):
    N = 2 * L
    T = L * nblk
    pos = np.arange(L, dtype=np.float64)
    fwd = np.zeros((16, T, 128), np.float32)
    inv = np.zeros((32 * 128, T), np.float32)
    for j in range(8):
        for p in range(128):
            bq, f = (0, j * 128 + p) if nblk == 1 else (j, p)
            sl = slice(bq * L, (bq + 1) * L)
            th = 2.0 * np.pi * (f + 0.5) * pos / N
            thm = 2.0 * np.pi * (L - 1 - f + 0.5) * pos / N
            fwd[j, sl, p] = np.cos(th)
            fwd[8 + j, sl, p] = -np.sin(th)
            inv[j * 128 + p, sl] = (2.0 / N) * np.cos(th)
            inv[(8 + j) * 128 + p, sl] = (2.0 / N) * np.cos(thm)
            inv[(16 + j) * 128 + p, sl] = -(2.0 / N) * np.sin(th)
            inv[(24 + j) * 128 + p, sl] = (2.0 / N) * np.sin(thm)
    fwd = fwd[:, _PERM, :]
    fwd_b = fwd.reshape(16, 16, 128, 128).transpose(0, 2, 1, 3).reshape(16, 128, 16 * 128)
    invp = np.zeros((2, 16 * 128, T // 2), np.float32)
    for j in range(8):
        for p in range(128):
            bq, f = (0, j * 128 + p) if nblk == 1 else (j, p)
            for par in range(2):
                tl = np.arange(par, L, 2, dtype=np.float64)
                cols = ((bq * L + tl - par) // 2).astype(np.int64)
                th = 2.0 * np.pi * (f + 0.5) * tl / N
                invp[par, j * 128 + p, cols] = (2.0 / N) * np.cos(th)
                invp[par, (8 + j) * 128 + p, cols] = -(2.0 / N) * np.sin(th)
    inv_b = invp.reshape(2, 16, 128, 1024)
    return (np.ascontiguousarray(fwd_b).astype(NPBF), np.ascontiguousarray(inv_b).astype(NPBF))


def _hyena_consts(L, nblk):
    t = np.linspace(0.0, 1.0, L, dtype=np.float32)[:, None]
    w = (2.0 * np.pi * np.arange(L, dtype=np.float32)[:, None] / L).astype(np.float32)
    f = np.linspace(1e-4, 15, 16, dtype=np.float32)[None, :]
    z = np.concatenate([t, np.cos(f * w), -np.sin(f * w)], axis=-1).astype(np.float32)
    deltas = np.abs(np.linspace(np.log(1e-2) / 1.5, np.log(1e-2) / 0.3, 256, dtype=np.float32))
    decay = np.exp(-t * deltas[None, :]).astype(np.float32)
    zT = np.ascontiguousarray(np.tile(z, (nblk, 1))[_PERM].T)
    dec = np.tile(decay, (nblk, 1))[_PERM].reshape(16, 128, 256).transpose(1, 0, 2)
    pos = np.arange(NT)[_PERM]
    wn = (pos < L).astype(np.float32).reshape(16, 128).T
    wn = np.repeat(wn[:, :, None], 128, axis=2)
    tau0 = (pos % L != 0).astype(np.float32).reshape(16, 128).T
    return zT, np.ascontiguousarray(dec), np.ascontiguousarray(wn), np.ascontiguousarray(tau0)


def _pool_consts(L, nblk):
    T = L * nblk
    out = np.zeros((128, 4, 3, 4, 128), np.float32)
    tl = np.arange(L)
    for g, wd in enumerate((2, 4, 8, 16)):
        lo = np.clip(tl - wd // 2, 0, L - 1)
        hi = np.clip(tl + (wd - 1 - wd // 2), 0, L - 1)
        M1 = np.zeros((L, L), np.float64)
        for t_ in range(L):
            M1[t_, lo[t_]:hi[t_] + 1] = 1.0 / (hi[t_] - lo[t_] + 1)
        M1 -= np.eye(L)
        M = np.zeros((T, T), np.float64)
        for b in range(nblk):
            M[b * L:(b + 1) * L, b * L:(b + 1) * L] = M1
        for pat, j in enumerate((0, 2, 3, 15)):
            for ri, r in enumerate((-1, 0, 1)):
                if 0 <= j + r < 16:
                    blk = M[j * 128:(j + 1) * 128, (j + r) * 128:(j + r + 1) * 128]
                    out[:, pat, ri, g, :] = blk.T
    return out.reshape(128, -1).astype(NPBF)


def _bias_bank(rel_bias, sample):
    out = np.full((2, 8, 128, 30, 128), NEG, np.float32)
    reps = (0, 1, 2, 3, 14, 15)
    if sample:
        r_all = np.arange(32)
        c_all = np.arange(64)
        row_start = np.clip(r_all - 4, 0, 24)
        col_start = np.clip(c_all - 8, 0, 48)
        for pat, i in enumerate(reps):
            st = start_attn(i)
            q = i * 128 + np.arange(128)
            qr, qc = q // 64, q % 64
            for s_ in range(5):
                k = (st + s_) * 128 + np.arange(128)
                kr, kc = k // 64, k % 64
                vr = (kr[None, :] >= row_start[qr][:, None]) & (kr[None, :] < row_start[qr][:, None] + 8)
                vc = (kc[None, :] >= col_start[qc][:, None]) & (kc[None, :] < col_start[qc][:, None] + 16)
                valid = vr & vc
                dr = np.clip(kr[None, :] - qr[:, None] + 7, 0, 14)
                dc = np.clip(kc[None, :] - qc[:, None], -15, 15) + 15
                vals = rel_bias[:, :, dr, dc]
                out[:, :, :, pat * 5 + s_, :] = np.where(valid[None, None], vals, NEG)
    else:
        for pat, i in enumerate(reps):
            st = start_attn(i)
            for s_ in range(5):
                j = st + s_
                if j // 2 == i // 2:
                    out[:, :, :, pat * 5 + s_, :] = 0.0
    out = np.ascontiguousarray(out.transpose(0, 1, 4, 3, 2))
    return out.reshape(2, 8, 128, 30 * 128).astype(NPBF)


def _colvec(v):
    return np.ascontiguousarray(np.asarray(v, np.float32).reshape(-1, 128).T)


_CACHE = {}


def _consts(sample):
    key = ("c", sample)
    if key not in _CACHE:
        L, nblk = (2048, 1) if sample else (256, 8)
        fwd_b, inv_b = _dft_tables(L, nblk)
        zT, dec, wn, tau0 = _hyena_consts(L, nblk)
        _CACHE[key] = dict(fwd=fwd_b, inv=inv_b, zT=zT, decay=dec, wn=wn, tau0=tau0,
                           apool=_pool_consts(L, nblk))
    return _CACHE[key]


def _get_nc(stop=None):
    key = ("nc", stop)
    if key not in _CACHE:
        _CACHE[key] = build_nc(stop=stop)
    return _CACHE[key]


def make_in_maps(inp):
    g = {k: np.asarray(v) for k, v in inp.items()}
    shared = dict(
        w_mod=g["w_mod"], w_in=g["w_in"], w_out=g["w_out"], w_up=g["w_up"], w_down=g["w_down"],
        pool_w=g["pool_w"], f1_w=g["hy_f1_w"], f2_w=g["hy_f2_w"], f3_w=g["hy_f3_w"],
        identb=np.eye(128, dtype=np.float32).astype(NPBF), identf=np.eye(128, dtype=np.float32),
        onesm=np.full((128, 128), 1.0 / 1024.0, np.float32).astype(NPBF))
    small64 = np.zeros((64, NV64), np.float32)
    for l in range(2):
        small64[:, l * 4 + 0] = g["hy_f1_b"][l]
        small64[:, l * 4 + 1] = g["hy_f1_freq"][l]
        small64[:, l * 4 + 2] = g["hy_f2_b"][l]
        small64[:, l * 4 + 3] = g["hy_f2_freq"][l]
    shared["small64"] = small64
    gqk = np.zeros((2, 128, 1024), np.float32)
    for l in range(2):
        gqk[l, :, 0:512] = np.tile(g["q_norm_g"][l], 8)[None, :]
        gqk[l, :, 512:1024] = np.tile(g["k_norm_g"][l], 8)[None, :]
    shared["gqk"] = gqk
    gcol = np.zeros((2, 128, 2), np.float32)
    for l in range(2):
        gcol[l, :, 0] = np.tile(g["q_norm_g"][l], 2)
        gcol[l, :, 1] = np.tile(g["k_norm_g"][l], 2)
    shared["gcol"] = gcol
    bias_s = _bias_bank(g["rel_bias"], True)
    bias_p = _bias_bank(g["rel_bias"], False)
    maps = []
    for core in range(8):
        sample = core >= 4
        cst = _consts(sample)
        m = dict(shared)
        small = np.zeros((128, NV), np.float32)
        for l in range(2):
            lo = l * PL
            small[:, lo + C_N1G:lo + C_N1G + 8] = _colvec(g["norm1_g"][l])
            small[:, lo + C_N2G:lo + C_N2G + 8] = _colvec(g["norm2_g"][l])
            small[:, lo + C_BMOD:lo + C_BMOD + 48] = _colvec(g["b_mod"][l])
            small[:, lo + C_PSC:lo + C_PSC + 2] = _colvec(g["pool_scale"][l])
            for tap in range(3):
                small[:, lo + C_CW + tap * 6:lo + C_CW + tap * 6 + 6] = _colvec(g["hy_conv_w"][l, tap])
            small[:, lo + C_CB:lo + C_CB + 6] = _colvec(g["hy_conv_b"][l])
            for o in range(2):
                small[:, lo + C_HB + o * 2:lo + C_HB + o * 2 + 2] = _colvec(g["hy_bias"][l, o])
        small[:, C_EPS] = 1e-6
        small[:, C_SGN] = np.where(np.arange(128) % 2 == 0, 1.0, -1.0)
        small[:, C_TAU0:C_TAU0 + 16] = cst["tau0"]
        if sample:
            b = core - 4
            m["x_in"] = np.ascontiguousarray(g["x_sample"][b])
            small[:, C_COND:C_COND + 8] = _colvec(g["c"][b])
            small[:, C_FLAG] = 0.0
            m["cachek"] = np.ascontiguousarray(g["cache_k"][b].reshape(2, 256, 512))
            cv = np.ones((2, 256, 8, 65), np.float32)
            cv[..., :64] = g["cache_v"][b]
            m["cachev"] = cv.reshape(2, 256, 8 * 65)
            m["biasbank"] = bias_s
        else:
            m["x_in"] = np.ascontiguousarray(g["x_prompt"][core * 8:(core + 1) * 8].reshape(NT, D))
            small[:, C_COND:C_COND + 8] = _colvec(g["c_ctx"])
            small[:, C_FLAG] = 1.0
            m["cachek"] = np.zeros((2, 256, 512), np.float32)
            m["cachev"] = np.zeros((2, 256, 8 * 65), np.float32)
            m["biasbank"] = bias_p
        m["small"] = small
        m["zT"], m["decay"], m["wn"], m["apool"] = cst["zT"], cst["decay"], cst["wn"], cst["apool"]
        m["fwd"], m["inv"] = cst["fwd"], cst["inv"]
        maps.append(m)
    return maps


def assemble(results):
    y_prompt = np.stack([results[c]["y"] for c in range(4)]).reshape(32, 256, D)
    y_sample = np.stack([results[c]["y"] for c in range(4, 8)])
    nk = np.zeros((32, 2, 256, 8, 64), np.float32)
    nv = np.zeros((32, 2, 256, 8, 64), np.float32)
    for c in range(4):
        ko = results[c]["kout"].reshape(2, 8, 256, 8, 64)
        vo = results[c]["vout"].reshape(2, 8, 256, 8, 64)
        nk[c * 8:(c + 1) * 8] = ko.transpose(1, 0, 2, 3, 4)
        nv[c * 8:(c + 1) * 8] = vo.transpose(1, 0, 2, 3, 4)
    return (y_prompt.astype(np.float32), y_sample.astype(np.float32), nk, nv)


def kernel(**inputs):
    nc = _get_nc()
    maps = make_in_maps(inputs)
    res = run_bass_kernel_spmd(nc, maps, core_ids=list(range(8)))
    return assemble(res.results)
```
